# Optimizing a Trainium2 kernel written in Bass

```python
import jax, jax.numpy as jnp
from jax import lax
import numpy as np

D_MODEL = 2048
BATCH = 4
SEQ = 2048
DEPTH = 2
DEC_BATCH = 128
DEC_SEQ = 8
PAST_LEN = 16384
PAGE_SIZE = 128

N_MIXERS = 2
N_A_LAYERS = (DEPTH + 1) // 2
N_B_LAYERS = DEPTH // 2
TOK_WIDTH = 3 * D_MODEL // 4
MEM_WIDTH = D_MODEL - TOK_WIDTH
MEM_HEADS = 4
MEM_HEAD_DIM = MEM_WIDTH // MEM_HEADS
N_MEM = 256
A_HEAD_DIM = 64
A_HEADS = TOK_WIDTH // A_HEAD_DIM
A_DECAY_RANK = 96
A_ICLR_RANK = 96
A_GATE_RANK = 256
A_SPLITS = (TOK_WIDTH, 2 * TOK_WIDTH, 3 * TOK_WIDTH, 3 * TOK_WIDTH + A_DECAY_RANK, 3 * TOK_WIDTH + A_DECAY_RANK + A_ICLR_RANK)
A_PROJ = 3 * TOK_WIDTH + A_DECAY_RANK + A_ICLR_RANK + A_GATE_RANK
B_EXPAND = 128
B_HEADS = TOK_WIDTH // B_EXPAND
B_HEAD_V = TOK_WIDTH // B_HEADS
B_PROJ = 4 * TOK_WIDTH
CHUNK = 64
D_FF = 5632
CONV_W = 3
RMS_EPS = 1e-6
GN_EPS = 64e-5

kernel_name = 'hybrid_rwkv7_hgrn2_memxattn_convffn_step'


def rmsnorm(x, g):
    xf = x.astype(jnp.float32)
    y = xf * lax.rsqrt(jnp.mean(xf * xf, axis=-1, keepdims=True) + RMS_EPS)
    return (y * g.astype(jnp.float32)).astype(x.dtype)


def memory_attend(q, mem_k, mem_v):
    s = jnp.einsum('bthd,bmhd->bhtm', q, mem_k).astype(jnp.float32) * (MEM_HEAD_DIM ** -0.5)
    p = jax.nn.softmax(s, axis=-1).astype(mem_v.dtype)
    o = jnp.einsum('bhtm,bmhd->bthd', p, mem_v)
    return o.reshape(q.shape[0], q.shape[1], MEM_WIDTH)


def rwkv7_mix(p, p_prev, s0, P, j):
    b, t, _ = p.shape
    f32 = jnp.float32
    xm = p + P['a_mu'][j] * (p_prev - p)
    r, k, v, xw, xa, xg = jnp.split(xm, A_SPLITS, axis=-1)
    w = -jax.nn.softplus(-(P['a_w0'][j] + jnp.tanh(xw) @ P['a_w2'][j])) - 0.5
    decay = jnp.exp(-jnp.exp(w.astype(f32)))
    a = jax.nn.sigmoid(P['a_a0'][j] + xa @ P['a_a2'][j])
    g = jax.nn.sigmoid(xg) @ P['a_g2'][j]
    heads = lambda z: z.astype(f32).reshape(b, t, A_HEADS, A_HEAD_DIM)
    kk = heads(k * P['a_k_k'][j])
    kk = kk * lax.rsqrt(jnp.maximum(jnp.sum(kk * kk, axis=-1, keepdims=True), 1e-24))
    k = k * (1.0 + (a - 1.0) * P['a_k_a'][j])
    r_h, k_h, v_h, a_h, d_h = heads(r), heads(k), heads(v), heads(a), heads(decay)

    def step(S, inp):
        r_t, k_t, v_t, kk_t, b_t, d_t = inp
        S = (S * d_t[:, :, None, :]
             - jnp.einsum('bhvk,bhk->bhv', S, kk_t)[..., None] * b_t[:, :, None, :]
             + v_t[..., None] * k_t[:, :, None, :])
        return S, jnp.einsum('bhvk,bhk->bhv', S, r_t)

    tm = lambda z: jnp.swapaxes(z, 0, 1)
    s_fin, y = lax.scan(step, s0.astype(f32), (tm(r_h), tm(k_h), tm(v_h), tm(kk), tm(kk * a_h), tm(d_h)))
    y = tm(y)
    mu = jnp.mean(y, axis=-1, keepdims=True)
    var = jnp.mean(jnp.square(y - mu), axis=-1, keepdims=True)
    yn = ((y - mu) * lax.rsqrt(var + GN_EPS)).reshape(b, t, TOK_WIDTH)
    yn = yn * P['a_ln_w'][j].astype(f32) + P['a_ln_b'][j].astype(f32)
    r_k = P['a_r_k'][j].astype(f32).reshape(A_HEADS, A_HEAD_DIM)
    bonus = (jnp.sum(r_h * k_h * r_k, axis=-1, keepdims=True) * v_h).reshape(b, t, TOK_WIDTH)
    out = (yn + bonus) * g.astype(f32)
    return out.astype(p.dtype), s_fin


def gla_chunked(q, k, v, log_f, s0):
    b, t, h, _ = q.shape
    vd = v.shape[-1]
    c = CHUNK if t % CHUNK == 0 else t
    n = t // c
    to_chunks = lambda z: z.astype(jnp.float32).reshape(b, n, c, h, z.shape[-1]).transpose(1, 0, 3, 2, 4)
    qc, kc, vc, gc = to_chunks(q), to_chunks(k), to_chunks(v), to_chunks(log_f)
    bc = jnp.cumsum(gc, axis=3)
    mask = jnp.tril(jnp.ones((c, c), dtype=bool))
    mid = (c - 1) // 2

    def step(S, inp):
        q_, k_, v_, b_ = inp
        m = b_[:, :, mid:mid + 1, :]
        att = jnp.einsum('bhtk,bhsk->bhts', q_ * jnp.exp(b_ - m), k_ * jnp.exp(m - b_))
        att = jnp.where(mask, att, 0.0)
        o = jnp.einsum('bhts,bhsv->bhtv', att, v_) + jnp.einsum('bhtk,bhkv->bhtv', q_ * jnp.exp(b_), S)
        b_last = b_[:, :, -1:, :]
        S = jnp.exp(b_last[:, :, 0, :])[..., None] * S + jnp.einsum('bhsk,bhsv->bhkv', k_ * jnp.exp(b_last - b_), v_)
        return S, o

    s_fin, o = lax.scan(step, s0.astype(jnp.float32), (qc, kc, vc, bc))
    o = o.transpose(1, 0, 3, 2, 4).reshape(b, t, h, vd)
    return o, s_fin


def hgrn2_mix(p, s0, lb, g_norm):
    b, t, _ = p.shape
    f32 = jnp.float32
    q, f, i, og = jnp.split(p, 4, axis=-1)
    fg = lb + (1.0 - lb) * jax.nn.sigmoid(f.astype(f32))
    heads = lambda z, d: z.astype(f32).reshape(b, t, B_HEADS, d)
    o, s_fin = gla_chunked(heads(jax.nn.silu(q), B_EXPAND), heads(1.0 - fg, B_EXPAND),
                           heads(i, B_HEAD_V), heads(jnp.log(fg), B_EXPAND), s0)
    o = rmsnorm(o, g_norm).reshape(b, t, TOK_WIDTH)
    out = o * jax.nn.silu(og.astype(f32))
    return out.astype(p.dtype), s_fin


def conv_ffn(h, buf, w_up, conv_w, conv_b, w_down):
    t = h.shape[1]
    a, v = jnp.split(h @ w_up, 2, axis=-1)
    a_ext = jnp.concatenate([buf.astype(a.dtype), a], axis=1)
    c = conv_b
    for tap in range(CONV_W):
        c = c + a_ext[:, tap:tap + t] * conv_w[tap]
    y = (jax.nn.gelu(c) * v) @ w_down
    return y, a_ext[:, t:]


def trunk(x, mem_k, mem_v, st_rwkv, st_shift, st_hgrn, st_conv, P):
    b, t, _ = x.shape
    lb_all = jnp.cumsum(jax.nn.softmax(P['b_lower_bounds'].astype(jnp.float32), axis=0), axis=0)
    lb_all = lb_all - lb_all[0]
    new_rwkv, new_shift, new_hgrn, new_conv = [], [], [], []
    for layer in range(DEPTH):
        j = layer // N_MIXERS
        h = rmsnorm(x, P['norm_mix'][layer])
        if layer % N_MIXERS == 0:
            h_ext = jnp.concatenate([st_shift[j][:, None, :].astype(h.dtype), h], axis=1)
            p_ext = h_ext @ P['a_w_in'][j]
            p = p_ext[:, 1:]
            tok, s_new = rwkv7_mix(p[..., MEM_WIDTH:], p_ext[:, :-1, MEM_WIDTH:], st_rwkv[j], P, j)
            new_rwkv.append(s_new)
            new_shift.append(h[:, -1])
            w_out = P['a_w_out'][j]
        else:
            p = h @ P['b_w_in'][j]
            tok, s_new = hgrn2_mix(p[..., MEM_WIDTH:], st_hgrn[j], lb_all[layer], P['b_g_norm'][j])
            new_hgrn.append(s_new)
            w_out = P['b_w_out'][j]
        q_mem = p[..., :MEM_WIDTH].reshape(b, t, MEM_HEADS, MEM_HEAD_DIM)
        mem_o = memory_attend(q_mem, mem_k[layer], mem_v[layer])
        x = x + jnp.concatenate([tok, mem_o.astype(tok.dtype)], axis=-1) @ w_out
        f, c_new = conv_ffn(rmsnorm(x, P['norm_ffn'][layer]), st_conv[layer], P['ffn_w_up'][layer],
                            P['ffn_conv_w'][layer], P['ffn_conv_b'][layer], P['ffn_w_down'][layer])
        new_conv.append(c_new)
        x = x + f
    return (rmsnorm(x, P['norm_final']), jnp.stack(new_rwkv), jnp.stack(new_shift),
            jnp.stack(new_hgrn), jnp.stack(new_conv))


def setup_inputs(seed: int = 0) -> dict:
    key = jax.random.key(seed)
    ks = iter(jax.random.split(key, 48))
    nrm = lambda shape, scale=1.0: jax.random.normal(next(ks), shape, jnp.float32) * scale
    uni = lambda shape, lo, hi: jax.random.uniform(next(ks), shape, jnp.float32, lo, hi)
    D = D_MODEL
    return {
        'x_prompt': nrm((BATCH, SEQ, D)),
        'x_sample': nrm((DEC_BATCH, DEC_SEQ, D)),
        'mem_prompt': nrm((BATCH, N_MEM, D)),
        'cache_mem_k': nrm((DEPTH, DEC_BATCH, N_MEM, MEM_HEADS, MEM_HEAD_DIM)),
        'cache_mem_v': nrm((DEPTH, DEC_BATCH, N_MEM, MEM_HEADS, MEM_HEAD_DIM)),
        'state_rwkv': nrm((N_A_LAYERS, DEC_BATCH, A_HEADS, A_HEAD_DIM, A_HEAD_DIM), 0.3),
        'state_shift': nrm((N_A_LAYERS, DEC_BATCH, D)),
        'state_hgrn': nrm((N_B_LAYERS, DEC_BATCH, B_HEADS, B_EXPAND, B_HEAD_V), 0.3),
        'state_conv': nrm((DEPTH, DEC_BATCH, CONV_W - 1, D_FF)),
        'norm_mix': 1.0 + nrm((DEPTH, D), 0.02),
        'norm_ffn': 1.0 + nrm((DEPTH, D), 0.02),
        'norm_final': 1.0 + nrm((D,), 0.02),
        'mem_norm': 1.0 + nrm((DEPTH, D), 0.02),
        'w_mem_kv': nrm((DEPTH, D, 2 * MEM_WIDTH), D ** -0.5),
        'a_w_in': nrm((N_A_LAYERS, D, MEM_WIDTH + A_PROJ), D ** -0.5),
        'a_mu': uni((N_A_LAYERS, A_PROJ), 0.0, 1.0),
        'a_w0': uni((N_A_LAYERS, TOK_WIDTH), -5.0, 0.0),
        'a_w2': nrm((N_A_LAYERS, A_DECAY_RANK, TOK_WIDTH), 0.1 * A_DECAY_RANK ** -0.5),
        'a_a0': nrm((N_A_LAYERS, TOK_WIDTH), 0.1),
        'a_a2': nrm((N_A_LAYERS, A_ICLR_RANK, TOK_WIDTH), 0.1 * A_ICLR_RANK ** -0.5),
        'a_g2': nrm((N_A_LAYERS, A_GATE_RANK, TOK_WIDTH), A_GATE_RANK ** -0.5),
        'a_k_k': 0.85 + nrm((N_A_LAYERS, TOK_WIDTH), 0.05),
        'a_k_a': 1.0 + nrm((N_A_LAYERS, TOK_WIDTH), 0.05),
        'a_r_k': nrm((N_A_LAYERS, TOK_WIDTH), 0.1),
        'a_ln_w': 1.0 + nrm((N_A_LAYERS, TOK_WIDTH), 0.02),
        'a_ln_b': nrm((N_A_LAYERS, TOK_WIDTH), 0.02),
        'a_w_out': nrm((N_A_LAYERS, D, D), D ** -0.5),
        'b_w_in': nrm((N_B_LAYERS, D, MEM_WIDTH + B_PROJ), D ** -0.5),
        'b_lower_bounds': nrm((DEPTH, TOK_WIDTH), 0.1),
        'b_g_norm': 1.0 + nrm((N_B_LAYERS, B_HEAD_V), 0.02),
        'b_w_out': nrm((N_B_LAYERS, D, D), D ** -0.5),
        'ffn_w_up': nrm((DEPTH, D, 2 * D_FF), D ** -0.5),
        'ffn_conv_w': nrm((DEPTH, CONV_W, D_FF), CONV_W ** -0.5),
        'ffn_conv_b': nrm((DEPTH, D_FF), 0.02),
        'ffn_w_down': nrm((DEPTH, D_FF, D), D_FF ** -0.5),
    }


def reference(x_prompt, x_sample, mem_prompt, cache_mem_k, cache_mem_v, state_rwkv, state_shift,
              state_hgrn, state_conv, norm_mix, norm_ffn, norm_final, mem_norm, w_mem_kv,
              a_w_in, a_mu, a_w0, a_w2, a_a0, a_a2, a_g2, a_k_k, a_k_a, a_r_k, a_ln_w, a_ln_b, a_w_out,
              b_w_in, b_lower_bounds, b_g_norm, b_w_out, ffn_w_up, ffn_conv_w, ffn_conv_b, ffn_w_down):
    P = dict(norm_mix=norm_mix, norm_ffn=norm_ffn, norm_final=norm_final,
             a_w_in=a_w_in, a_mu=a_mu, a_w0=a_w0, a_w2=a_w2, a_a0=a_a0, a_a2=a_a2, a_g2=a_g2,
             a_k_k=a_k_k, a_k_a=a_k_a, a_r_k=a_r_k, a_ln_w=a_ln_w, a_ln_b=a_ln_b, a_w_out=a_w_out,
             b_w_in=b_w_in, b_lower_bounds=b_lower_bounds, b_g_norm=b_g_norm, b_w_out=b_w_out,
             ffn_w_up=ffn_w_up, ffn_conv_w=ffn_conv_w, ffn_conv_b=ffn_conv_b, ffn_w_down=ffn_w_down)
    bp = x_prompt.shape[0]
    dt = x_prompt.dtype
    mem_h = rmsnorm(mem_prompt[None], mem_norm[:, None, None, :])
    mem_kv = jnp.einsum('lbmd,ldk->lbmk', mem_h, w_mem_kv)
    mem_k_prompt = mem_kv[..., :MEM_WIDTH].reshape(DEPTH, bp, N_MEM, MEM_HEADS, MEM_HEAD_DIM)
    mem_v_prompt = mem_kv[..., MEM_WIDTH:].reshape(DEPTH, bp, N_MEM, MEM_HEADS, MEM_HEAD_DIM)
    y_prompt, rwkv_prompt, shift_prompt, hgrn_prompt, conv_prompt = trunk(
        x_prompt, mem_k_prompt, mem_v_prompt,
        jnp.zeros((N_A_LAYERS, bp, A_HEADS, A_HEAD_DIM, A_HEAD_DIM), dt),
        jnp.zeros((N_A_LAYERS, bp, D_MODEL), dt),
        jnp.zeros((N_B_LAYERS, bp, B_HEADS, B_EXPAND, B_HEAD_V), dt),
        jnp.zeros((DEPTH, bp, CONV_W - 1, D_FF), dt), P)
    y_sample, rwkv_sample, shift_sample, hgrn_sample, conv_sample = trunk(
        x_sample, cache_mem_k, cache_mem_v, state_rwkv, state_shift, state_hgrn, state_conv, P)
    return (y_prompt, y_sample, mem_k_prompt, mem_v_prompt, rwkv_prompt, rwkv_sample,
            shift_prompt, shift_sample, hgrn_prompt, hgrn_sample, conv_prompt, conv_sample)
```

```python
import numpy as np
from contextlib import ExitStack
import concourse.bass as bass
import concourse.mybir as mybir
from concourse.bass_utils import run_bass_kernel_spmd

F32 = mybir.dt.float32
BF16 = mybir.dt.bfloat16
AF = mybir.ActivationFunctionType
ALU = mybir.AluOpType
AX = mybir.AxisListType
F32R = mybir.dt.float32r


def R(ap):
    return ap.bitcast(F32R)

D = 2048
TP = 2048
NS = 16
TS = 8
TSM = NS * TS
TT = TP + TSM
TTX = TT + NS
NMEM = 256
DFF = 5632
NCA = 5568
NCB = 6656
EPS = 1e-6
GN_EPS = 64e-5

C_ID, C_ONES, C_BLK, C_SL64, C_SU64, C_IU64, C_SL8, C_SU8, C_IU8, C_HIU64, C_HIU8, C_SEL64, C_SEL8, C_BLK8 = range(14)
NCONST = 14


def make_consts():
    c = np.zeros((128, NCONST, 128), np.float32)
    c[:, C_ID, :] = np.eye(128)
    c[:, C_ONES, :] = 1.0
    for h in range(2):
        c[64 * h:64 * h + 64, C_BLK, 64 * h:64 * h + 64] = 1.0
    for (C, sl, su, iu) in ((64, C_SL64, C_SU64, C_IU64), (8, C_SL8, C_SU8, C_IU8)):
        n = 2 * C
        for h in range(2):
            for t in range(C):
                for s in range(C):
                    if s < t:
                        c[h * C + t, sl, h * C + s] = 1.0
                        c[h * C + s, su, h * C + t] = 1.0
                    if s <= t:
                        c[h * C + s, iu, h * C + t] = 1.0
    for (C, hiu) in ((64, C_HIU64), (8, C_HIU8)):
        for t in range(C):
            for s in range(t + 1):
                c[s, hiu, t] = 1.0
    for (C, sel) in ((64, C_SEL64), (8, C_SEL8)):
        for h in range(2):
            for t in range(C):
                c[h * C + t, sel, t] = 1.0
    c[0:8, C_BLK8, 0:64] = 1.0
    c[8:16, C_BLK8, 64:128] = 1.0
    return c


class Res:
    __slots__ = ("w", "r")

    def __init__(self):
        self.w = None
        self.r = {}


class T:
    def __init__(self, t):
        self.t = t
        self.res = Res()

    def __getitem__(self, k):
        return self.t[k]


class Eng:
    def __init__(self, kb, name, h, is_pe=False):
        self.kb = kb
        self.name = name
        self.h = h
        self.is_pe = is_pe
        self.seen = {}
        self.own = set()
        self.sem = None
        self.cnt = 0
        self.ring = []
        self.ri = 0
        self.idx = 0

    def tick(self):
        if self.sem is None or self.idx >= 30000:
            self.sem = self.kb.newsem(self.name)
            self.own.add(self.sem)
            self.cnt = 0
            self.idx = 0
            self.epoch = getattr(self, "epoch", -1) + 1
            self.kb.semkey[self.sem] = (self.name, self.epoch)
        self.idx += 1
        if self.kb.needed is None:
            self.cnt = self.idx
            return (self.sem, self.cnt), True
        if (self.name, self.epoch, self.idx) in self.kb.needed:
            self.cnt += 1
            return (self.sem, self.cnt), True
        return (self.sem, self.cnt), False


class KB:
    def __init__(self, nc, needed=None):
        self.nc = nc
        self.needed = needed
        self.waited = set()
        self.semkey = {}
        self.nsem = 0
        self.pe = Eng(self, "pe", nc.tensor, True)
        self.act = Eng(self, "act", nc.scalar)
        self.dve = Eng(self, "dve", nc.vector)
        self.pool = Eng(self, "pool", nc.gpsimd)
        self.sp = Eng(self, "sp", nc.sync)
        self.engs = [self.pe, self.act, self.dve, self.pool, self.sp]
        self.dma_slots = []
        self.nins = 0

    def newsem(self, name):
        self.nsem += 1
        return self.nc.alloc_semaphore(f"s_{name}_{self.nsem}")

    def _deps(self, reads, writes):
        deps = {}

        def add(ev):
            if ev is not None:
                if deps.get(ev[0], 0) < ev[1]:
                    deps[ev[0]] = ev[1]
        for r in reads:
            add(r.res.w)
        for w in writes:
            add(w.res.w)
            for s, v in w.res.r.items():
                add((s, v))
        return deps

    def _waits(self, eng, deps):
        for sem, val in deps.items():
            if eng.is_pe and sem in eng.own:
                continue
            if eng.seen.get(sem, 0) >= val:
                continue
            self._emit_wait(eng, sem, val)

    def _emit_wait(self, eng, sem, val):
        if val <= 0:
            eng.seen[sem] = val
            return
        eng.h.wait_ge(sem, val)
        eng.seen[sem] = val
        k = self.semkey.get(sem)
        if k is not None and self.needed is None:
            self.waited.add((k[0], k[1], val))

    def _mark(self, ev, reads, writes):
        for r in reads:
            if r.res.r.get(ev[0], 0) < ev[1]:
                r.res.r[ev[0]] = ev[1]
        for w in writes:
            w.res.w = ev
            w.res.r = {}

    def op(self, eng, fn, reads=(), writes=()):
        self._waits(eng, self._deps(reads, writes))
        ins = fn(eng.h)
        ev, do_inc = eng.tick()
        if do_inc:
            ins.then_inc(ev[0], 1)
        self._mark(ev, reads, writes)
        self.nins += 1
        eng.total = getattr(eng, "total", 0) + 1
        return ev

    def dma(self, eng, out, in_, reads=(), writes=(), **kw):
        self._waits(eng, self._deps(reads, writes))
        if len(eng.ring) < 12:
            slot = [self.newsem(eng.name + "d"), 0]
            eng.ring.append(slot)
            self.dma_slots.append(slot)
        else:
            slot = eng.ring[eng.ri % len(eng.ring)]
            eng.ri += 1
            if eng.seen.get(slot[0], 0) < slot[1]:
                self._emit_wait(eng, slot[0], slot[1])
        ins = eng.h.dma_start(out=out, in_=in_, **kw)
        slot[1] += 16
        ins.then_inc(slot[0], 16)
        ev = (slot[0], slot[1])
        self._mark(ev, reads, writes)
        self.nins += 1
        return ev

    def barrier(self):
        evs = {}
        for e in self.engs:
            if e.sem is not None and e.cnt > 0:
                evs[e.sem] = e.cnt
        for s in self.dma_slots:
            if s[1] > 0:
                evs[s[0]] = s[1]
        for e in self.engs:
            for sem, val in evs.items():
                if e.seen.get(sem, 0) >= val:
                    continue
                if sem in e.own and e.is_pe:
                    continue
                self._emit_wait(e, sem, val)


class Ctx:
    pass


_UN = [0]


def uname(n):
    _UN[0] += 1
    return f"{n}_{_UN[0]}"


def build_program(stop_after=None, dbg=False):
    _UN[0] = 0
    nc1, kb1 = _build_program(stop_after, dbg, None)
    _UN[0] = 0
    return _build_program(stop_after, dbg, kb1.waited)


def _build_program(stop_after, dbg, needed):
    nc = bass.Bass("TRN2", target_bir_lowering=False)
    kb = KB(nc, needed)
    g = Ctx()
    g.nc, g.kb = nc, kb
    di = lambda n, s: nc.dram_tensor(n, list(s), F32, kind="ExternalInput").ap()
    do = lambda n, s: nc.dram_tensor(n, list(s), F32, kind="ExternalOutput").ap()
    ds = lambda n, s, dt=F32: nc.dram_tensor(n, list(s), dt, kind="Internal").ap()
    I = Ctx()
    g.I = I
    I.x_prompt = di("x_prompt", (TP, D))
    I.x_sample = di("x_sample", (TSM, D))
    I.mem_prompt = di("mem_prompt", (NMEM, D))
    I.cache_k = di("cache_mem_k", (2, NS, NMEM, 512))
    I.cache_v = di("cache_mem_v", (2, NS, NMEM, 512))
    I.state_rwkv = di("state_rwkv", (NS, 24, 64, 64))
    I.state_shift = di("state_shift", (NS, D))
    I.state_hgrn = di("state_hgrn", (NS, 12, 128, 128))
    I.state_conv = di("state_conv", (2, NS * 2, DFF))
    I.norm_mix = di("norm_mix", (2, D))
    I.norm_ffn = di("norm_ffn", (2, D))
    I.norm_final = di("norm_final", (1, D))
    I.mem_norm = di("mem_norm", (2, D))
    I.w_mem_kv = di("w_mem_kv", (2, D, 1024))
    I.a_w_in = di("a_w_in", (D, NCA))
    I.a_mu = di("a_mu", (5056, 1))
    I.a_w0 = di("a_w0", (1536, 1))
    I.a_w2 = di("a_w2", (96, 1536))
    I.a_a0 = di("a_a0", (1536, 1))
    I.a_a2 = di("a_a2", (96, 1536))
    I.a_g2 = di("a_g2", (256, 1536))
    I.a_k_k = di("a_k_k", (1536, 1))
    I.a_k_a = di("a_k_a", (1536, 1))
    I.a_r_k = di("a_r_k", (1536, 1))
    I.a_ln_w = di("a_ln_w", (1536, 1))
    I.a_ln_b = di("a_ln_b", (1536, 1))
    I.a_w_out = di("a_w_out", (D, D))
    I.b_w_in = di("b_w_in", (D, NCB))
    I.b_lb = di("b_lower_bounds", (2, 1536))
    I.b_g_norm = di("b_g_norm", (128, 1))
    I.b_w_out = di("b_w_out", (D, D))
    I.ffn_w_up = di("ffn_w_up", (2, D, 2 * DFF))
    I.ffn_conv_w = di("ffn_conv_w", (2, 3, DFF))
    I.ffn_conv_b = di("ffn_conv_b", (2, DFF))
    I.ffn_w_down = di("ffn_w_down", (2, DFF, D))
    I.consts = di("consts", (128, NCONST, 128))
    O = Ctx()
    g.O = O
    O.y_prompt = do("y_prompt", (TP, D))
    O.y_sample = do("y_sample", (TSM, D))
    O.mem_k = do("mem_k_prompt", (2, NMEM, 512))
    O.mem_v = do("mem_v_prompt", (2, NMEM, 512))
    O.rwkv_prompt = do("rwkv_prompt", (24 * 64, 64))
    O.rwkv_sample = do("rwkv_sample", (NS, 24 * 64, 64))
    O.shift_prompt = do("shift_prompt", (1, D))
    O.shift_sample = do("shift_sample", (NS, D))
    O.hgrn_prompt = do("hgrn_prompt", (12, 128, 128))
    O.hgrn_sample = do("hgrn_sample", (NS, 12, 128, 128))
    O.conv_prompt = do("conv_prompt", (2, 2, DFF))
    O.conv_sample = do("conv_sample", (2, NS * 2, DFF))
    S = Ctx()
    g.S = S
    S.xres = ds("xres", (TT, D))
    S.pT = ds("pT", (NCB, TTX))
    S.auxT = ds("auxT", (3, 1536, TT))
    S.catT = ds("catT", (D, TT), BF16)
    S.wupb = ds("wupb", (2, 44, 128, 16 * 128), BF16)
    S.wdnb = ds("wdnb", (4, 128, 44 * 512), BF16)
    if dbg:
        O.dbg = do("dbg", (NCB, TTX))
        O.dbg2 = do("dbg2", (TT, D))
        O.dbg3 = nc.dram_tensor("dbg3", [D, TT], BF16, kind="ExternalOutput").ap()
        O.dbg4 = do("dbg4", (3, 1536, TT))

    g.cst = T(nc.alloc_sbuf_tensor("cst", [128, NCONST, 128], F32))
    g.ps = [T(nc.alloc_psum_tensor(f"ps{i}", [128, 512], F32)) for i in range(8)]
    g.psi = 0
    g.KTp = T(nc.alloc_sbuf_tensor("KTp", [128, 2, 4, NMEM], F32))
    g.Vp = T(nc.alloc_sbuf_tensor("Vp", [128, 2, 2, 512], F32))
    kb.dma(kb.sp, g.cst[:], I.consts[:, :, :], writes=[g.cst])

    phases = [("memkv", phase_memkv)]
    for l in range(2):
        phases.append((f"norm{l}", lambda g, l=l: phase_norm_mix(g, l)))
        phases.append((f"inproj{l}", lambda g, l=l: phase_inproj(g, l)))
        phases.append((f"attn{l}", lambda g, l=l: phase_attn(g, l)))
        phases.append((f"mix{l}", phase_rwkv if l == 0 else phase_hgrn))
        phases.append((f"outproj{l}", lambda g, l=l: phase_outproj(g, l)))
        phases.append((f"ffn{l}", lambda g, l=l: phase_ffn(g, l)))
    phases.append(("final", phase_final))
    kb.marks = []
    for name, fn in phases:
        fn(g)
        kb.barrier()
        kb.marks.append((name, getattr(kb.pe, "total", 0), getattr(kb.dve, "total", 0)))
        if stop_after == name:
            break
    if dbg:
        kb.dma(kb.sp, O.dbg[:, :], S.pT[:, :])
        kb.dma(kb.sp, O.dbg2[:, :], S.xres[:, :])
        kb.dma(kb.sp, O.dbg3[:, :], S.catT[:, :])
        kb.dma(kb.sp, O.dbg4[:, :, :], S.auxT[:, :, :])
    kb.barrier()
    return nc, kb


def next_ps(g):
    p = g.ps[g.psi % getattr(g, "ps_n", 8)]
    g.psi += 1
    return p


def rms_rstd(g, es_tiles, xt, n, tag):
    kb = g.kb
    sq, ssq, rstd = es_tiles
    kb.op(kb.act, lambda e: e.activation(out=sq[0:n, :], in_=xt[0:n, :], func=AF.Square, accum_out=ssq[0:n, :]),
          reads=[xt], writes=[sq, ssq])
    kb.op(kb.dve, lambda e: e.tensor_scalar(out=rstd[0:n, :], in0=ssq[0:n, :], scalar1=1.0 / D, scalar2=EPS,
                                            op0=ALU.mult, op1=ALU.add), reads=[ssq], writes=[rstd])
    kb.op(kb.act, lambda e: e.sqrt(out=rstd[0:n, :], in_=rstd[0:n, :]), reads=[rstd], writes=[rstd])
    kb.op(kb.dve, lambda e: e.reciprocal(out=rstd[0:n, :], in_=rstd[0:n, :]), reads=[rstd], writes=[rstd])
    return rstd


def transpose_to_hT(g, h, n, hT, col0, scale_cols=None):
    kb = g.kb
    for q in range(4):
        ps = next_ps(g)
        for i in range(4):
            kc = q * 4 + i
            kb.op(kb.pe, lambda e, kc=kc, i=i: e.transpose(out=ps[:, i * 128:i * 128 + n],
                                                           in_=h[0:n, kc * 128:(kc + 1) * 128],
                                                           identity=g.cst[0:n, C_ID, 0:n]),
                  reads=[h, g.cst], writes=[ps])
        src = ps[:, :].rearrange("p (a b) -> p a b", b=128)[:, :, 0:n]
        dst = hT[:, q * 4:(q + 1) * 4, col0:col0 + n]
        eng = kb.dve if q % 2 == 0 else kb.act
        if eng is kb.dve:
            kb.op(eng, lambda e: e.tensor_copy(out=dst, in_=src), reads=[ps], writes=[hT])
        else:
            kb.op(eng, lambda e: e.copy(out=dst, in_=src), reads=[ps], writes=[hT])


def phase_memkv(g):
    nc, kb, I, O = g.nc, g.kb, g.I, g.O
    with ExitStack() as es:
        al = lambda name, shape, dt=F32: T(es.enter_context(nc.sbuf_tensor(uname(name), shape, dt)))
        xt = [al(f"mk_x{i}", [128, D]) for i in range(2)]
        sq = al("mk_sq", [128, D])
        ssq = al("mk_ssq", [128, 1])
        rstd = al("mk_rstd", [128, 1])
        mn = al("mk_mn", [128, 2, 16])
        hmT = al("mk_hmT", [128, 16, NMEM], F32)
        hml = [al(f"mk_hml{l}", [128, 16, NMEM], BF16) for l in range(2)]
        wf = [al(f"mk_wf{i}", [128, 16, 256]) for i in range(2)]
        wb = al("mk_wb", [128, 16, 1024], BF16)
        stg = [al(f"mk_stg{i}", [128, 512]) for i in range(2)]
        kb.dma(kb.sp, mn[:], I.mem_norm.rearrange("l (kc p) -> p l kc", p=128), writes=[mn],
               allow_slow_non_contiguous=True)
        for mt in range(2):
            kb.dma(kb.sp, xt[mt][:], I.mem_prompt[mt * 128:(mt + 1) * 128, :], writes=[xt[mt]])
            r = rms_rstd(g, (sq, ssq, rstd), xt[mt], 128, "mk")
            kb.op(kb.act, lambda e: e.activation(out=xt[mt][:], in_=xt[mt][:], func=AF.Copy, scale=r[:, 0:1]),
                  reads=[xt[mt], r], writes=[xt[mt]])
            for q in range(4):
                ps = next_ps(g)
                for i in range(4):
                    kc = q * 4 + i
                    kb.op(kb.pe, lambda e, kc=kc, i=i: e.transpose(out=ps[:, i * 128:(i + 1) * 128],
                                                                   in_=xt[mt][:, kc * 128:(kc + 1) * 128],
                                                                   identity=g.cst[:, C_ID, :]),
                          reads=[xt[mt], g.cst], writes=[ps])
                kb.op(kb.dve, lambda e: e.tensor_copy(out=hmT[:, q * 4:(q + 1) * 4, mt * 128:(mt + 1) * 128],
                                                      in_=ps[:, :].rearrange("p (a b) -> p a b", b=128)),
                      reads=[ps], writes=[hmT])
        for l in range(2):
            for kc in range(16):
                kb.op(kb.dve, lambda e, kc=kc: e.tensor_scalar(out=hml[l][:, kc, :], in0=hmT[:, kc, :],
                                                               scalar1=mn[:, l, kc:kc + 1], scalar2=None,
                                                               op0=ALU.mult), reads=[hmT, mn], writes=[hml[l]])
            wsrc = I.w_mem_kv[l].rearrange("(kc p) n -> p kc n", p=128)
            for pc in range(4):
                w = wf[pc % 2]
                kb.dma(kb.sp, w[:], wsrc[:, :, pc * 256:(pc + 1) * 256], writes=[w])
                kb.op(kb.pool, lambda e, pc=pc, w=w: e.tensor_copy(out=wb[:, :, pc * 256:(pc + 1) * 256], in_=w[:]),
                      reads=[w], writes=[wb])
            for mt in range(2):
                for ct in range(2):
                    ps = next_ps(g)
                    for kc in range(16):
                        kb.op(kb.pe, lambda e, kc=kc: e.matmul(ps[:, :], lhsT=hml[l][:, kc, mt * 128:(mt + 1) * 128],
                                                               rhs=wb[:, kc, ct * 512:(ct + 1) * 512],
                                                               start=(kc == 0), stop=(kc == 15)),
                              reads=[hml[l], wb], writes=[ps])
                    if ct == 0:
                        s = stg[mt % 2]
                        kb.op(kb.act, lambda e: e.copy(out=s[:], in_=ps[:, :]), reads=[ps], writes=[s])
                        kb.dma(kb.pool, O.mem_k[l, mt * 128:(mt + 1) * 128, :], s[:], reads=[s])
                    else:
                        kb.op(kb.act, lambda e: e.copy(out=g.Vp[:, l, mt, :], in_=ps[:, :]), reads=[ps], writes=[g.Vp])
                        kb.dma(kb.pool, O.mem_v[l, mt * 128:(mt + 1) * 128, :], g.Vp[:, l, mt, :], reads=[g.Vp])
            for h in range(4):
                ps = next_ps(g)
                for kc in range(16):
                    kb.op(kb.pe, lambda e, kc=kc: e.matmul(ps[:, 0:NMEM], lhsT=wb[:, kc, h * 128:(h + 1) * 128],
                                                           rhs=hml[l][:, kc, :], start=(kc == 0), stop=(kc == 15)),
                          reads=[hml[l], wb], writes=[ps])
                kb.op(kb.dve, lambda e: e.tensor_copy(out=g.KTp[:, l, h, :], in_=ps[:, 0:NMEM]),
                      reads=[ps], writes=[g.KTp])


def phase_norm_mix(g, l):
    nc, kb, I, O, S = g.nc, g.kb, g.I, g.O, g.S
    g.hT_stack = ExitStack()
    ntok = TTX if l == 0 else TT
    g.hT = T(g.hT_stack.enter_context(nc.sbuf_tensor(uname(f"hT{l}"), [128, 16, ntok], BF16)))
    with ExitStack() as es:
        al = lambda name, shape, dt=F32: T(es.enter_context(nc.sbuf_tensor(uname(name), shape, dt)))
        xt = [al(f"nm_x{i}", [128, D]) for i in range(2)]
        ht = [al(f"nm_h{i}", [128, D]) for i in range(2)]
        sq = al("nm_sq", [128, D])
        ssq = al("nm_ssq", [128, 1])
        rstd = al("nm_rstd", [128, 1])
        gbc = al("nm_g", [128, D])
        kb.dma(kb.sp, gbc[:], I.norm_mix[l:l + 1, :].partition_broadcast(128), writes=[gbc])
        for i in range(17):
            x = xt[i % 2]
            h = ht[i % 2]
            if l == 0:
                src = I.x_prompt[i * 128:(i + 1) * 128, :] if i < 16 else I.x_sample[:, :]
            else:
                src = S.xres[i * 128:(i + 1) * 128, :]
            kb.dma(kb.sp, x[:], src, writes=[x])
            if l == 0:
                kb.dma(kb.pool, S.xres[i * 128:(i + 1) * 128, :], x[:], reads=[x])
            r = rms_rstd(g, (sq, ssq, rstd), x, 128, "nm")
            kb.op(kb.dve, lambda e: e.scalar_tensor_tensor(out=h[:], in0=x[:], scalar=r[:, 0:1], in1=gbc[:],
                                                           op0=ALU.mult, op1=ALU.mult),
                  reads=[x, r, gbc], writes=[h])
            if l == 0:
                if i == 15:
                    kb.dma(kb.pool, O.shift_prompt[0:1, :], h[127:128, :], reads=[h])
                if i == 16:
                    kb.dma(kb.pool, O.shift_sample[:, :],
                           h[TS - 1:128:TS, :], reads=[h])
            transpose_to_hT(g, h, 128, g.hT, i * 128)
        if l == 0:
            x = xt[1]
            kb.dma(kb.sp, x[0:NS, :], I.state_shift[:, :], writes=[x])
            transpose_to_hT(g, x, NS, g.hT, TT)


def linear_fm(g, es, wsrc, ncols, KC, actT, tok_tiles, epilogue, tag, wblk=256):
    nc, kb = g.nc, g.kb
    al = lambda name, shape, dt=F32: T(es.enter_context(nc.sbuf_tensor(uname(name), shape, dt)))
    wf = [al(f"{tag}_wf{i}", [128, KC, wblk]) for i in range(2)]
    wb = [al(f"{tag}_wb{i}", [128, KC, wblk], BF16) for i in range(2)]
    nblk = (ncols + wblk - 1) // wblk
    for b in range(nblk):
        c0 = b * wblk
        cw = min(wblk, ncols - c0)
        f = wf[b % 2]
        w = wb[b % 2]
        kb.dma(kb.sp, f[:, :, 0:cw], wsrc[:, :, c0:c0 + cw], writes=[f])
        ceng = kb.pool if b % 2 == 0 else kb.dve
        kb.op(ceng, lambda e: e.tensor_copy(out=w[:, :, 0:cw], in_=f[:, :, 0:cw]), reads=[f], writes=[w])
        for jj in range(0, cw, 128):
            ncj = min(128, cw - jj)
            j = (c0 + jj) // 128
            for ti, (t0, n) in enumerate(tok_tiles):
                ps = next_ps(g)
                for kc in range(KC):
                    kb.op(kb.pe, lambda e, kc=kc: e.matmul(ps[0:ncj, 0:n], lhsT=w[:, kc, jj:jj + ncj],
                                                           rhs=actT[:, kc, t0:t0 + n],
                                                           start=(kc == 0), stop=(kc == KC - 1)),
                          reads=[w, actT], writes=[ps])
                epilogue(j, ncj, ti, t0, n, ps)


def tok_tiles_of(n):
    out = []
    t = 0
    while t < n:
        m = min(512, n - t)
        out.append((t, m))
        t += m
    return out


def phase_inproj(g, l):
    nc, kb, I, O, S = g.nc, g.kb, g.I, g.O, g.S
    ncols = NCA if l == 0 else NCB
    ntok = TTX if l == 0 else TT
    w = (I.a_w_in if l == 0 else I.b_w_in).rearrange("(kc p) n -> p kc n", p=128)
    with ExitStack() as es:
        al = lambda name, shape, dt=F32: T(es.enter_context(nc.sbuf_tensor(uname(name), shape, dt)))
        stg = [al(f"ip_stg{i}", [128, 512]) for i in range(4)]
        cnt = [0]

        def epi(j, ncj, ti, t0, n, ps):
            s = stg[cnt[0] % 4]
            if cnt[0] % 2 == 0:
                kb.op(kb.act, lambda e: e.copy(out=s[0:ncj, 0:n], in_=ps[0:ncj, 0:n]), reads=[ps], writes=[s])
            else:
                kb.op(kb.dve, lambda e: e.tensor_copy(out=s[0:ncj, 0:n], in_=ps[0:ncj, 0:n]), reads=[ps], writes=[s])
            kb.dma(kb.pool, S.pT[j * 128:j * 128 + ncj, t0:t0 + n], s[0:ncj, 0:n], reads=[s])
            cnt[0] += 1
        linear_fm(g, es, w, ncols, 16, g.hT, tok_tiles_of(ntok), epi, f"ip{l}")
    g.hT_stack.close()


def phase_attn(g, l):
    nc, kb, I, O, S = g.nc, g.kb, g.I, g.O, g.S
    scale = 128.0 ** -0.5
    with ExitStack() as es:
        al = lambda name, shape, dt=F32: T(es.enter_context(nc.sbuf_tensor(uname(name), shape, dt)))
        qp = [al(f"at_qp{i}", [128, TP]) for i in range(2)]
        qs = al("at_qs", [128, 4, TSM])
        pb = [al(f"at_p{i}", [128, NMEM]) for i in range(3)]
        pT = [al(f"at_pT{i}", [128, 2, 128]) for i in range(3)]
        sm = [[al(f"at_sm{i}_{k}", [128, 1]) for k in range(4)] for i in range(3)]
        ob = [al(f"at_o{i}", [128, 128], BF16) for i in range(3)]
        kin = [al(f"at_kin{i}", [128, 2, 512]) for i in range(2)]
        vin = [al(f"at_vin{i}", [128, 2, 512]) for i in range(2)]
        kts = [al(f"at_kts{i}", [128, 4, NMEM]) for i in range(2)]
        uc = [0]

        def unit(qt, qap, n, ktT, ktap, vT, vap, dest):
            u = uc[0] % 3
            uc[0] += 1
            p, pt, (mx, nb, sme, rs), o = pb[u], pT[u], sm[u], ob[u]
            ps = next_ps(g)
            kb.op(kb.pe, lambda e: e.matmul(ps[0:n, 0:NMEM], lhsT=qap, rhs=ktap, start=True, stop=True),
                  reads=[qt, ktT], writes=[ps])
            kb.op(kb.dve, lambda e: e.tensor_reduce(out=mx[0:n, :], in_=ps[0:n, 0:NMEM], axis=AX.X, op=ALU.max),
                  reads=[ps], writes=[mx])
            kb.op(kb.dve, lambda e: e.tensor_scalar(out=nb[0:n, :], in0=mx[0:n, :], scalar1=-scale, scalar2=None,
                                                    op0=ALU.mult), reads=[mx], writes=[nb])
            kb.op(kb.act, lambda e: e.activation(out=p[0:n, :], in_=ps[0:n, 0:NMEM], func=AF.Exp, bias=nb[0:n, 0:1],
                                                 scale=scale, accum_out=sme[0:n, :]),
                  reads=[ps, nb], writes=[p, sme])
            kb.op(kb.dve, lambda e: e.reciprocal(out=rs[0:n, :], in_=sme[0:n, :]), reads=[sme], writes=[rs])
            kb.op(kb.dve, lambda e: e.tensor_scalar(out=p[0:n, :], in0=p[0:n, :], scalar1=rs[0:n, 0:1], scalar2=None,
                                                    op0=ALU.mult), reads=[p, rs], writes=[p])
            ps2 = next_ps(g)
            for mc in range(2):
                kb.op(kb.pe, lambda e, mc=mc: e.transpose(out=ps2[:, mc * 128:mc * 128 + n],
                                                          in_=p[0:n, mc * 128:(mc + 1) * 128],
                                                          identity=g.cst[0:n, C_ID, 0:n]),
                      reads=[p, g.cst], writes=[ps2])
            kb.op(kb.act, lambda e: e.copy(out=pt[:, :, 0:n],
                                           in_=ps2[:, 0:256].rearrange("p (a b) -> p a b", b=128)[:, :, 0:n]),
                  reads=[ps2], writes=[pt])
            ps3 = next_ps(g)
            for mc in range(2):
                kb.op(kb.pe, lambda e, mc=mc: e.matmul(ps3[:, 0:n], lhsT=vap(mc), rhs=pt[:, mc, 0:n],
                                                       start=(mc == 0), stop=(mc == 1)),
                      reads=[vT, pt], writes=[ps3])
            kb.op(kb.dve, lambda e: e.tensor_copy(out=o[:, 0:n], in_=ps3[:, 0:n]), reads=[ps3], writes=[o])
            kb.dma(kb.pool, dest, o[:, 0:n], reads=[o])

        for h in range(4):
            q = qp[h % 2]
            kb.dma(kb.sp, q[:], S.pT[h * 128:(h + 1) * 128, 0:TP], writes=[q])
            for i in range(16):
                unit(q, q[:, i * 128:(i + 1) * 128], 128, g.KTp, g.KTp[:, l, h, :], g.Vp,
                     lambda mc, h=h: g.Vp[:, l, mc, h * 128:(h + 1) * 128],
                     S.catT[(12 + h) * 128:(13 + h) * 128, i * 128:(i + 1) * 128])
        kb.dma(kb.sp, qs[:], S.pT[0:512, TP:TT].rearrange("(h p) t -> p h t", p=128), writes=[qs])
        for s in range(NS):
            ki, vi, kt = kin[s % 2], vin[s % 2], kts[s % 2]
            kb.dma(kb.sp, ki[:], I.cache_k[l, s].rearrange("(mt p) c -> p mt c", p=128), writes=[ki])
            kb.dma(kb.sp, vi[:], I.cache_v[l, s].rearrange("(mt p) c -> p mt c", p=128), writes=[vi])
            for hh in range(2):
                ps = next_ps(g)
                for a in range(2):
                    for mt in range(2):
                        h = hh * 2 + a
                        kb.op(kb.pe, lambda e, h=h, mt=mt, a=a: e.transpose(
                            out=ps[:, (a * 2 + mt) * 128:(a * 2 + mt + 1) * 128],
                            in_=ki[:, mt, h * 128:(h + 1) * 128], identity=g.cst[:, C_ID, :]),
                            reads=[ki, g.cst], writes=[ps])
                kb.op(kb.act, lambda e: e.copy(out=kt[:, hh * 2:hh * 2 + 2, :],
                                               in_=ps[:, :].rearrange("p (a b) -> p a b", b=256)),
                      reads=[ps], writes=[kt])
            for h in range(4):
                unit(qs, qs[:, h, s * TS:(s + 1) * TS], TS, kt, kt[:, h, :], vi,
                     lambda mc, h=h, vi=vi: vi[:, mc, h * 128:(h + 1) * 128],
                     S.catT[(12 + h) * 128:(13 + h) * 128, TP + s * TS:TP + (s + 1) * TS])


def phase_outproj(g, l):
    nc, kb, I, O, S = g.nc, g.kb, g.I, g.O, g.S
    w = (I.a_w_out if l == 0 else I.b_w_out).rearrange("(kc p) n -> p kc n", p=128)
    with ExitStack() as es:
        al = lambda name, shape, dt=F32: T(es.enter_context(nc.sbuf_tensor(uname(name), shape, dt)))
        catT = al("op_cat", [128, 16, TT], BF16)
        wb = al("op_wb", [128, 16, D], BF16)
        wf = [al(f"op_wf{i}", [128, 16, 128]) for i in range(2)]
        xt = [al(f"op_x{i}", [128, D]) for i in range(2)]
        kb.dma(kb.sp, catT[:], S.catT.rearrange("(kc p) t -> p kc t", p=128), writes=[catT])
        for pc in range(16):
            f = wf[pc % 2]
            kb.dma(kb.sp, f[:], w[:, :, pc * 128:(pc + 1) * 128], writes=[f])
            ce = kb.pool if pc % 2 == 0 else kb.act
            if ce is kb.pool:
                kb.op(ce, lambda e: e.tensor_copy(out=wb[:, :, pc * 128:(pc + 1) * 128], in_=f[:]), reads=[f], writes=[wb])
            else:
                kb.op(ce, lambda e: e.copy(out=wb[:, :, pc * 128:(pc + 1) * 128], in_=f[:]), reads=[f], writes=[wb])
        for i in range(17):
            x = xt[i % 2]
            kb.dma(kb.sp, x[:], S.xres[i * 128:(i + 1) * 128, :], writes=[x])
            for ct in range(4):
                ps = next_ps(g)
                for kc in range(16):
                    kb.op(kb.pe, lambda e, kc=kc: e.matmul(ps[:, :], lhsT=catT[:, kc, i * 128:(i + 1) * 128],
                                                           rhs=wb[:, kc, ct * 512:(ct + 1) * 512],
                                                           start=(kc == 0), stop=(kc == 15)),
                          reads=[catT, wb], writes=[ps])
                kb.op(kb.dve, lambda e: e.tensor_tensor(out=x[:, ct * 512:(ct + 1) * 512], in0=ps[:, :],
                                                        in1=x[:, ct * 512:(ct + 1) * 512], op=ALU.add),
                      reads=[ps, x], writes=[x])
            kb.dma(kb.pool, S.xres[i * 128:(i + 1) * 128, :], x[:], reads=[x])


FFN_GROUPS = [(0, 768, False), (768, 768, False), (1536, 512, True)]


def phase_ffn(g, l):
    nc, kb, I, O, S = g.nc, g.kb, g.I, g.O, g.S
    wup = I.ffn_w_up[l].rearrange("(kc p) n -> p kc n", p=128)
    wdn = I.ffn_w_down[l].rearrange("(j p) n -> p j n", p=128)
    with ExitStack() as es0:
        al0 = lambda name, shape, dt=F32: T(es0.enter_context(nc.sbuf_tensor(uname(name), shape, dt)))
        carry = al0("ff_carry", [128, 44, 2])
        cvo = al0("ff_cvo", [128, 44, 2 + 2 * NS])
        scA = al0("ff_scA", [128, 44, 2 * NS])
        cw = al0("ff_cw", [128, 3, 44])
        cb = al0("ff_cb", [128, 44])
        kb.dma(kb.sp, cw[:], I.ffn_conv_w[l].rearrange("k (j p) -> p k j", p=128), writes=[cw],
               allow_slow_non_contiguous=True)
        kb.dma(kb.sp, cb[:], I.ffn_conv_b[l:l + 1, :].rearrange("o (j p) -> p (o j)", p=128), writes=[cb],
               allow_slow_non_contiguous=True)
        kb.op(kb.dve, lambda e: e.memset(carry[:], 0.0), writes=[carry])
        with ExitStack() as es:
            al = lambda name, shape, dt=F32: T(es.enter_context(nc.sbuf_tensor(uname(name), shape, dt)))
            sc = al("ff_sc", [2 * NS, DFF])
            kb.dma(kb.sp, sc[:], I.state_conv[l], writes=[sc])
            for j in range(44):
                ps = next_ps(g)
                kb.op(kb.pe, lambda e: e.transpose(out=ps[:, 0:2 * NS], in_=sc[:, j * 128:(j + 1) * 128],
                                                   identity=g.cst[0:2 * NS, C_ID, 0:2 * NS]),
                      reads=[sc, g.cst], writes=[ps])
                kb.op(kb.dve, lambda e: e.tensor_copy(out=scA[:, j, :], in_=ps[:, 0:2 * NS]), reads=[ps], writes=[scA])
        kb.barrier()
        for gi, (p0, pn, has_s) in enumerate(FFN_GROUPS):
            ntok = pn + (TSM if has_s else 0)
            with ExitStack() as esg:
                alg = lambda name, shape, dt=F32: T(esg.enter_context(nc.sbuf_tensor(uname(name), shape, dt)))
                gT = alg("ff_gT", [128, 44, ntok], BF16)
                with ExitStack() as es:
                    al = lambda name, shape, dt=F32: T(es.enter_context(nc.sbuf_tensor(uname(name), shape, dt)))
                    hT2 = al("ff_hT", [128, 16, ntok], BF16)
                    with ExitStack() as es2:
                        al2 = lambda name, shape, dt=F32: T(es2.enter_context(nc.sbuf_tensor(uname(name), shape, dt)))
                        xt = [al2(f"ff_x{i}", [128, D]) for i in range(2)]
                        ht = [al2(f"ff_h{i}", [128, D]) for i in range(2)]
                        sq = al2("ff_sq", [128, D])
                        ssq = al2("ff_ssq", [128, 1])
                        rstd = al2("ff_rstd", [128, 1])
                        gbc = al2("ff_g", [128, D])
                        kb.dma(kb.sp, gbc[:], I.norm_ffn[l:l + 1, :].partition_broadcast(128), writes=[gbc])
                        rows = [p0 + k * 128 for k in range(pn // 128)] + ([TP] if has_s else [])
                        for k, r0 in enumerate(rows):
                            x, h = xt[k % 2], ht[k % 2]
                            kb.dma(kb.sp, x[:], S.xres[r0:r0 + 128, :], writes=[x])
                            r = rms_rstd(g, (sq, ssq, rstd), x, 128, "ff")
                            kb.op(kb.dve, lambda e: e.scalar_tensor_tensor(out=h[:], in0=x[:], scalar=r[:, 0:1],
                                                                           in1=gbc[:], op0=ALU.mult, op1=ALU.mult),
                                  reads=[x, r, gbc], writes=[h])
                            transpose_to_hT(g, h, 128, hT2, k * 128)
                    kb.barrier()
                    wfa = [al(f"ff_wfa{i}", [128, 16, 128]) for i in range(3)]
                    wfv = [al(f"ff_wfv{i}", [128, 16, 128]) for i in range(3)]
                    wba = [al(f"ff_wba{i}", [128, 16, 128], BF16) for i in range(2)]
                    wbv = [al(f"ff_wbv{i}", [128, 16, 128], BF16) for i in range(2)]
                    aext = [al(f"ff_ae{i}", [128, 2 + pn]) for i in range(2)]
                    aexs = [al(f"ff_as{i}", [128, NS, TS + 2]) for i in range(2)]
                    tmp = [al(f"ff_t{i}", [128, 512]) for i in range(2)]
                    tmp2 = [al(f"ff_u{i}", [128, 512]) for i in range(2)]
                    ptiles = tok_tiles_of(pn)
                    tc = [0]
                    for j in range(44):
                        fa, fv, ba, bv = wfa[j % 3], wfv[j % 3], wba[j % 2], wbv[j % 2]
                        ae, asx = aext[j % 2], aexs[j % 2]
                        if gi == 0:
                            kb.dma(kb.sp, fa[:], wup[:, :, j * 128:(j + 1) * 128], writes=[fa])
                            kb.dma(kb.sp, fv[:], wup[:, :, DFF + j * 128:DFF + (j + 1) * 128], writes=[fv])
                            kb.op(kb.act, lambda e: e.copy(out=ba[:], in_=fa[:]), reads=[fa], writes=[ba])
                            kb.op(kb.pool, lambda e: e.tensor_copy(out=bv[:], in_=fv[:]), reads=[fv], writes=[bv])
                            kb.dma(kb.pool, S.wupb[0, j], ba[:, :, :].rearrange("p a b -> p (a b)"), reads=[ba])
                            kb.dma(kb.pool, S.wupb[1, j], bv[:, :, :].rearrange("p a b -> p (a b)"), reads=[bv])
                        else:
                            kb.dma(kb.sp, ba[:, :, :].rearrange("p a b -> p (a b)"), S.wupb[0, j], writes=[ba])
                            kb.dma(kb.sp, bv[:, :, :].rearrange("p a b -> p (a b)"), S.wupb[1, j], writes=[bv])
                        kb.op(kb.dve, lambda e: e.tensor_copy(out=ae[:, 0:2], in_=carry[:, j, :]), reads=[carry], writes=[ae])
                        for (t0, n) in ptiles:
                            psa = next_ps(g)
                            for kc in range(16):
                                kb.op(kb.pe, lambda e, kc=kc: e.matmul(psa[:, 0:n], lhsT=ba[:, kc, :],
                                                                       rhs=hT2[:, kc, t0:t0 + n],
                                                                       start=(kc == 0), stop=(kc == 15)),
                                      reads=[ba, hT2], writes=[psa])
                            psv = next_ps(g)
                            for kc in range(16):
                                kb.op(kb.pe, lambda e, kc=kc: e.matmul(psv[:, 0:n], lhsT=bv[:, kc, :],
                                                                       rhs=hT2[:, kc, t0:t0 + n],
                                                                       start=(kc == 0), stop=(kc == 15)),
                                      reads=[bv, hT2], writes=[psv])
                            kb.op(kb.act, lambda e: e.copy(out=ae[:, 2 + t0:2 + t0 + n], in_=psa[:, 0:n]),
                                  reads=[psa], writes=[ae])
                            t1, t2 = tmp[tc[0] % 2], tmp2[tc[0] % 2]
                            tc[0] += 1
                            kb.op(kb.dve, lambda e: e.tensor_scalar(out=t1[:, 0:n], in0=ae[:, t0:t0 + n],
                                                                    scalar1=cw[:, 0, j:j + 1], scalar2=cb[:, j:j + 1],
                                                                    op0=ALU.mult, op1=ALU.add),
                                  reads=[ae, cw, cb], writes=[t1])
                            for tap in (1, 2):
                                kb.op(kb.dve, lambda e, tap=tap: e.scalar_tensor_tensor(
                                    out=t1[:, 0:n], in0=ae[:, t0 + tap:t0 + tap + n], scalar=cw[:, tap, j:j + 1],
                                    in1=t1[:, 0:n], op0=ALU.mult, op1=ALU.add), reads=[ae, cw, t1], writes=[t1])
                            kb.op(kb.act, lambda e: e.activation(out=t2[:, 0:n], in_=t1[:, 0:n], func=AF.Gelu_apprx_tanh),
                                  reads=[t1], writes=[t2])
                            kb.op(kb.dve, lambda e: e.tensor_tensor(out=gT[:, j, t0:t0 + n], in0=psv[:, 0:n],
                                                                    in1=t2[:, 0:n], op=ALU.mult),
                                  reads=[psv, t2], writes=[gT])
                        if gi < len(FFN_GROUPS) - 1:
                            kb.op(kb.dve, lambda e: e.tensor_copy(out=carry[:, j, :], in_=ae[:, pn:pn + 2]),
                                  reads=[ae], writes=[carry])
                        else:
                            kb.op(kb.dve, lambda e: e.tensor_copy(out=cvo[:, j, 0:2], in_=ae[:, pn:pn + 2]),
                                  reads=[ae], writes=[cvo])
                        if has_s:
                            psa = next_ps(g)
                            for kc in range(16):
                                kb.op(kb.pe, lambda e, kc=kc: e.matmul(psa[:, 0:TSM], lhsT=ba[:, kc, :],
                                                                       rhs=hT2[:, kc, pn:pn + TSM],
                                                                       start=(kc == 0), stop=(kc == 15)),
                                      reads=[ba, hT2], writes=[psa])
                            psv = next_ps(g)
                            for kc in range(16):
                                kb.op(kb.pe, lambda e, kc=kc: e.matmul(psv[:, 0:TSM], lhsT=bv[:, kc, :],
                                                                       rhs=hT2[:, kc, pn:pn + TSM],
                                                                       start=(kc == 0), stop=(kc == 15)),
                                      reads=[bv, hT2], writes=[psv])
                            kb.op(kb.act, lambda e: e.copy(out=asx[:, :, 2:2 + TS],
                                                           in_=psa[:, 0:TSM].rearrange("p (s t) -> p s t", t=TS)),
                                  reads=[psa], writes=[asx])
                            kb.op(kb.dve, lambda e: e.tensor_copy(out=asx[:, :, 0:2],
                                                                  in_=scA[:, j, :].rearrange("p (s t) -> p s t", t=2)),
                                  reads=[scA], writes=[asx])
                            kb.op(kb.dve, lambda e: e.tensor_copy(out=cvo[:, j, 2:2 + 2 * NS].rearrange("p (s t) -> p s t", t=2),
                                                                  in_=asx[:, :, TS:TS + 2]), reads=[asx], writes=[cvo])
                            t1, t2 = tmp[tc[0] % 2], tmp2[tc[0] % 2]
                            tc[0] += 1
                            v3 = lambda tt: tt[:, 0:TSM].rearrange("p (s t) -> p s t", t=TS)
                            kb.op(kb.dve, lambda e: e.tensor_scalar(out=v3(t1), in0=asx[:, :, 0:TS],
                                                                    scalar1=cw[:, 0, j:j + 1], scalar2=cb[:, j:j + 1],
                                                                    op0=ALU.mult, op1=ALU.add),
                                  reads=[asx, cw, cb], writes=[t1])
                            for tap in (1, 2):
                                kb.op(kb.dve, lambda e, tap=tap: e.scalar_tensor_tensor(
                                    out=v3(t1), in0=asx[:, :, tap:tap + TS], scalar=cw[:, tap, j:j + 1],
                                    in1=v3(t1), op0=ALU.mult, op1=ALU.add), reads=[asx, cw, t1], writes=[t1])
                            kb.op(kb.act, lambda e: e.activation(out=t2[:, 0:TSM], in_=t1[:, 0:TSM], func=AF.Gelu_apprx_tanh),
                                  reads=[t1], writes=[t2])
                            kb.op(kb.dve, lambda e: e.tensor_tensor(out=gT[:, j, pn:pn + TSM], in0=psv[:, 0:TSM],
                                                                    in1=t2[:, 0:TSM], op=ALU.mult),
                                  reads=[psv, t2], writes=[gT])
                kb.barrier()
                with ExitStack() as es:
                    al = lambda name, shape, dt=F32: T(es.enter_context(nc.sbuf_tensor(uname(name), shape, dt)))
                    wdbs = [al(f"ff_wdb{i}", [128, 44, 512], BF16) for i in range(2)]
                    wdf = [al(f"ff_wdf{i}", [128, 2, 512]) for i in range(2)]
                    xs = [al(f"ff_xs{i}", [128, 512]) for i in range(3)]
                    rows = [(p0 + k * 128, k * 128) for k in range(pn // 128)] + ([(TP, pn)] if has_s else [])
                    xc = [0]
                    for ct in range(4):
                        wdb = wdbs[ct % 2]
                        if gi > 0:
                            kb.dma(kb.sp, wdb[:, :, :].rearrange("p a b -> p (a b)"), S.wdnb[ct], writes=[wdb])
                        for pc in (range(22) if gi == 0 else []):
                            f = wdf[pc % 2]
                            kb.dma(kb.sp, f[:], wdn[:, pc * 2:(pc + 1) * 2, ct * 512:(ct + 1) * 512], writes=[f])
                            if pc % 3 == 0:
                                kb.op(kb.pool, lambda e: e.tensor_copy(out=wdb[:, pc * 2:(pc + 1) * 2, :], in_=f[:]),
                                      reads=[f], writes=[wdb])
                            else:
                                kb.op(kb.act, lambda e: e.copy(out=wdb[:, pc * 2:(pc + 1) * 2, :], in_=f[:]),
                                      reads=[f], writes=[wdb])
                        if gi == 0:
                            kb.dma(kb.pool, S.wdnb[ct], wdb[:, :, :].rearrange("p a b -> p (a b)"), reads=[wdb])
                        for b0 in range(0, len(rows), 4):
                            batch = rows[b0:b0 + 4]
                            pss = [next_ps(g) for _ in batch]
                            for j in range(44):
                                for (r0, c0), ps in zip(batch, pss):
                                    kb.op(kb.pe, lambda e, c0=c0, ps=ps: e.matmul(ps[:, :], lhsT=gT[:, j, c0:c0 + 128],
                                                                                  rhs=wdb[:, j, :],
                                                                                  start=(j == 0), stop=(j == 43)),
                                          reads=[gT, wdb], writes=[ps])
                            for (r0, c0), ps in zip(batch, pss):
                                x = xs[xc[0] % 3]
                                xc[0] += 1
                                kb.dma(kb.sp, x[:], S.xres[r0:r0 + 128, ct * 512:(ct + 1) * 512], writes=[x])
                                kb.op(kb.dve, lambda e: e.tensor_tensor(out=x[:], in0=ps[:, :], in1=x[:], op=ALU.add),
                                      reads=[ps, x], writes=[x])
                                kb.dma(kb.pool, S.xres[r0:r0 + 128, ct * 512:(ct + 1) * 512], x[:], reads=[x])
            kb.barrier()
        with ExitStack() as es:
            al = lambda name, shape, dt=F32: T(es.enter_context(nc.sbuf_tensor(uname(name), shape, dt)))
            co = al("ff_co", [2 + 2 * NS, DFF])
            nr = 2 + 2 * NS
            for j in range(44):
                ps = next_ps(g)
                kb.op(kb.pe, lambda e: e.transpose(out=ps[0:nr, 0:128], in_=cvo[:, j, :], identity=g.cst[:, C_ID, :]),
                      reads=[cvo, g.cst], writes=[ps])
                kb.op(kb.dve, lambda e: e.tensor_copy(out=co[:, j * 128:(j + 1) * 128], in_=ps[0:nr, 0:128]),
                      reads=[ps], writes=[co])
            kb.dma(kb.pool, O.conv_prompt[l], co[0:2, :], reads=[co])
            kb.dma(kb.pool, O.conv_sample[l], co[2:nr, :], reads=[co])
            kb.barrier()


def phase_final(g):
    nc, kb, I, O, S = g.nc, g.kb, g.I, g.O, g.S
    with ExitStack() as es:
        al = lambda name, shape, dt=F32: T(es.enter_context(nc.sbuf_tensor(uname(name), shape, dt)))
        xt = [al(f"fn_x{i}", [128, D]) for i in range(2)]
        ht = [al(f"fn_h{i}", [128, D]) for i in range(2)]
        sq = al("fn_sq", [128, D])
        ssq = al("fn_ssq", [128, 1])
        rstd = al("fn_rstd", [128, 1])
        gbc = al("fn_g", [128, D])
        kb.dma(kb.sp, gbc[:], I.norm_final[0:1, :].partition_broadcast(128), writes=[gbc])
        for i in range(17):
            x, h = xt[i % 2], ht[i % 2]
            kb.dma(kb.sp, x[:], S.xres[i * 128:(i + 1) * 128, :], writes=[x])
            r = rms_rstd(g, (sq, ssq, rstd), x, 128, "fn")
            kb.op(kb.dve, lambda e: e.scalar_tensor_tensor(out=h[:], in0=x[:], scalar=r[:, 0:1], in1=gbc[:],
                                                           op0=ALU.mult, op1=ALU.mult), reads=[x, r, gbc], writes=[h])
            dst = O.y_prompt[i * 128:(i + 1) * 128, :] if i < 16 else O.y_sample[:, :]
            kb.dma(kb.pool, dst, h[:], reads=[h])


import math
LOG_SCALE = -math.exp(-0.5)
SEGS = [(k * 512, 512, 64, 8, False) for k in range(4)] + [(TP, TSM, 8, NS, True)]


def pvec(g, dst_ap, src, T_dst):
    g.kb.dma(g.kb.sp, dst_ap, src.rearrange("(c p) o -> p (c o)", p=128), writes=[T_dst],
             allow_slow_non_contiguous=True)


def phase_rwkv(g):
    nc, kb, I, O, S = g.nc, g.kb, g.I, g.O, g.S
    dve, act, pe, pool, sp = kb.dve, kb.act, kb.pe, kb.pool, kb.sp
    cst = g.cst
    with ExitStack() as es:
        al = lambda name, shape, dt=F32: T(es.enter_context(nc.sbuf_tensor(uname(name), shape, dt)))
        praw = al("ra_p", [128, TTX])
        dtm = al("ra_d", [128, TT])
        xw = al("ra_xw", [128, TT])
        xa = al("ra_xa", [128, TT])
        xg = al("ra_xg", [128, 2, TT])
        mus = al("ra_mu", [128, 4])
        w2 = al("ra_w2", [96, 1536])
        a2 = al("ra_a2", [96, 1536])
        g2 = al("ra_g2", [128, 2, 1536])
        w0 = al("ra_w0", [128, 12])
        a0 = al("ra_a0", [128, 12])
        stg = [al(f"ra_stg{i}", [128, 512]) for i in range(4)]
        kb.dma(sp, w2[:], I.a_w2[:, :], writes=[w2])
        kb.dma(sp, a2[:], I.a_a2[:, :], writes=[a2])
        kb.dma(sp, g2[:], I.a_g2.rearrange("(kc p) n -> p kc n", p=128), writes=[g2])
        pvec(g, w0[:], I.a_w0, w0)
        pvec(g, a0[:], I.a_a0, a0)
        kb.dma(sp, mus[0:96, 0:1], I.a_mu[4608:4704, :], writes=[mus])
        kb.dma(sp, mus[0:96, 1:2], I.a_mu[4704:4800, :], writes=[mus])
        kb.dma(sp, mus[:, 2:3], I.a_mu[4800:4928, :], writes=[mus])
        kb.dma(sp, mus[:, 3:4], I.a_mu[4928:5056, :], writes=[mus])

        def tshift_full(row0, P, mucol, out_ap, out_T, func):
            kb.dma(sp, praw[0:P, :], S.pT[row0:row0 + P, 0:TTX], writes=[praw])
            p = praw
            kb.op(dve, lambda e: e.tensor_tensor(out=dtm[0:P, 1:TP], in0=p[0:P, 0:TP - 1], in1=p[0:P, 1:TP],
                                                 op=ALU.subtract), reads=[p], writes=[dtm])
            kb.op(dve, lambda e: e.tensor_scalar(out=dtm[0:P, 0:1], in0=p[0:P, 0:1], scalar1=-1.0, scalar2=None,
                                                 op0=ALU.mult), reads=[p], writes=[dtm])
            p3 = p[0:P, TP:TT].rearrange("p (s t) -> p s t", t=TS)
            d3 = dtm[0:P, TP:TT].rearrange("p (s t) -> p s t", t=TS)
            kb.op(dve, lambda e: e.tensor_tensor(out=d3[:, :, 1:TS], in0=p3[:, :, 0:TS - 1], in1=p3[:, :, 1:TS],
                                                 op=ALU.subtract), reads=[p], writes=[dtm])
            kb.op(dve, lambda e: e.tensor_tensor(out=d3[:, :, 0:1],
                                                 in0=p[0:P, TT:TTX].rearrange("p (s o) -> p s o", o=1),
                                                 in1=p3[:, :, 0:1], op=ALU.subtract), reads=[p], writes=[dtm])
            kb.op(dve, lambda e: e.scalar_tensor_tensor(out=out_ap, in0=dtm[0:P, 0:TT], scalar=mus[0:P, mucol:mucol + 1],
                                                        in1=p[0:P, 0:TT], op0=ALU.mult, op1=ALU.add),
                  reads=[dtm, mus, p], writes=[out_T])
            if func is not None:
                kb.op(act, lambda e: e.activation(out=out_ap, in_=out_ap, func=func), reads=[out_T], writes=[out_T])

        tshift_full(5120, 96, 0, xw[0:96, :], xw, AF.Tanh)
        tshift_full(5216, 96, 1, xa[0:96, :], xa, None)
        tshift_full(5312, 128, 2, xg[:, 0, :], xg, AF.Sigmoid)
        tshift_full(5440, 128, 3, xg[:, 1, :], xg, AF.Sigmoid)
        k = 0
        for c12 in range(12):
            cs = slice(c12 * 128, (c12 + 1) * 128)
            for (t0, n) in tok_tiles_of(TT):
                ps = next_ps(g)
                kb.op(pe, lambda e: e.matmul(ps[:, 0:n], lhsT=w2[0:96, cs], rhs=xw[0:96, t0:t0 + n], start=True, stop=True),
                      reads=[w2, xw], writes=[ps])
                s_ = stg[k % 4]; k += 1
                kb.op(act, lambda e: e.activation(out=s_[:, 0:n], in_=ps[:, 0:n], func=AF.Sigmoid,
                                                  bias=w0[:, c12:c12 + 1], scale=1.0), reads=[ps, w0], writes=[s_])
                kb.op(dve, lambda e: e.tensor_scalar(out=s_[:, 0:n], in0=s_[:, 0:n], scalar1=LOG_SCALE, scalar2=None,
                                                     op0=ALU.mult), reads=[s_], writes=[s_])
                kb.dma(pool, S.auxT[0, cs, t0:t0 + n], s_[:, 0:n], reads=[s_])
                ps = next_ps(g)
                kb.op(pe, lambda e: e.matmul(ps[:, 0:n], lhsT=a2[0:96, cs], rhs=xa[0:96, t0:t0 + n], start=True, stop=True),
                      reads=[a2, xa], writes=[ps])
                s_ = stg[k % 4]; k += 1
                kb.op(act, lambda e: e.activation(out=s_[:, 0:n], in_=ps[:, 0:n], func=AF.Sigmoid,
                                                  bias=a0[:, c12:c12 + 1], scale=1.0), reads=[ps, a0], writes=[s_])
                kb.dma(pool, S.auxT[1, cs, t0:t0 + n], s_[:, 0:n], reads=[s_])
                ps = next_ps(g)
                for kc in range(2):
                    kb.op(pe, lambda e, kc=kc: e.matmul(ps[:, 0:n], lhsT=g2[:, kc, cs], rhs=xg[:, kc, t0:t0 + n],
                                                        start=(kc == 0), stop=(kc == 1)), reads=[g2, xg], writes=[ps])
                s_ = stg[k % 4]; k += 1
                kb.op(dve, lambda e: e.tensor_copy(out=s_[:, 0:n], in_=ps[:, 0:n]), reads=[ps], writes=[s_])
                kb.dma(pool, S.auxT[2, cs, t0:t0 + n], s_[:, 0:n], reads=[s_])
    kb.barrier()
    rwkv_scan(g)


def run_interleaved(a, b):
    gens = [x for x in (b, a) if x is not None]
    while gens:
        for x in list(gens):
            try:
                next(x)
            except StopIteration:
                gens.remove(x)


RSEGS = [(k * 256, 256, 64, 4, False) for k in range(8)] + [(TP, TSM, 8, NS, True)]


def rwkv_scan(g):
    nc, kb, I, O, S = g.nc, g.kb, g.I, g.O, g.S
    dve, act, pe, pool, sp = kb.dve, kb.act, kb.pe, kb.pool, kb.sp
    cst = g.cst
    g.ps_n = 7
    psO = g.ps[7]
    Rb = lambda ap: ap
    with ExitStack() as es:
        al = lambda name, shape, dt=F32: T(es.enter_context(nc.sbuf_tensor(uname(name), shape, dt)))
        vec = al("rw_vec", [128, 12, 8])
        for idx, src in enumerate([I.a_mu[0:1536, :], I.a_mu[1536:3072, :], I.a_mu[3072:4608, :], I.a_k_k, I.a_k_a,
                                   I.a_r_k, I.a_ln_w, I.a_ln_b]):
            pvec(g, vec[:, :, idx], src, vec)
        omk = al("rw_omk", [128, 12])
        kb.op(dve, lambda e: e.tensor_scalar(out=omk[:], in0=vec[:, :, 4], scalar1=-1.0, scalar2=1.0,
                                             op0=ALU.mult, op1=ALU.add), reads=[vec], writes=[omk])
        cr = al("rw_cr", [128, 3, 128], BF16)
        idb = al("rw_idb", [128, 128], BF16)
        kb.op(dve, lambda e: e.tensor_copy(out=idb[:], in_=cst[:, C_ID, :]), reads=[cst], writes=[idb])
        for i, cslot in enumerate((C_BLK, C_SEL64, C_SEL8)):
            kb.op(dve, lambda e: e.tensor_copy(out=Rb(cr[:, i, :]), in_=cst[:, cslot, :]), reads=[cst], writes=[cr])
        raw = [al(f"rw_raw{i}", [128, 288]) for i in range(3)]
        names = ["a", "ld", "gt", "d", "r", "k", "v", "kk", "b", "L", "Lm", "LCmL", "bon", "tmp", "eL", "enL", "eLm",
                 "eLC", "ynT", "o1", "tmpr", "d2", "d3", "kp", "ta", "tmpb"]
        Bsets = [{nm: al(f"rw_{nm}{q}", [128, 256], BF16 if nm in ("tmpr", "tmpb") else F32) for nm in names} for q in range(2)]
        ocat = [al(f"rw_ocat{q}", [128, 256], BF16) for q in range(2)]
        LCs = [al(f"rw_LC{q}", [128, 16]) for q in range(2)]
        GCs = [al(f"rw_GC{q}", [128, 16]) for q in range(2)]
        padn = ["Rt", "Kh", "Bh", "KK", "Kb", "Bb", "Vp", "KKr", "Rtr"]
        pdt = lambda nm: F32 if nm in ("KKr", "Rtr") else BF16
        padP = [{nm: al(f"rw_pp{nm}{q}", [128, 4, 128], pdt(nm)) for nm in padn} for q in range(2)]
        padS = [{nm: al(f"rw_ps{nm}{q}", [128, 24, 16], pdt(nm)) for nm in padn} for q in range(2)]
        zt = al("rw_zero", [128, 512])
        kb.op(pool, lambda e: e.memset(zt[:], 0.0), writes=[zt])
        for q in range(2):
            for nm in padn:
                kb.op(pool, lambda e: e.tensor_copy(out=(R if nm in ("KKr", "Rtr") else Rb)(padP[q][nm][:]), in_=zt[:, 0:512].rearrange("p (a b) -> p a b", b=128)),
                      reads=[zt], writes=[padP[q][nm]])
                kb.op(pool, lambda e: e.tensor_copy(out=(R if nm in ("KKr", "Rtr") else Rb)(padS[q][nm][:]), in_=zt[:, 0:384].rearrange("p (a b) -> p a b", b=16)),
                      reads=[zt], writes=[padS[q][nm]])
        un = ["A", "Bm", "AkkT", "ArkT", "ArbT", "Vbd", "Kbd", "Bbd", "AkkV", "P0", "P1", "Aa", "Ab", "Ba", "Bb2", "KKbd",
              "WT", "nU0"]
        UB = [{nm: al(f"rw_u{q}{nm}", [128, 4, 128], F32 if nm in ("WT", "nU0") else BF16) for nm in un} for q in range(2)]
        for q in range(2):
            for nm in un:
                if nm != "nU0":
                    kb.op(pool, lambda e: e.tensor_copy(out=(R if nm == "WT" else Rb)(UB[q][nm][:]), in_=zt[:, 0:512].rearrange("p (a b) -> p a b", b=128)),
                          reads=[zt], writes=[UB[q][nm]])
        X = [al(f"rw_X{i}", [128, 128], BF16) for i in range(2)]
        nU = [al(f"rw_nU{i}", [128, 128], BF16) for i in range(2)]
        Osb = al("rw_O", [128, 4, 128])
        yn = al("rw_yn", [128, 4, 128])
        ynr = al("rw_ynr", [128, 4, 128], BF16)
        junk = al("rw_junk", [128, 4, 128])
        Hs = [al(f"rw_H{i}", [128, 128]) for i in range(3)]
        Hpad = al("rw_Hpad", [128, 128])
        Ht = al("rw_Ht", [128, 128])
        sin1 = al("rw_sin", [128, NS, 64])
        sout1 = al("rw_sout", [128, NS, 64])
        sin = [sin1, sin1]
        sout = [sout1, sout1]
        st = [al(f"rw_st{k}", [128, 4]) for k in range(5)]
        kb.op(pool, lambda e: e.memset(Hpad[:], 0.0), writes=[Hpad])
        hst = {"H": None, "hi": 0, "sc": 0}

        def newH():
            h = Hs[hst["hi"] % 3]
            hst["hi"] += 1
            return h

        def genA(item):
            pr, gs, (t0, ntok, C, nch, is_s), bi, first, last, k = item[:7]
            n = 2 * C
            SL, SU, IU = (C_SL64, C_SU64, C_IU64) if C == 64 else (C_SL8, C_SU8, C_IU8)
            levels = 5 if C == 64 else 2
            Bq = Bsets[gs % 2]
            pad = (padS if is_s else padP)[gs % 2]
            LC, GC = LCs[gs % 2], GCs[gs % 2]
            U = UB[k % 2]
            rs_ = slice(pr * 128, (pr + 1) * 128)
            W = slice(0, ntok)
            if first:
                if is_s:
                    kb.dma(sp, sin[pr % 2][:], I.state_rwkv[:, 2 * pr:2 * pr + 2].rearrange("s h v k -> (h v) s k"),
                           writes=[sin[pr % 2]])
                r, k_, v, kk, b, a, ld, L, Lm, LCmL, bon, tmp, tmpr = (Bq[x] for x in ("r", "k", "v", "kk", "b", "a", "ld", "L",
                                                                                     "Lm", "LCmL", "bon", "tmp", "tmpr"))
                kp, ta = Bq["kp"], Bq["ta"]
                dd = [Bq["d"], Bq["d2"], Bq["d3"]]
                for xi, (nm, row0) in enumerate((("r", 512), ("k", 2048), ("v", 3584))):
                    rw = raw[xi]
                    rows = slice(row0 + pr * 128, row0 + (pr + 1) * 128)
                    if not is_s:
                        if t0 == 0:
                            kb.op(pool, lambda e: e.memset(rw[:, 0:1], 0.0), writes=[rw])
                            kb.dma(sp, rw[:, 1:ntok + 1], S.pT[rows, 0:ntok], writes=[rw])
                        else:
                            kb.dma(sp, rw[:, 0:ntok + 1], S.pT[rows, t0 - 1:t0 + ntok], writes=[rw])
                    else:
                        kb.dma(sp, rw[:, 0:TSM + NS], S.pT[rows, TP:TTX], writes=[rw])
                for ai, nm in ((1, "a"), (0, "ld"), (2, "gt")):
                    kb.dma(sp, Bq[nm][:, W], S.auxT[ai, rs_, t0:t0 + ntok], writes=[Bq[nm]])
                yield
                for xi in range(3):
                    rw, d = raw[xi], dd[xi]
                    if not is_s:
                        kb.op(dve, lambda e: e.tensor_tensor(out=d[:, W], in0=rw[:, 0:ntok], in1=rw[:, 1:ntok + 1],
                                                             op=ALU.subtract), reads=[rw], writes=[d])
                    else:
                        p3 = rw[:, 0:TSM].rearrange("p (s t) -> p s t", t=TS)
                        d3 = d[:, 0:TSM].rearrange("p (s t) -> p s t", t=TS)
                        kb.op(dve, lambda e: e.tensor_tensor(out=d3[:, :, 1:TS], in0=p3[:, :, 0:TS - 1],
                                                             in1=p3[:, :, 1:TS], op=ALU.subtract), reads=[rw], writes=[d])
                        kb.op(pool, lambda e: e.tensor_tensor(out=d3[:, :, 0:1],
                                                              in0=rw[:, TSM:TSM + NS].rearrange("p (s o) -> p s o", o=1),
                                                              in1=p3[:, :, 0:1], op=ALU.subtract), reads=[rw], writes=[d])
                yield
                for c in range(nch):
                    cs = slice(c * C, (c + 1) * C)
                    kb.op(dve, lambda e: e.tensor_tensor_scan(out=L[:, cs], data0=cst[:, C_ONES, 0:C], data1=ld[:, cs],
                                                              initial=0.0, op0=ALU.mult, op1=ALU.add),
                          reads=[cst, ld], writes=[L])
                    if c % 4 == 3:
                        yield
                kb.op(pool, lambda e: e.tensor_scalar(out=ta[:, W], in0=a[:, W], scalar1=vec[:, pr, 4:5],
                                                      scalar2=omk[:, pr:pr + 1], op0=ALU.mult, op1=ALU.add),
                      reads=[a, vec, omk], writes=[ta])
                for xi, nm in enumerate(("r", "k", "v")):
                    rw, d, dst = raw[xi], dd[xi], Bq[nm]
                    src = rw[:, 1:ntok + 1] if not is_s else rw[:, 0:TSM]
                    kb.op(dve, lambda e: e.scalar_tensor_tensor(out=dst[:, W], in0=d[:, W], scalar=vec[:, pr, xi:xi + 1],
                                                                in1=src, op0=ALU.mult, op1=ALU.add),
                          reads=[d, vec, rw], writes=[dst])
                yield
                L3 = L[:, W].rearrange("p (c t) -> p c t", t=C)
                kb.op(dve, lambda e: e.tensor_copy(out=LC[:, 0:nch], in_=L3[:, :, C - 1]), reads=[L], writes=[LC])
                kb.op(pool, lambda e: e.tensor_tensor(out=Lm[:, W], in0=L[:, W], in1=ld[:, W], op=ALU.subtract),
                      reads=[L, ld], writes=[Lm])
                kb.op(act, lambda e: e.activation(out=Bq["eL"][:, W], in_=L[:, W], func=AF.Exp), reads=[L], writes=[Bq["eL"]])
                kb.op(act, lambda e: e.activation(out=Bq["enL"][:, W], in_=L[:, W], func=AF.Exp, scale=-1.0),
                      reads=[L], writes=[Bq["enL"]])
                yield
                kb.op(dve, lambda e: e.tensor_scalar(out=kk[:, W], in0=k_[:, W], scalar1=vec[:, pr, 3:4], scalar2=None,
                                                     op0=ALU.mult), reads=[k_, vec], writes=[kk])
                kb.op(dve, lambda e: e.tensor_tensor(out=LCmL[:, W].rearrange("p (c t) -> p c t", t=C),
                                                     in0=LC[:, 0:nch].unsqueeze(2).to_broadcast([128, nch, C]),
                                                     in1=L3, op=ALU.subtract), reads=[L, LC], writes=[LCmL])
                kb.op(act, lambda e: e.activation(out=GC[:, 0:nch], in_=LC[:, 0:nch], func=AF.Exp), reads=[LC], writes=[GC])
                kb.op(act, lambda e: e.activation(out=Bq["eLm"][:, W], in_=Lm[:, W], func=AF.Exp), reads=[Lm], writes=[Bq["eLm"]])
                yield
                kb.op(dve, lambda e: e.tensor_tensor(out=tmpr[:, W], in0=kk[:, W], in1=kk[:, W], op=ALU.mult),
                      reads=[kk], writes=[tmpr])
                ps_ss = next_ps(g)
                kb.op(pe, lambda e: e.matmul(ps_ss[:, W], lhsT=cr[:, 0, :], rhs=tmpr[:, W], start=True, stop=True),
                      reads=[cr, tmpr], writes=[ps_ss])
                kb.op(dve, lambda e: e.tensor_tensor(out=kp[:, W], in0=k_[:, W], in1=ta[:, W], op=ALU.mult),
                      reads=[k_, ta], writes=[kp])
                kb.op(act, lambda e: e.activation(out=Bq["eLC"][:, W], in_=LCmL[:, W], func=AF.Exp),
                      reads=[LCmL], writes=[Bq["eLC"]])
                yield

                def padmul(dn, an, bn, eng_pair):
                    for h in range(2):
                        hs = slice(h * 64, (h + 1) * 64)
                        o_ap = (R if dn in ("KKr", "Rtr") else Rb)(pad[dn][hs, 0:nch, h * C:(h + 1) * C])
                        i0 = Bq[an][hs, W].rearrange("p (c t) -> p c t", t=C)
                        eng = eng_pair[h]
                        if bn is None:
                            kb.op(eng, lambda e: e.tensor_copy(out=o_ap, in_=i0), reads=[Bq[an]], writes=[pad[dn]])
                        else:
                            i1 = Bq[bn][hs, W].rearrange("p (c t) -> p c t", t=C)
                            kb.op(eng, lambda e: e.tensor_tensor(out=o_ap, in0=i0, in1=i1, op=ALU.mult),
                                  reads=[Bq[an], Bq[bn]], writes=[pad[dn]])
                padmul("Rt", "r", "eL", (dve, pool))
                padmul("Rtr", "r", "eL", (pool, dve))
                yield
                padmul("Vp", "v", None, (pool, pool))
                padmul("Kh", "kp", "enL", (dve, pool))
                yield
                padmul("Kb", "kp", "eLC", (pool, dve))
                tmpb = Bq["tmpb"]
                kb.op(dve, lambda e: e.scalar_tensor_tensor(out=tmpb[:, W], in0=r[:, W], scalar=vec[:, pr, 5:6],
                                                            in1=kp[:, W], op0=ALU.mult, op1=ALU.mult),
                      reads=[r, vec, kp], writes=[tmpb])
                ps_b = next_ps(g)
                kb.op(pe, lambda e: e.matmul(ps_b[:, W], lhsT=cr[:, 0, :], rhs=tmpb[:, W], start=True, stop=True),
                      reads=[cr, tmpb], writes=[ps_b])
                yield
                kb.op(act, lambda e: e.sqrt(out=tmp[:, W], in_=ps_ss[:, W]), reads=[ps_ss], writes=[tmp])
                kb.op(dve, lambda e: e.tensor_tensor(out=bon[:, W], in0=ps_b[:, W], in1=v[:, W], op=ALU.mult),
                      reads=[ps_b, v], writes=[bon])
                kb.op(dve, lambda e: e.tensor_scalar_max(out=tmp[:, W], in0=tmp[:, W], scalar1=1e-12), reads=[tmp], writes=[tmp])
                yield
                kb.op(dve, lambda e: e.reciprocal(out=tmp[:, W], in_=tmp[:, W]), reads=[tmp], writes=[tmp])
                yield
                kb.op(dve, lambda e: e.tensor_tensor(out=kk[:, W], in0=kk[:, W], in1=tmp[:, W], op=ALU.mult),
                      reads=[kk, tmp], writes=[kk])
                yield
                kb.op(dve, lambda e: e.tensor_tensor(out=b[:, W], in0=kk[:, W], in1=a[:, W], op=ALU.mult),
                      reads=[kk, a], writes=[b])
                padmul("KK", "kk", "eLm", (pool, dve))
                yield
                padmul("Bh", "b", "enL", (dve, pool))
                padmul("Bb", "b", "eLC", (pool, dve))
                yield
            units = list(range(bi * 4, bi * 4 + 4))

            def padl(nm, c):
                return pad[nm][:, :, :].rearrange("p c j -> p (c j)")[:, c * n:c * n + 128]

            def mm4(lhs_fn, rhs_fn, wcols, reads):
                ps = next_ps(g)
                for u, c in enumerate(units):
                    kb.op(pe, lambda e: e.matmul(ps[0:128, u * 128:u * 128 + wcols], lhsT=lhs_fn(u, c), rhs=rhs_fn(u, c),
                                                 start=True, stop=True), reads=reads, writes=[ps])
                return ps

            def ps3d(ps, wcols):
                return ps[0:n, 0:512].rearrange("p (u j) -> p u j", j=128)[:, :, 0:wcols]

            for (nm, ln, rn, mk) in (("A", "KK", "Bh", SL), ("Bm", "Bh", "KK", SU), ("AkkT", "Kh", "KK", SU),
                                     ("ArkT", "Kh", "Rt", IU), ("ArbT", "Bh", "Rt", IU)):
                ps = mm4(lambda u, c: Rb(padl(ln, c)), lambda u, c: Rb(pad[rn][:, c, 0:n]), n, [pad[ln], pad[rn]])
                kb.op(dve, lambda e: e.tensor_tensor(out=Rb(U[nm][0:n, :, 0:n]), in0=ps3d(ps, n),
                                                     in1=cst[0:n, mk, 0:n].unsqueeze(1).to_broadcast([n, 4, n]),
                                                     op=ALU.mult), reads=[ps, cst], writes=[U[nm]])
                yield
            for (nm, pn_) in (("Vbd", "Vp"), ("Kbd", "Kb"), ("Bbd", "Bb"), ("KKbd", "KK")):
                ps = next_ps(g)
                psb = ps[:, :].bitcast(BF16)
                for u, c in enumerate(units):
                    kb.op(pe, lambda e: e.transpose(out=psb[0:n, u * 128:(u + 1) * 128], in_=pad[pn_][:, c, 0:n],
                                                    identity=idb[:, :]), reads=[pad[pn_], idb], writes=[ps])
                kb.op(act, lambda e: e.copy(out=U[nm][0:n, :, :],
                                            in_=psb[0:n, 0:512].rearrange("p (u j) -> p u j", j=128)),
                      reads=[ps], writes=[U[nm]])
                yield
            ps = mm4(lambda u, c: Rb(U["AkkT"][0:n, u, 0:128]), lambda u, c: Rb(U["Vbd"][0:n, u, :]), 128, [U["AkkT"], U["Vbd"]])
            kb.op(act, lambda e: e.copy(out=U["AkkV"][0:n, :, :], in_=ps3d(ps, 128)), reads=[ps], writes=[U["AkkV"]])
            kb.op(dve, lambda e: e.tensor_tensor(out=Rb(U["P0"][0:n, :, 0:n]),
                                                 in0=cst[0:n, C_ID, 0:n].unsqueeze(1).to_broadcast([n, 4, n]),
                                                 in1=U["Bm"][0:n, :, 0:n], op=ALU.subtract),
                  reads=[cst, U["Bm"]], writes=[U["P0"]])
            yield
            Aj, Bj, Pj = "A", "Bm", "P0"
            for lev in range(levels):
                An = "Aa" if lev % 2 == 0 else "Ab"
                Bn = "Ba" if lev % 2 == 0 else "Bb2"
                Pn = "P1" if lev % 2 == 0 else "P0"
                ps_a = mm4(lambda u, c: Rb(U[Bj][0:n, u, 0:128]), lambda u, c: Rb(U[Aj][0:n, u, 0:n]), n, [U[Bj], U[Aj]])
                if lev < levels - 1:
                    ps_b = mm4(lambda u, c: Rb(U[Aj][0:n, u, 0:128]), lambda u, c: Rb(U[Bj][0:n, u, 0:n]), n, [U[Bj], U[Aj]])
                kb.op(act, lambda e: e.copy(out=Rb(U[An][0:n, :, 0:n]), in_=ps3d(ps_a, n)), reads=[ps_a], writes=[U[An]])
                if lev < levels - 1:
                    kb.op(dve, lambda e: e.tensor_copy(out=Rb(U[Bn][0:n, :, 0:n]), in_=ps3d(ps_b, n)), reads=[ps_b], writes=[U[Bn]])
                yield
                ps = mm4(lambda u, c: Rb(U[An][0:n, u, 0:128]), lambda u, c: Rb(U[Pj][0:n, u, 0:n]), n, [U[An], U[Pj]])
                kb.op(dve, lambda e: e.tensor_tensor(out=Rb(U[Pn][0:n, :, 0:n]), in0=ps3d(ps, n), in1=U[Pj][0:n, :, 0:n],
                                                     op=ALU.add), reads=[ps, U[Pj]], writes=[U[Pn]])
                yield
                Aj, Bj, Pj = An, Bn, Pn
            item[-1]["P"] = Pj
            ps = mm4(lambda u, c: Rb(U["KKbd"][0:n, u, 0:128]), lambda u, c: Rb(U[Pj][0:n, u, 0:n]), n, [U["KKbd"], U[Pj]])
            kb.op(act, lambda e: e.copy(out=R(U["WT"][:, :, 0:n]),
                                        in_=ps[:, 0:512].rearrange("p (u j) -> p u j", j=128)[:, :, 0:n]),
                  reads=[ps], writes=[U["WT"]])
            yield
            ps = mm4(lambda u, c: Rb(U[Pj][0:n, u, 0:128]), lambda u, c: Rb(U["AkkV"][0:n, u, :]), 128, [U[Pj], U["AkkV"]])
            kb.op(act, lambda e: e.mul(out=U["nU0"][0:n, :, :], in_=ps3d(ps, 128), mul=-1.0), reads=[ps], writes=[U["nU0"]])
            yield

        def genB(item):
            pr, gs, (t0, ntok, C, nch, is_s), bi, first, last, k = item[:7]
            Pj = item[-1]["P"]
            n = 2 * C
            BLKm = C_BLK if C == 64 else C_BLK8
            SELi = 1 if C == 64 else 2
            Bq = Bsets[gs % 2]
            pad = (padS if is_s else padP)[gs % 2]
            GC = GCs[gs % 2]
            U = UB[k % 2]
            rs_ = slice(pr * 128, (pr + 1) * 128)
            W = slice(0, ntok)
            units = list(range(bi * 4, bi * 4 + 4))

            def padl(nm, c):
                return pad[nm][:, :, :].rearrange("p c j -> p (c j)")[:, c * n:c * n + 128]
            ynT = Bq["ynT"]
            if first and (not is_s) and t0 == 0:
                h0 = newH()
                kb.op(dve, lambda e: e.tensor_copy(out=R(h0[:]), in_=zt[:, 0:128]), reads=[zt], writes=[h0])
                hst["H"] = h0
            for u, c in enumerate(units):
                if is_s:
                    for h in range(2):
                        hs = slice(h * 64, (h + 1) * 64)
                        kb.op(dve, lambda e: e.tensor_copy(out=Hpad[hs, hs], in_=sin[pr % 2][hs, c, :]),
                              reads=[sin[pr % 2]], writes=[Hpad])
                    ps = next_ps(g)
                    kb.op(pe, lambda e: e.transpose(out=ps[:, 0:128], in_=Hpad[:, :], identity=cst[:, C_ID, :]),
                          reads=[Hpad, cst], writes=[ps])
                    hn = newH()
                    kb.op(act, lambda e: e.copy(out=R(hn[:, :]), in_=ps[:, 0:128]), reads=[ps], writes=[hn])
                    hst["H"] = hn
                    yield
                Hcur = hst["H"]
                q = hst["sc"] % 2
                hst["sc"] += 1
                Xq, nUq = X[q], nU[q]
                ps4 = next_ps(g)
                kb.op(pe, lambda e: e.matmul(ps4[:, 0:128], lhsT=Rb(U["Kbd"][0:n, u, :]), rhs=Rb(U["Vbd"][0:n, u, :]),
                                             start=True, stop=False), reads=[U["Kbd"], U["Vbd"]], writes=[ps4])
                ps = next_ps(g)
                kb.op(pe, lambda e: e.matmul(ps[0:128, 0:128], lhsT=R(U["WT"][:, u, 0:128]), rhs=R(Hcur[:, :]),
                                             start=True, stop=True), reads=[U["WT"], Hcur], writes=[ps])
                kb.op(dve, lambda e: e.scalar_tensor_tensor(out=Rb(nUq[0:n, :]), in0=ps[0:n, 0:128], scalar=-1.0,
                                                            in1=U["nU0"][0:n, u, :], op0=ALU.mult, op1=ALU.add),
                      reads=[ps, U["nU0"]], writes=[nUq])
                yield
                kb.op(pe, lambda e: e.matmul(ps4[:, 0:128], lhsT=Rb(U["Bbd"][0:n, u, :]), rhs=Rb(nUq[0:n, :]),
                                             start=False, stop=True), reads=[U["Bbd"], nUq], writes=[ps4])
                Hn = newH()
                kb.op(dve, lambda e: e.scalar_tensor_tensor(out=R(Hn[:, :]), in0=Hcur[:, :], scalar=GC[:, c:c + 1],
                                                            in1=ps4[:, 0:128], op0=ALU.mult, op1=ALU.add),
                      reads=[Hcur, GC, ps4], writes=[Hn])
                hst["H"] = Hn
                yield
                oc = slice(u * 128, (u + 1) * 128)
                kb.op(pe, lambda e: e.matmul(psO[0:128, oc], lhsT=R(padl("Rtr", c)), rhs=R(Hcur[:, :]),
                                             start=True, stop=False), reads=[pad["Rtr"], Hcur], writes=[psO])
                kb.op(pe, lambda e: e.matmul(psO[0:128, oc], lhsT=Rb(U["ArkT"][0:n, u, 0:128]), rhs=Rb(U["Vbd"][0:n, u, :]),
                                             start=False, stop=False), reads=[U["ArkT"], U["Vbd"]], writes=[psO])
                kb.op(pe, lambda e: e.matmul(psO[0:128, oc], lhsT=Rb(U["ArbT"][0:n, u, 0:128]), rhs=Rb(nUq[0:n, :]),
                                             start=False, stop=True), reads=[U["ArbT"], nUq], writes=[psO])

                if is_s:
                    ps = next_ps(g)
                    kb.op(pe, lambda e: e.transpose(out=ps[:, 0:128], in_=Hn[:, :], identity=cst[:, C_ID, :]),
                          reads=[Hn, cst], writes=[ps])
                    kb.op(act, lambda e: e.copy(out=Ht[:, :], in_=ps[:, 0:128]), reads=[ps], writes=[Ht])
                    for h in range(2):
                        hs = slice(h * 64, (h + 1) * 64)
                        kb.op(dve, lambda e: e.tensor_copy(out=sout[pr % 2][hs, c, :], in_=Ht[hs, hs]),
                              reads=[Ht], writes=[sout[pr % 2]])
                    yield
            s1, s2, mean, var, msq = st
            kb.op(act, lambda e: e.copy(out=Osb[0:n, :, :], in_=psO[0:n, 0:512].rearrange("p (u j) -> p u j", j=128)),
                  reads=[psO], writes=[Osb])
            kb.op(dve, lambda e: e.tensor_reduce(out=s1[0:n, :], in_=Osb[0:n, :, :], axis=AX.X, op=ALU.add),
                  reads=[Osb], writes=[s1])
            kb.op(act, lambda e: e.activation(out=junk[0:n, :, :], in_=Osb[0:n, :, :], func=AF.Square),
                  reads=[Osb], writes=[junk])
            kb.op(dve, lambda e: e.tensor_reduce(out=s2[0:n, :], in_=junk[0:n, :, :], axis=AX.X, op=ALU.add),
                  reads=[junk], writes=[s2])
            yield
            kb.op(dve, lambda e: e.tensor_scalar(out=mean[0:n, :], in0=s1[0:n, :], scalar1=1.0 / 64, scalar2=None,
                                                 op0=ALU.mult), reads=[s1], writes=[mean])
            kb.op(dve, lambda e: e.tensor_tensor(out=msq[0:n, :], in0=mean[0:n, :], in1=mean[0:n, :], op=ALU.mult),
                  reads=[mean], writes=[msq])
            kb.op(dve, lambda e: e.scalar_tensor_tensor(out=var[0:n, :], in0=s2[0:n, :], scalar=1.0 / 64,
                                                        in1=msq[0:n, :], op0=ALU.mult, op1=ALU.subtract),
                  reads=[s2, msq], writes=[var])
            kb.op(dve, lambda e: e.tensor_scalar_add(out=var[0:n, :], in0=var[0:n, :], scalar1=GN_EPS),
                  reads=[var], writes=[var])
            kb.op(act, lambda e: e.sqrt(out=var[0:n, :], in_=var[0:n, :]), reads=[var], writes=[var])
            kb.op(dve, lambda e: e.reciprocal(out=var[0:n, :], in_=var[0:n, :]), reads=[var], writes=[var])
            yield
            kb.op(dve, lambda e: e.tensor_tensor(out=yn[0:n, :, :], in0=Osb[0:n, :, :],
                                                 in1=mean[0:n, :].unsqueeze(2).to_broadcast([n, 4, 128]), op=ALU.subtract),
                  reads=[Osb, mean], writes=[yn])
            kb.op(dve, lambda e: e.tensor_tensor(out=yn[0:n, :, :], in0=yn[0:n, :, :],
                                                 in1=var[0:n, :].unsqueeze(2).to_broadcast([n, 4, 128]), op=ALU.mult),
                  reads=[yn, var], writes=[yn])
            kb.op(dve, lambda e: e.tensor_tensor(out=Rb(ynr[0:n, :, :]), in0=yn[0:n, :, :],
                                                 in1=cst[0:n, BLKm, :].unsqueeze(1).to_broadcast([n, 4, 128]), op=ALU.mult),
                  reads=[yn, cst], writes=[ynr])
            psF = next_ps(g)
            for u, c in enumerate(units):
                kb.op(pe, lambda e: e.matmul(psF[:, u * C:(u + 1) * C], lhsT=Rb(ynr[0:n, u, :]), rhs=Rb(cr[0:n, SELi, 0:C]),
                                             start=True, stop=True), reads=[ynr, cr], writes=[psF])
            kb.op(act, lambda e: e.copy(out=ynT[:, bi * 4 * C:(bi * 4 + 4) * C], in_=psF[:, 0:4 * C]),
                  reads=[psF], writes=[ynT])
            yield
            if last:
                o1 = Bq["o1"]
                oc_ = ocat[gs % 2]
                kb.op(dve, lambda e: e.tensor_scalar(out=o1[:, W], in0=ynT[:, W], scalar1=vec[:, pr, 6:7],
                                                     scalar2=vec[:, pr, 7:8], op0=ALU.mult, op1=ALU.add),
                      reads=[ynT, vec], writes=[o1])
                kb.op(dve, lambda e: e.tensor_tensor(out=o1[:, W], in0=o1[:, W], in1=Bq["bon"][:, W], op=ALU.add),
                      reads=[o1, Bq["bon"]], writes=[o1])
                kb.op(dve, lambda e: e.tensor_tensor(out=oc_[:, W], in0=o1[:, W], in1=Bq["gt"][:, W], op=ALU.mult),
                      reads=[o1, Bq["gt"]], writes=[oc_])
                kb.dma(pool, S.catT[rs_, t0:t0 + ntok], oc_[:, W], reads=[oc_])
                if (not is_s) and t0 + ntok == TP:
                    Hcur = hst["H"]
                    ps = next_ps(g)
                    kb.op(pe, lambda e: e.transpose(out=ps[:, 0:128], in_=Hcur[:, :], identity=cst[:, C_ID, :]),
                          reads=[Hcur, cst], writes=[ps])
                    kb.op(act, lambda e: e.copy(out=Ht[:, :], in_=ps[:, 0:128]), reads=[ps], writes=[Ht])
                    for h in range(2):
                        hs = slice(h * 64, (h + 1) * 64)
                        kb.dma(pool, O.rwkv_prompt[pr * 128 + h * 64:pr * 128 + (h + 1) * 64, :], Ht[hs, hs], reads=[Ht])
                if is_s:
                    kb.dma(pool, O.rwkv_sample[:, rs_, :].rearrange("s r k -> r s k"), sout[pr % 2][:],
                           reads=[sout[pr % 2]])
                yield

        pending = None
        gs = 0
        k = 0
        for pr in range(12):
            for seg in RSEGS:
                nb = seg[3] // 4
                for bi in range(nb):
                    item = (pr, gs, seg, bi, bi == 0, bi == nb - 1, k, {})
                    run_interleaved(genA(item), pending)
                    pending = genB(item)
                    k += 1
                gs += 1
        run_interleaved(None, pending)
    g.ps_n = 8


HSEGS = [(k * 256, 256, 64, 4, False) for k in range(8)] + [(TP, TSM, 8, NS, True)]


def phase_hgrn(g):
    nc, kb, I, O, S = g.nc, g.kb, g.I, g.O, g.S
    dve, act, pe, pool, sp = kb.dve, kb.act, kb.pe, kb.pool, kb.sp
    cst = g.cst
    g.ps_n = 7
    psO = g.ps[7]
    with ExitStack() as es:
        al = lambda name, shape, dt=F32: T(es.enter_context(nc.sbuf_tensor(uname(name), shape, dt)))
        lbr = al("hg_lbr", [128, 2, 12])
        lb = al("hg_lb", [128, 12])
        oml = al("hg_oml", [128, 12])
        gn = al("hg_gn", [128, 1])
        onesr = al("hg_ones", [128, 128])
        zt = al("hg_zero", [128, 128])
        kb.op(pool, lambda e: e.memset(zt[:], 0.0), writes=[zt])
        kb.op(dve, lambda e: e.tensor_copy(out=R(onesr[:]), in_=cst[:, C_ONES, :]), reads=[cst], writes=[onesr])
        kb.dma(sp, lbr[:], I.b_lb.rearrange("l (c p) -> p l c", p=128), writes=[lbr], allow_slow_non_contiguous=True)
        kb.dma(sp, gn[:], I.b_g_norm[:, :], writes=[gn])
        kb.op(dve, lambda e: e.tensor_tensor(out=lb[:], in0=lbr[:, 1, :], in1=lbr[:, 0, :], op=ALU.subtract),
              reads=[lbr], writes=[lb])
        kb.op(act, lambda e: e.activation(out=lb[:], in_=lb[:], func=AF.Sigmoid), reads=[lb], writes=[lb])
        kb.op(dve, lambda e: e.tensor_scalar(out=oml[:], in0=lb[:], scalar1=-1.0, scalar2=1.0, op0=ALU.mult, op1=ALU.add),
              reads=[lb], writes=[oml])
        names = ["q", "f", "i", "og", "sq", "kk", "lf", "L", "LmM", "LCmL", "e1", "e2", "Qt", "Kt", "Qh", "Kb", "oseg",
                 "t1", "t1r"]
        Bs = [{nm: al(f"hg_{nm}{i}", [128, 384 if nm == "Kt" else 256], BF16 if nm in ("Kt", "Qt") else F32) for nm in names} for i in range(2)]
        for i in range(2):
            for j3 in range(3):
                kb.op(dve, lambda e: e.tensor_copy(out=Bs[i]["Kt"][:, j3 * 128:(j3 + 1) * 128], in_=zt[:]),
                      reads=[zt], writes=[Bs[i]["Kt"]])
        ocat = [al(f"hg_ocat{i}", [128, 256], BF16) for i in range(2)]
        LCs = [al(f"hg_LC{i}", [128, 16]) for i in range(2)]
        MDs = [al(f"hg_MD{i}", [128, 16]) for i in range(2)]
        GCs = [al(f"hg_GC{i}", [128, 16]) for i in range(2)]
        attT = [al(f"hg_att{i}", [64, 4, 64], BF16) for i in range(2)]
        VK = [al(f"hg_vk{i}", [64, 8, 128], BF16) for i in range(2)]
        Ss = [al(f"hg_S{i}", [128, 128]) for i in range(3)]
        sin = [al(f"hg_sin{i}", [128, NS, 128]) for i in range(2)]
        sout = [al(f"hg_sout{i}", [128, NS, 128]) for i in range(2)]
        hst = {"S": None, "si": 0}

        def newS():
            t_ = Ss[hst["si"] % 3]
            hst["si"] += 1
            return t_

        def genA(item):
            hd, gs, (t0, ntok, C, nch, is_s), bi, first, last, k = item[:7]
            W = slice(0, ntok)
            HIU = C_HIU64 if C == 64 else C_HIU8
            mid = (C - 1) // 2
            Bq = Bs[gs % 2]
            LC, MD, GC = LCs[gs % 2], MDs[gs % 2], GCs[gs % 2]
            q, f, iv, og, sq, kk, lf, L, LmM, LCmL, e1, e2, Qt, Kt, Qh, Kb, oseg, t1, t1r = (Bq[x] for x in names)
            if first:
                if is_s:
                    kb.dma(sp, sin[hd % 2][:], I.state_hgrn[:, hd].rearrange("s k v -> k s v"), writes=[sin[hd % 2]])
                for nm, row0 in (("q", 512), ("f", 2048), ("i", 3584), ("og", 5120)):
                    kb.dma(sp, Bq[nm][:, W], S.pT[row0 + hd * 128:row0 + (hd + 1) * 128, t0:t0 + ntok], writes=[Bq[nm]])
                kb.op(act, lambda e: e.activation(out=sq[:, W], in_=q[:, W], func=AF.Silu), reads=[q], writes=[sq])
                kb.op(act, lambda e: e.activation(out=og[:, W], in_=og[:, W], func=AF.Silu), reads=[og], writes=[og])
                kb.op(act, lambda e: e.activation(out=f[:, W], in_=f[:, W], func=AF.Sigmoid), reads=[f], writes=[f])
                yield
                kb.op(dve, lambda e: e.tensor_scalar(out=f[:, W], in0=f[:, W], scalar1=oml[:, hd:hd + 1],
                                                     scalar2=lb[:, hd:hd + 1], op0=ALU.mult, op1=ALU.add),
                      reads=[f, oml, lb], writes=[f])
                kb.op(dve, lambda e: e.tensor_scalar(out=kk[:, W], in0=f[:, W], scalar1=-1.0, scalar2=1.0,
                                                     op0=ALU.mult, op1=ALU.add), reads=[f], writes=[kk])
                kb.op(act, lambda e: e.activation(out=lf[:, W], in_=f[:, W], func=AF.Ln), reads=[f], writes=[lf])
                yield
                for c in range(nch):
                    cs = slice(c * C, (c + 1) * C)
                    kb.op(dve, lambda e: e.tensor_tensor_scan(out=L[:, cs], data0=cst[:, C_ONES, 0:C], data1=lf[:, cs],
                                                              initial=0.0, op0=ALU.mult, op1=ALU.add),
                          reads=[cst, lf], writes=[L])
                    if c % 4 == 3:
                        yield
                L3 = L[:, W].rearrange("p (c t) -> p c t", t=C)
                kb.op(dve, lambda e: e.tensor_copy(out=LC[:, 0:nch], in_=L3[:, :, C - 1]), reads=[L], writes=[LC])
                kb.op(dve, lambda e: e.tensor_copy(out=MD[:, 0:nch], in_=L3[:, :, mid]), reads=[L], writes=[MD])
                kb.op(dve, lambda e: e.tensor_tensor(out=LmM[:, W].rearrange("p (c t) -> p c t", t=C), in0=L3,
                                                     in1=MD[:, 0:nch].unsqueeze(2).to_broadcast([128, nch, C]),
                                                     op=ALU.subtract), reads=[L, MD], writes=[LmM])
                kb.op(dve, lambda e: e.tensor_tensor(out=LCmL[:, W].rearrange("p (c t) -> p c t", t=C),
                                                     in0=LC[:, 0:nch].unsqueeze(2).to_broadcast([128, nch, C]),
                                                     in1=L3, op=ALU.subtract), reads=[L, LC], writes=[LCmL])
                yield
                kb.op(act, lambda e: e.activation(out=GC[:, 0:nch], in_=LC[:, 0:nch], func=AF.Exp), reads=[LC], writes=[GC])
                kb.op(act, lambda e: e.activation(out=e1[:, W], in_=LmM[:, W], func=AF.Exp), reads=[LmM], writes=[e1])
                kb.op(dve, lambda e: e.tensor_tensor(out=Qt[:, W], in0=sq[:, W], in1=e1[:, W], op=ALU.mult),
                      reads=[sq, e1], writes=[Qt])
                kb.op(act, lambda e: e.activation(out=e2[:, W], in_=LmM[:, W], func=AF.Exp, scale=-1.0), reads=[LmM], writes=[e2])
                kb.op(dve, lambda e: e.tensor_tensor(out=Kt[:, W], in0=kk[:, W], in1=e2[:, W], op=ALU.mult),
                      reads=[kk, e2], writes=[Kt])
                yield
                kb.op(act, lambda e: e.activation(out=e1[:, W], in_=L[:, W], func=AF.Exp), reads=[L], writes=[e1])
                kb.op(dve, lambda e: e.tensor_tensor(out=R(Qh[:, W]), in0=sq[:, W], in1=e1[:, W], op=ALU.mult),
                      reads=[sq, e1], writes=[Qh])
                kb.op(act, lambda e: e.activation(out=e2[:, W], in_=LCmL[:, W], func=AF.Exp), reads=[LCmL], writes=[e2])
                kb.op(dve, lambda e: e.tensor_tensor(out=Kb[:, W], in0=kk[:, W], in1=e2[:, W], op=ALU.mult),
                      reads=[kk, e2], writes=[Kb])
                yield
            units = list(range(bi * 4, bi * 4 + 4))
            at, vk = attT[k % 2], VK[k % 2]
            ps = next_ps(g)
            for u, c in enumerate(units):
                cs = slice(c * C, (c + 1) * C)
                kb.op(pe, lambda e: e.matmul(ps[0:128, u * 64:u * 64 + C], lhsT=Kt[:, c * C:c * C + 128], rhs=Qt[:, cs],
                                             start=True, stop=True), reads=[Kt, Qt], writes=[ps])
            kb.op(dve, lambda e: e.tensor_tensor(out=at[0:C, :, 0:C],
                                                 in0=ps[0:C, 0:256].rearrange("p (u j) -> p u j", j=64)[:, :, 0:C],
                                                 in1=cst[0:C, HIU, 0:C].unsqueeze(1).to_broadcast([C, 4, C]), op=ALU.mult),
                  reads=[ps, cst], writes=[at])
            yield
            for half in range(2):
                ps = next_ps(g)
                for uu in range(2):
                    u = half * 2 + uu
                    cs = slice(units[u] * C, (units[u] + 1) * C)
                    kb.op(pe, lambda e: e.transpose(out=ps[0:C, (uu * 2) * 128:(uu * 2 + 1) * 128], in_=iv[:, cs],
                                                    identity=cst[:, C_ID, :]), reads=[iv, cst], writes=[ps])
                    kb.op(pe, lambda e: e.transpose(out=ps[0:C, (uu * 2 + 1) * 128:(uu * 2 + 2) * 128], in_=Kb[:, cs],
                                                    identity=cst[:, C_ID, :]), reads=[Kb, cst], writes=[ps])
                kb.op(act, lambda e: e.copy(out=vk[0:C, half * 4:half * 4 + 4, :],
                                            in_=ps[0:C, 0:512].rearrange("p (u j) -> p u j", j=128)),
                      reads=[ps], writes=[vk])
                yield

        def genB(item):
            hd, gs, (t0, ntok, C, nch, is_s), bi, first, last, k = item[:7]
            W = slice(0, ntok)
            rs_ = slice(hd * 128, (hd + 1) * 128)
            Bq = Bs[gs % 2]
            GC = GCs[gs % 2]
            Qh, oseg, t1, t1r, og = Bq["Qh"], Bq["oseg"], Bq["t1"], Bq["t1r"], Bq["og"]
            units = list(range(bi * 4, bi * 4 + 4))
            at, vk = attT[k % 2], VK[k % 2]
            if first and (not is_s) and t0 == 0:
                s0 = newS()
                kb.op(dve, lambda e: e.tensor_copy(out=R(s0[:]), in_=zt[:]), reads=[zt], writes=[s0])
                hst["S"] = s0
            for u, c in enumerate(units):
                cs = slice(c * C, (c + 1) * C)
                if is_s:
                    sn = newS()
                    kb.op(dve, lambda e: e.tensor_copy(out=R(sn[:, :]), in_=sin[hd % 2][:, c, :]),
                          reads=[sin[hd % 2]], writes=[sn])
                    hst["S"] = sn
                Scur = hst["S"]
                kb.op(pe, lambda e: e.matmul(psO[:, u * C:(u + 1) * C], lhsT=vk[0:C, u * 2, :], rhs=at[0:C, u, 0:C],
                                             start=True, stop=False), reads=[vk, at], writes=[psO])
                kb.op(pe, lambda e: e.matmul(psO[:, u * C:(u + 1) * C], lhsT=R(Scur[:, :]), rhs=R(Qh[:, cs]),
                                             start=False, stop=True), reads=[Scur, Qh], writes=[psO])
                ps = next_ps(g)
                kb.op(pe, lambda e: e.matmul(ps[:, 0:128], lhsT=vk[0:C, u * 2 + 1, :], rhs=vk[0:C, u * 2, :],
                                             start=True, stop=True), reads=[vk], writes=[ps])
                if is_s:
                    kb.op(dve, lambda e: e.scalar_tensor_tensor(out=sout[hd % 2][:, c, :], in0=Scur[:, :],
                                                                scalar=GC[:, c:c + 1], in1=ps[:, 0:128],
                                                                op0=ALU.mult, op1=ALU.add),
                          reads=[Scur, GC, ps], writes=[sout[hd % 2]])
                else:
                    Sn = newS()
                    kb.op(dve, lambda e: e.scalar_tensor_tensor(out=R(Sn[:, :]), in0=Scur[:, :], scalar=GC[:, c:c + 1],
                                                                in1=ps[:, 0:128], op0=ALU.mult, op1=ALU.add),
                          reads=[Scur, GC, ps], writes=[Sn])
                    hst["S"] = Sn
                yield
            kb.op(act, lambda e: e.copy(out=oseg[:, bi * 4 * C:(bi * 4 + 4) * C], in_=psO[:, 0:4 * C]),
                  reads=[psO], writes=[oseg])
            yield
            if last:
                kb.op(dve, lambda e: e.tensor_tensor(out=R(t1r[:, W]), in0=oseg[:, W], in1=oseg[:, W], op=ALU.mult),
                      reads=[oseg], writes=[t1r])
                ps = next_ps(g)
                kb.op(pe, lambda e: e.matmul(ps[:, W], lhsT=R(onesr[:, :]), rhs=R(t1r[:, W]), start=True, stop=True),
                      reads=[onesr, t1r], writes=[ps])
                kb.op(dve, lambda e: e.tensor_scalar(out=t1[:, W], in0=ps[:, W], scalar1=1.0 / 128, scalar2=EPS,
                                                     op0=ALU.mult, op1=ALU.add), reads=[ps], writes=[t1])
                yield
                kb.op(act, lambda e: e.sqrt(out=t1[:, W], in_=t1[:, W]), reads=[t1], writes=[t1])
                kb.op(dve, lambda e: e.reciprocal(out=t1[:, W], in_=t1[:, W]), reads=[t1], writes=[t1])
                kb.op(dve, lambda e: e.scalar_tensor_tensor(out=t1[:, W], in0=oseg[:, W], scalar=gn[:, 0:1], in1=t1[:, W],
                                                            op0=ALU.mult, op1=ALU.mult), reads=[oseg, gn, t1], writes=[t1])
                oc_ = ocat[gs % 2]
                kb.op(dve, lambda e: e.tensor_tensor(out=oc_[:, W], in0=t1[:, W], in1=og[:, W], op=ALU.mult),
                      reads=[t1, og], writes=[oc_])
                kb.dma(pool, S.catT[rs_, t0:t0 + ntok], oc_[:, W], reads=[oc_])
                if (not is_s) and t0 + ntok == TP:
                    kb.dma(pool, O.hgrn_prompt[hd], hst["S"][:, :], reads=[hst["S"]])
                if is_s:
                    kb.dma(pool, O.hgrn_sample[:, hd].rearrange("s k v -> k s v"), sout[hd % 2][:], reads=[sout[hd % 2]])
                yield

        pending = None
        gs = 0
        k = 0
        for hd in range(12):
            for seg in HSEGS:
                nb = seg[3] // 4
                for bi in range(nb):
                    item = (hd, gs, seg, bi, bi == 0, bi == nb - 1, k, {})
                    run_interleaved(genA(item), pending)
                    pending = genB(item)
                    k += 1
                gs += 1
        run_interleaved(None, pending)
    g.ps_n = 8


_CACHE = {}


def _prep_inputs(inp, c):
    b = c % 4
    s0, s1 = NS * c, NS * (c + 1)
    f = lambda a: np.ascontiguousarray(a, dtype=np.float32)
    m = {
        "x_prompt": f(inp["x_prompt"][b]),
        "x_sample": f(inp["x_sample"][s0:s1].reshape(TSM, D)),
        "mem_prompt": f(inp["mem_prompt"][b]),
        "cache_mem_k": f(inp["cache_mem_k"][:, s0:s1].reshape(2, NS, NMEM, 512)),
        "cache_mem_v": f(inp["cache_mem_v"][:, s0:s1].reshape(2, NS, NMEM, 512)),
        "state_rwkv": f(inp["state_rwkv"][0, s0:s1]),
        "state_shift": f(inp["state_shift"][0, s0:s1]),
        "state_hgrn": f(inp["state_hgrn"][0, s0:s1]),
        "state_conv": f(inp["state_conv"][:, s0:s1].reshape(2, NS * 2, DFF)),
        "norm_mix": f(inp["norm_mix"]), "norm_ffn": f(inp["norm_ffn"]),
        "norm_final": f(inp["norm_final"].reshape(1, D)), "mem_norm": f(inp["mem_norm"]),
        "w_mem_kv": f(inp["w_mem_kv"]), "a_w_in": f(inp["a_w_in"][0]),
        "a_mu": f(inp["a_mu"][0].reshape(-1, 1)), "a_w0": f(inp["a_w0"][0].reshape(-1, 1)),
        "a_w2": f(inp["a_w2"][0]), "a_a0": f(inp["a_a0"][0].reshape(-1, 1)), "a_a2": f(inp["a_a2"][0]),
        "a_g2": f(inp["a_g2"][0]), "a_k_k": f(inp["a_k_k"][0].reshape(-1, 1)),
        "a_k_a": f(inp["a_k_a"][0].reshape(-1, 1)), "a_r_k": f(inp["a_r_k"][0].reshape(-1, 1)),
        "a_ln_w": f(inp["a_ln_w"][0].reshape(-1, 1)), "a_ln_b": f(inp["a_ln_b"][0].reshape(-1, 1)),
        "a_w_out": f(inp["a_w_out"][0]), "b_w_in": f(inp["b_w_in"][0]),
        "b_lower_bounds": f(inp["b_lower_bounds"]), "b_g_norm": f(inp["b_g_norm"][0].reshape(-1, 1)),
        "b_w_out": f(inp["b_w_out"][0]), "ffn_w_up": f(inp["ffn_w_up"]),
        "ffn_conv_w": f(inp["ffn_conv_w"]), "ffn_conv_b": f(inp["ffn_conv_b"]),
        "ffn_w_down": f(inp["ffn_w_down"]), "consts": make_consts(),
    }
    return m


def run(inputs, stop_after=None, dbg=False, ncores=8):
    key = (stop_after, dbg)
    if key not in _CACHE:
        _CACHE[key] = build_program(stop_after, dbg)
    nc, kb = _CACHE[key]
    in_maps = [_prep_inputs(inputs, c) for c in range(ncores)]
    res = run_bass_kernel_spmd(nc, in_maps, core_ids=list(range(ncores)))
    return res.results


def kernel(**inputs):
    r = run(inputs)
    f = np.float32
    B, DEC = 4, 128
    y_prompt = np.stack([r[b]["y_prompt"] for b in range(B)]).astype(f)
    y_sample = np.concatenate([r[c]["y_sample"].reshape(NS, TS, D) for c in range(8)], 0).astype(f)
    mem_k = np.stack([r[b]["mem_k_prompt"] for b in range(B)], 1).reshape(2, B, NMEM, 4, 128).astype(f)
    mem_v = np.stack([r[b]["mem_v_prompt"] for b in range(B)], 1).reshape(2, B, NMEM, 4, 128).astype(f)
    rwkv_p = np.stack([r[b]["rwkv_prompt"].reshape(24, 64, 64) for b in range(B)])[None].astype(f)
    rwkv_s = np.concatenate([r[c]["rwkv_sample"].reshape(NS, 24, 64, 64) for c in range(8)], 0)[None].astype(f)
    shift_p = np.stack([r[b]["shift_prompt"].reshape(D) for b in range(B)])[None].astype(f)
    shift_s = np.concatenate([r[c]["shift_sample"] for c in range(8)], 0)[None].astype(f)
    hgrn_p = np.stack([r[b]["hgrn_prompt"] for b in range(B)])[None].astype(f)
    hgrn_s = np.concatenate([r[c]["hgrn_sample"] for c in range(8)], 0)[None].astype(f)
    conv_p = np.stack([r[b]["conv_prompt"] for b in range(B)], 1).astype(f)
    conv_s = np.concatenate([r[c]["conv_sample"].reshape(2, NS, 2, DFF) for c in range(8)], 1).astype(f)
    return (y_prompt, y_sample, mem_k, mem_v, rwkv_p, rwkv_s, shift_p, shift_s, hgrn_p, hgrn_s, conv_p, conv_s)
```

```python
import numpy as np
from contextlib import ExitStack
import concourse.bass as bass
import concourse.mybir as mybir
from concourse.bass_utils import run_bass_kernel_spmd

F32 = mybir.dt.float32
BF16 = mybir.dt.bfloat16
AF = mybir.ActivationFunctionType
ALU = mybir.AluOpType
AX = mybir.AxisListType
F32R = mybir.dt.float32r


def R(ap):
    return ap.bitcast(F32R)

D = 2048
TP = 2048
NS = 16
TS = 8
TSM = NS * TS
TT = TP + TSM
TTX = TT + NS
NMEM = 256
DFF = 5632
NCA = 5568
NCB = 6656
EPS = 1e-6
GN_EPS = 64e-5

C_ID, C_ONES, C_BLK, C_SL64, C_SU64, C_IU64, C_SL8, C_SU8, C_IU8, C_HIU64, C_HIU8, C_SEL64, C_SEL8, C_BLK8 = range(14)
NCONST = 14


def make_consts():
    c = np.zeros((128, NCONST, 128), np.float32)
    c[:, C_ID, :] = np.eye(128)
    c[:, C_ONES, :] = 1.0
    for h in range(2):
        c[64 * h:64 * h + 64, C_BLK, 64 * h:64 * h + 64] = 1.0
    for (C, sl, su, iu) in ((64, C_SL64, C_SU64, C_IU64), (8, C_SL8, C_SU8, C_IU8)):
        n = 2 * C
        for h in range(2):
            for t in range(C):
                for s in range(C):
                    if s < t:
                        c[h * C + t, sl, h * C + s] = 1.0
                        c[h * C + s, su, h * C + t] = 1.0
                    if s <= t:
                        c[h * C + s, iu, h * C + t] = 1.0
    for (C, hiu) in ((64, C_HIU64), (8, C_HIU8)):
        for t in range(C):
            for s in range(t + 1):
                c[s, hiu, t] = 1.0
    for (C, sel) in ((64, C_SEL64), (8, C_SEL8)):
        for h in range(2):
            for t in range(C):
                c[h * C + t, sel, t] = 1.0
    c[0:8, C_BLK8, 0:64] = 1.0
    c[8:16, C_BLK8, 64:128] = 1.0
    return c


class Res:
    __slots__ = ("w", "r")

    def __init__(self):
        self.w = None
        self.r = {}


class T:
    def __init__(self, t):
        self.t = t
        self.res = Res()

    def __getitem__(self, k):
        return self.t[k]


class Eng:
    def __init__(self, kb, name, h, is_pe=False):
        self.kb = kb
        self.name = name
        self.h = h
        self.is_pe = is_pe
        self.seen = {}
        self.own = set()
        self.sem = None
        self.cnt = 0
        self.ring = []
        self.ri = 0
        self.idx = 0

    def tick(self):
        if self.sem is None or self.idx >= 30000:
            self.sem = self.kb.newsem(self.name)
            self.own.add(self.sem)
            self.cnt = 0
            self.idx = 0
            self.epoch = getattr(self, "epoch", -1) + 1
            self.kb.semkey[self.sem] = (self.name, self.epoch)
        self.idx += 1
        if self.kb.needed is None:
            self.cnt = self.idx
            return (self.sem, self.cnt), True
        if (self.name, self.epoch, self.idx) in self.kb.needed:
            self.cnt += 1
            return (self.sem, self.cnt), True
        return (self.sem, self.cnt), False


class KB:
    def __init__(self, nc, needed=None):
        self.nc = nc
        self.needed = needed
        self.waited = set()
        self.semkey = {}
        self.nsem = 0
        self.pe = Eng(self, "pe", nc.tensor, True)
        self.act = Eng(self, "act", nc.scalar)
        self.dve = Eng(self, "dve", nc.vector)
        self.pool = Eng(self, "pool", nc.gpsimd)
        self.sp = Eng(self, "sp", nc.sync)
        self.engs = [self.pe, self.act, self.dve, self.pool, self.sp]
        self.dma_slots = []
        self.nins = 0

    def newsem(self, name):
        self.nsem += 1
        return self.nc.alloc_semaphore(f"s_{name}_{self.nsem}")

    def _deps(self, reads, writes):
        deps = {}

        def add(ev):
            if ev is not None:
                if deps.get(ev[0], 0) < ev[1]:
                    deps[ev[0]] = ev[1]
        for r in reads:
            add(r.res.w)
        for w in writes:
            add(w.res.w)
            for s, v in w.res.r.items():
                add((s, v))
        return deps

    def _waits(self, eng, deps):
        for sem, val in deps.items():
            if eng.is_pe and sem in eng.own:
                continue
            if eng.seen.get(sem, 0) >= val:
                continue
            self._emit_wait(eng, sem, val)

    def _emit_wait(self, eng, sem, val):
        if val <= 0:
            eng.seen[sem] = val
            return
        eng.h.wait_ge(sem, val)
        eng.seen[sem] = val
        k = self.semkey.get(sem)
        if k is not None and self.needed is None:
            self.waited.add((k[0], k[1], val))

    def _mark(self, ev, reads, writes):
        for r in reads:
            if r.res.r.get(ev[0], 0) < ev[1]:
                r.res.r[ev[0]] = ev[1]
        for w in writes:
            w.res.w = ev
            w.res.r = {}

    def op(self, eng, fn, reads=(), writes=()):
        self._waits(eng, self._deps(reads, writes))
        ins = fn(eng.h)
        ev, do_inc = eng.tick()
        if do_inc:
            ins.then_inc(ev[0], 1)
        self._mark(ev, reads, writes)
        self.nins += 1
        eng.total = getattr(eng, "total", 0) + 1
        return ev

    def dma(self, eng, out, in_, reads=(), writes=(), **kw):
        self._waits(eng, self._deps(reads, writes))
        if len(eng.ring) < 12:
            slot = [self.newsem(eng.name + "d"), 0]
            eng.ring.append(slot)
            self.dma_slots.append(slot)
        else:
            slot = eng.ring[eng.ri % len(eng.ring)]
            eng.ri += 1
            if eng.seen.get(slot[0], 0) < slot[1]:
                self._emit_wait(eng, slot[0], slot[1])
        ins = eng.h.dma_start(out=out, in_=in_, **kw)
        slot[1] += 16
        ins.then_inc(slot[0], 16)
        ev = (slot[0], slot[1])
        self._mark(ev, reads, writes)
        self.nins += 1
        return ev

    def barrier(self):
        evs = {}
        for e in self.engs:
            if e.sem is not None and e.cnt > 0:
                evs[e.sem] = e.cnt
        for s in self.dma_slots:
            if s[1] > 0:
                evs[s[0]] = s[1]
        for e in self.engs:
            for sem, val in evs.items():
                if e.seen.get(sem, 0) >= val:
                    continue
                if sem in e.own and e.is_pe:
                    continue
                self._emit_wait(e, sem, val)


class Ctx:
    pass


_UN = [0]


def uname(n):
    _UN[0] += 1
    return f"{n}_{_UN[0]}"


def build_program(stop_after=None, dbg=False):
    _UN[0] = 0
    nc1, kb1 = _build_program(stop_after, dbg, None)
    _UN[0] = 0
    return _build_program(stop_after, dbg, kb1.waited)


def _build_program(stop_after, dbg, needed):
    nc = bass.Bass("TRN2", target_bir_lowering=False)
    kb = KB(nc, needed)
    g = Ctx()
    g.nc, g.kb = nc, kb
    di = lambda n, s: nc.dram_tensor(n, list(s), F32, kind="ExternalInput").ap()
    do = lambda n, s: nc.dram_tensor(n, list(s), F32, kind="ExternalOutput").ap()
    ds = lambda n, s, dt=F32: nc.dram_tensor(n, list(s), dt, kind="Internal").ap()
    I = Ctx()
    g.I = I
    I.x_prompt = di("x_prompt", (TP, D))
    I.x_sample = di("x_sample", (TSM, D))
    I.mem_prompt = di("mem_prompt", (NMEM, D))
    I.cache_k = di("cache_mem_k", (2, NS, NMEM, 512))
    I.cache_v = di("cache_mem_v", (2, NS, NMEM, 512))
    I.state_rwkv = di("state_rwkv", (NS, 24, 64, 64))
    I.state_shift = di("state_shift", (NS, D))
    I.state_hgrn = di("state_hgrn", (NS, 12, 128, 128))
    I.state_conv = di("state_conv", (2, NS * 2, DFF))
    I.norm_mix = di("norm_mix", (2, D))
    I.norm_ffn = di("norm_ffn", (2, D))
    I.norm_final = di("norm_final", (1, D))
    I.mem_norm = di("mem_norm", (2, D))
    I.w_mem_kv = di("w_mem_kv", (2, D, 1024))
    I.a_w_in = di("a_w_in", (D, NCA))
    I.a_mu = di("a_mu", (5056, 1))
    I.a_w0 = di("a_w0", (1536, 1))
    I.a_w2 = di("a_w2", (96, 1536))
    I.a_a0 = di("a_a0", (1536, 1))
    I.a_a2 = di("a_a2", (96, 1536))
    I.a_g2 = di("a_g2", (256, 1536))
    I.a_k_k = di("a_k_k", (1536, 1))
    I.a_k_a = di("a_k_a", (1536, 1))
    I.a_r_k = di("a_r_k", (1536, 1))
    I.a_ln_w = di("a_ln_w", (1536, 1))
    I.a_ln_b = di("a_ln_b", (1536, 1))
    I.a_w_out = di("a_w_out", (D, D))
    I.b_w_in = di("b_w_in", (D, NCB))
    I.b_lb = di("b_lower_bounds", (2, 1536))
    I.b_g_norm = di("b_g_norm", (128, 1))
    I.b_w_out = di("b_w_out", (D, D))
    I.ffn_w_up = di("ffn_w_up", (2, D, 2 * DFF))
    I.ffn_conv_w = di("ffn_conv_w", (2, 3, DFF))
    I.ffn_conv_b = di("ffn_conv_b", (2, DFF))
    I.ffn_w_down = di("ffn_w_down", (2, DFF, D))
    I.consts = di("consts", (128, NCONST, 128))
    O = Ctx()
    g.O = O
    O.y_prompt = do("y_prompt", (TP, D))
    O.y_sample = do("y_sample", (TSM, D))
    O.mem_k = do("mem_k_prompt", (2, NMEM, 512))
    O.mem_v = do("mem_v_prompt", (2, NMEM, 512))
    O.rwkv_prompt = do("rwkv_prompt", (24 * 64, 64))
    O.rwkv_sample = do("rwkv_sample", (NS, 24 * 64, 64))
    O.shift_prompt = do("shift_prompt", (1, D))
    O.shift_sample = do("shift_sample", (NS, D))
    O.hgrn_prompt = do("hgrn_prompt", (12, 128, 128))
    O.hgrn_sample = do("hgrn_sample", (NS, 12, 128, 128))
    O.conv_prompt = do("conv_prompt", (2, 2, DFF))
    O.conv_sample = do("conv_sample", (2, NS * 2, DFF))
    S = Ctx()
    g.S = S
    S.xres = ds("xres", (TT, D))
    S.pT = ds("pT", (NCB, TTX))
    S.auxT = ds("auxT", (3, 1536, TT))
    S.catT = ds("catT", (D, TT), BF16)
    S.wupb = ds("wupb", (2, 44, 128, 16 * 128), BF16)
    S.wdnb = ds("wdnb", (4, 128, 44 * 512), BF16)
    if dbg:
        O.dbg = do("dbg", (NCB, TTX))
        O.dbg2 = do("dbg2", (TT, D))
        O.dbg3 = nc.dram_tensor("dbg3", [D, TT], BF16, kind="ExternalOutput").ap()
        O.dbg4 = do("dbg4", (3, 1536, TT))

    g.cst = T(nc.alloc_sbuf_tensor("cst", [128, NCONST, 128], F32))
    g.ps = [T(nc.alloc_psum_tensor(f"ps{i}", [128, 512], F32)) for i in range(8)]
    g.psi = 0
    g.KTp = T(nc.alloc_sbuf_tensor("KTp", [128, 2, 4, NMEM], BF16))
    g.Vp = T(nc.alloc_sbuf_tensor("Vp", [128, 2, 2, 512], BF16))
    kb.dma(kb.sp, g.cst[:], I.consts[:, :, :], writes=[g.cst])

    phases = [("memkv", phase_memkv)]
    for l in range(2):
        phases.append((f"norm{l}", lambda g, l=l: phase_norm_mix(g, l)))
        phases.append((f"inproj{l}", lambda g, l=l: phase_inproj(g, l)))
        phases.append((f"attn{l}", lambda g, l=l: phase_attn(g, l)))
        phases.append((f"mix{l}", phase_rwkv if l == 0 else phase_hgrn))
        phases.append((f"outproj{l}", lambda g, l=l: phase_outproj(g, l)))
        phases.append((f"ffn{l}", lambda g, l=l: phase_ffn(g, l)))
    phases.append(("final", phase_final))
    kb.marks = []
    for name, fn in phases:
        fn(g)
        kb.barrier()
        kb.marks.append((name, getattr(kb.pe, "total", 0), getattr(kb.dve, "total", 0)))
        if stop_after == name:
            break
    if dbg:
        kb.dma(kb.sp, O.dbg[:, :], S.pT[:, :])
        kb.dma(kb.sp, O.dbg2[:, :], S.xres[:, :])
        kb.dma(kb.sp, O.dbg3[:, :], S.catT[:, :])
        kb.dma(kb.sp, O.dbg4[:, :, :], S.auxT[:, :, :])
    kb.barrier()
    return nc, kb


def next_ps(g):
    p = g.ps[g.psi % getattr(g, "ps_n", 8)]
    g.psi += 1
    return p


def rms_rstd(g, es_tiles, xt, n, tag):
    kb = g.kb
    sq, ssq, rstd = es_tiles
    kb.op(kb.act, lambda e: e.activation(out=sq[0:n, :], in_=xt[0:n, :], func=AF.Square, accum_out=ssq[0:n, :]),
          reads=[xt], writes=[sq, ssq])
    kb.op(kb.dve, lambda e: e.tensor_scalar(out=rstd[0:n, :], in0=ssq[0:n, :], scalar1=1.0 / D, scalar2=EPS,
                                            op0=ALU.mult, op1=ALU.add), reads=[ssq], writes=[rstd])
    kb.op(kb.act, lambda e: e.sqrt(out=rstd[0:n, :], in_=rstd[0:n, :]), reads=[rstd], writes=[rstd])
    kb.op(kb.dve, lambda e: e.reciprocal(out=rstd[0:n, :], in_=rstd[0:n, :]), reads=[rstd], writes=[rstd])
    return rstd


def transpose_to_hT(g, h, n, hT, col0, scale_cols=None):
    kb = g.kb
    for q in range(4):
        ps = next_ps(g)
        for i in range(4):
            kc = q * 4 + i
            kb.op(kb.pe, lambda e, kc=kc, i=i: e.transpose(out=ps[:, i * 128:i * 128 + n],
                                                           in_=h[0:n, kc * 128:(kc + 1) * 128],
                                                           identity=g.cst[0:n, C_ID, 0:n]),
                  reads=[h, g.cst], writes=[ps])
        src = ps[:, :].rearrange("p (a b) -> p a b", b=128)[:, :, 0:n]
        dst = hT[:, q * 4:(q + 1) * 4, col0:col0 + n]
        eng = kb.dve if q % 2 == 0 else kb.act
        if eng is kb.dve:
            kb.op(eng, lambda e: e.tensor_copy(out=dst, in_=src), reads=[ps], writes=[hT])
        else:
            kb.op(eng, lambda e: e.copy(out=dst, in_=src), reads=[ps], writes=[hT])


def phase_memkv(g):
    nc, kb, I, O = g.nc, g.kb, g.I, g.O
    with ExitStack() as es:
        al = lambda name, shape, dt=F32: T(es.enter_context(nc.sbuf_tensor(uname(name), shape, dt)))
        xt = [al(f"mk_x{i}", [128, D]) for i in range(2)]
        sq = al("mk_sq", [128, D])
        ssq = al("mk_ssq", [128, 1])
        rstd = al("mk_rstd", [128, 1])
        mn = al("mk_mn", [128, 2, 16])
        hmT = al("mk_hmT", [128, 16, NMEM], F32)
        hml = [al(f"mk_hml{l}", [128, 16, NMEM], BF16) for l in range(2)]
        wf = [al(f"mk_wf{i}", [128, 16, 256]) for i in range(2)]
        wb = al("mk_wb", [128, 16, 1024], BF16)
        stg = [al(f"mk_stg{i}", [128, 512]) for i in range(2)]
        kb.dma(kb.sp, mn[:], I.mem_norm.rearrange("l (kc p) -> p l kc", p=128), writes=[mn],
               allow_slow_non_contiguous=True)
        for mt in range(2):
            kb.dma(kb.sp, xt[mt][:], I.mem_prompt[mt * 128:(mt + 1) * 128, :], writes=[xt[mt]])
            r = rms_rstd(g, (sq, ssq, rstd), xt[mt], 128, "mk")
            kb.op(kb.act, lambda e: e.activation(out=xt[mt][:], in_=xt[mt][:], func=AF.Copy, scale=r[:, 0:1]),
                  reads=[xt[mt], r], writes=[xt[mt]])
            for q in range(4):
                ps = next_ps(g)
                for i in range(4):
                    kc = q * 4 + i
                    kb.op(kb.pe, lambda e, kc=kc, i=i: e.transpose(out=ps[:, i * 128:(i + 1) * 128],
                                                                   in_=xt[mt][:, kc * 128:(kc + 1) * 128],
                                                                   identity=g.cst[:, C_ID, :]),
                          reads=[xt[mt], g.cst], writes=[ps])
                kb.op(kb.dve, lambda e: e.tensor_copy(out=hmT[:, q * 4:(q + 1) * 4, mt * 128:(mt + 1) * 128],
                                                      in_=ps[:, :].rearrange("p (a b) -> p a b", b=128)),
                      reads=[ps], writes=[hmT])
        for l in range(2):
            for kc in range(16):
                kb.op(kb.dve, lambda e, kc=kc: e.tensor_scalar(out=hml[l][:, kc, :], in0=hmT[:, kc, :],
                                                               scalar1=mn[:, l, kc:kc + 1], scalar2=None,
                                                               op0=ALU.mult), reads=[hmT, mn], writes=[hml[l]])
            wsrc = I.w_mem_kv[l].rearrange("(kc p) n -> p kc n", p=128)
            for pc in range(4):
                w = wf[pc % 2]
                kb.dma(kb.sp, w[:], wsrc[:, :, pc * 256:(pc + 1) * 256], writes=[w])
                kb.op(kb.pool, lambda e, pc=pc, w=w: e.tensor_copy(out=wb[:, :, pc * 256:(pc + 1) * 256], in_=w[:]),
                      reads=[w], writes=[wb])
            for mt in range(2):
                for ct in range(2):
                    ps = next_ps(g)
                    for kc in range(16):
                        kb.op(kb.pe, lambda e, kc=kc: e.matmul(ps[:, :], lhsT=hml[l][:, kc, mt * 128:(mt + 1) * 128],
                                                               rhs=wb[:, kc, ct * 512:(ct + 1) * 512],
                                                               start=(kc == 0), stop=(kc == 15)),
                              reads=[hml[l], wb], writes=[ps])
                    if ct == 0:
                        s = stg[mt % 2]
                        kb.op(kb.act, lambda e: e.copy(out=s[:], in_=ps[:, :]), reads=[ps], writes=[s])
                        kb.dma(kb.pool, O.mem_k[l, mt * 128:(mt + 1) * 128, :], s[:], reads=[s])
                    else:
                        s = stg[mt % 2]
                        kb.op(kb.act, lambda e: e.copy(out=s[:], in_=ps[:, :]), reads=[ps], writes=[s])
                        kb.op(kb.dve, lambda e: e.tensor_copy(out=g.Vp[:, l, mt, :], in_=s[:]), reads=[s], writes=[g.Vp])
                        kb.dma(kb.pool, O.mem_v[l, mt * 128:(mt + 1) * 128, :], s[:], reads=[s])
            for h in range(4):
                ps = next_ps(g)
                for kc in range(16):
                    kb.op(kb.pe, lambda e, kc=kc: e.matmul(ps[:, 0:NMEM], lhsT=wb[:, kc, h * 128:(h + 1) * 128],
                                                           rhs=hml[l][:, kc, :], start=(kc == 0), stop=(kc == 15)),
                          reads=[hml[l], wb], writes=[ps])
                kb.op(kb.dve, lambda e: e.tensor_copy(out=g.KTp[:, l, h, :], in_=ps[:, 0:NMEM]),
                      reads=[ps], writes=[g.KTp])


def phase_norm_mix(g, l):
    nc, kb, I, O, S = g.nc, g.kb, g.I, g.O, g.S
    g.hT_stack = ExitStack()
    ntok = TTX if l == 0 else TT
    g.hT = T(g.hT_stack.enter_context(nc.sbuf_tensor(uname(f"hT{l}"), [128, 16, ntok], BF16)))
    with ExitStack() as es:
        al = lambda name, shape, dt=F32: T(es.enter_context(nc.sbuf_tensor(uname(name), shape, dt)))
        xt = [al(f"nm_x{i}", [128, D]) for i in range(2)]
        ht = [al(f"nm_h{i}", [128, D]) for i in range(2)]
        sq = al("nm_sq", [128, D])
        ssq = al("nm_ssq", [128, 1])
        rstd = al("nm_rstd", [128, 1])
        gbc = al("nm_g", [128, D])
        kb.dma(kb.sp, gbc[:], I.norm_mix[l:l + 1, :].partition_broadcast(128), writes=[gbc])
        for i in range(17):
            x = xt[i % 2]
            h = ht[i % 2]
            if l == 0:
                src = I.x_prompt[i * 128:(i + 1) * 128, :] if i < 16 else I.x_sample[:, :]
            else:
                src = S.xres[i * 128:(i + 1) * 128, :]
            kb.dma(kb.sp, x[:], src, writes=[x])
            if l == 0:
                kb.dma(kb.pool, S.xres[i * 128:(i + 1) * 128, :], x[:], reads=[x])
            r = rms_rstd(g, (sq, ssq, rstd), x, 128, "nm")
            kb.op(kb.dve, lambda e: e.scalar_tensor_tensor(out=h[:], in0=x[:], scalar=r[:, 0:1], in1=gbc[:],
                                                           op0=ALU.mult, op1=ALU.mult),
                  reads=[x, r, gbc], writes=[h])
            if l == 0:
                if i == 15:
                    kb.dma(kb.pool, O.shift_prompt[0:1, :], h[127:128, :], reads=[h])
                if i == 16:
                    kb.dma(kb.pool, O.shift_sample[:, :],
                           h[TS - 1:128:TS, :], reads=[h])
            transpose_to_hT(g, h, 128, g.hT, i * 128)
        if l == 0:
            x = xt[1]
            kb.dma(kb.sp, x[0:NS, :], I.state_shift[:, :], writes=[x])
            transpose_to_hT(g, x, NS, g.hT, TT)


def linear_fm(g, es, wsrc, ncols, KC, actT, tok_tiles, epilogue, tag, wblk=256):
    nc, kb = g.nc, g.kb
    al = lambda name, shape, dt=F32: T(es.enter_context(nc.sbuf_tensor(uname(name), shape, dt)))
    wf = [al(f"{tag}_wf{i}", [128, KC, wblk]) for i in range(2)]
    wb = [al(f"{tag}_wb{i}", [128, KC, wblk], BF16) for i in range(2)]
    nblk = (ncols + wblk - 1) // wblk
    for b in range(nblk):
        c0 = b * wblk
        cw = min(wblk, ncols - c0)
        f = wf[b % 2]
        w = wb[b % 2]
        kb.dma(kb.sp, f[:, :, 0:cw], wsrc[:, :, c0:c0 + cw], writes=[f])
        ceng = kb.pool if b % 2 == 0 else kb.dve
        kb.op(ceng, lambda e: e.tensor_copy(out=w[:, :, 0:cw], in_=f[:, :, 0:cw]), reads=[f], writes=[w])
        for jj in range(0, cw, 128):
            ncj = min(128, cw - jj)
            j = (c0 + jj) // 128
            for ti, (t0, n) in enumerate(tok_tiles):
                ps = next_ps(g)
                for kc in range(KC):
                    kb.op(kb.pe, lambda e, kc=kc: e.matmul(ps[0:ncj, 0:n], lhsT=w[:, kc, jj:jj + ncj],
                                                           rhs=actT[:, kc, t0:t0 + n],
                                                           start=(kc == 0), stop=(kc == KC - 1)),
                          reads=[w, actT], writes=[ps])
                epilogue(j, ncj, ti, t0, n, ps)


def tok_tiles_of(n):
    out = []
    t = 0
    while t < n:
        m = min(512, n - t)
        out.append((t, m))
        t += m
    return out


def phase_inproj(g, l):
    nc, kb, I, O, S = g.nc, g.kb, g.I, g.O, g.S
    ncols = NCA if l == 0 else NCB
    ntok = TTX if l == 0 else TT
    w = (I.a_w_in if l == 0 else I.b_w_in).rearrange("(kc p) n -> p kc n", p=128)
    with ExitStack() as es:
        al = lambda name, shape, dt=F32: T(es.enter_context(nc.sbuf_tensor(uname(name), shape, dt)))
        stg = [al(f"ip_stg{i}", [128, 512]) for i in range(4)]
        cnt = [0]

        def epi(j, ncj, ti, t0, n, ps):
            s = stg[cnt[0] % 4]
            if cnt[0] % 2 == 0:
                kb.op(kb.act, lambda e: e.copy(out=s[0:ncj, 0:n], in_=ps[0:ncj, 0:n]), reads=[ps], writes=[s])
            else:
                kb.op(kb.dve, lambda e: e.tensor_copy(out=s[0:ncj, 0:n], in_=ps[0:ncj, 0:n]), reads=[ps], writes=[s])
            kb.dma(kb.pool, S.pT[j * 128:j * 128 + ncj, t0:t0 + n], s[0:ncj, 0:n], reads=[s])
            cnt[0] += 1
        linear_fm(g, es, w, ncols, 16, g.hT, tok_tiles_of(ntok), epi, f"ip{l}")
    g.hT_stack.close()


def phase_attn(g, l):
    nc, kb, I, O, S = g.nc, g.kb, g.I, g.O, g.S
    scale = 128.0 ** -0.5
    with ExitStack() as es:
        al = lambda name, shape, dt=F32: T(es.enter_context(nc.sbuf_tensor(uname(name), shape, dt)))
        qp0 = [al(f"at_qp0{i}", [128, TP]) for i in range(2)]
        qp = [al(f"at_qp{i}", [128, TP], BF16) for i in range(2)]
        qs0 = al("at_qs0", [128, 4, TSM])
        qs = al("at_qs", [128, 4, TSM], BF16)
        vin0 = [al(f"at_vin0{i}", [128, 2, 512]) for i in range(2)]
        pb = [al(f"at_p{i}", [128, NMEM]) for i in range(3)]
        pT = [al(f"at_pT{i}", [128, 2, 128], BF16) for i in range(3)]
        sm = [[al(f"at_sm{i}_{k}", [128, 1]) for k in range(4)] for i in range(3)]
        ob = [al(f"at_o{i}", [128, 128], BF16) for i in range(3)]
        kin = [al(f"at_kin{i}", [128, 2, 512]) for i in range(2)]
        vin = [al(f"at_vin{i}", [128, 2, 512], BF16) for i in range(2)]
        kts = [al(f"at_kts{i}", [128, 4, NMEM], BF16) for i in range(2)]
        uc = [0]

        def unit(qt, qap, n, ktT, ktap, vT, vap, dest):
            u = uc[0] % 3
            uc[0] += 1
            p, pt, (mx, nb, sme, rs), o = pb[u], pT[u], sm[u], ob[u]
            ps = next_ps(g)
            kb.op(kb.pe, lambda e: e.matmul(ps[0:n, 0:NMEM], lhsT=qap, rhs=ktap, start=True, stop=True),
                  reads=[qt, ktT], writes=[ps])
            kb.op(kb.dve, lambda e: e.tensor_reduce(out=mx[0:n, :], in_=ps[0:n, 0:NMEM], axis=AX.X, op=ALU.max),
                  reads=[ps], writes=[mx])
            kb.op(kb.dve, lambda e: e.tensor_scalar(out=nb[0:n, :], in0=mx[0:n, :], scalar1=-scale, scalar2=None,
                                                    op0=ALU.mult), reads=[mx], writes=[nb])
            kb.op(kb.act, lambda e: e.activation(out=p[0:n, :], in_=ps[0:n, 0:NMEM], func=AF.Exp, bias=nb[0:n, 0:1],
                                                 scale=scale, accum_out=sme[0:n, :]),
                  reads=[ps, nb], writes=[p, sme])
            kb.op(kb.dve, lambda e: e.reciprocal(out=rs[0:n, :], in_=sme[0:n, :]), reads=[sme], writes=[rs])
            kb.op(kb.dve, lambda e: e.tensor_scalar(out=p[0:n, :], in0=p[0:n, :], scalar1=rs[0:n, 0:1], scalar2=None,
                                                    op0=ALU.mult), reads=[p, rs], writes=[p])
            ps2 = next_ps(g)
            for mc in range(2):
                kb.op(kb.pe, lambda e, mc=mc: e.transpose(out=ps2[:, mc * 128:mc * 128 + n],
                                                          in_=p[0:n, mc * 128:(mc + 1) * 128],
                                                          identity=g.cst[0:n, C_ID, 0:n]),
                      reads=[p, g.cst], writes=[ps2])
            kb.op(kb.act, lambda e: e.copy(out=pt[:, :, 0:n],
                                           in_=ps2[:, 0:256].rearrange("p (a b) -> p a b", b=128)[:, :, 0:n]),
                  reads=[ps2], writes=[pt])
            ps3 = next_ps(g)
            for mc in range(2):
                kb.op(kb.pe, lambda e, mc=mc: e.matmul(ps3[:, 0:n], lhsT=vap(mc), rhs=pt[:, mc, 0:n],
                                                       start=(mc == 0), stop=(mc == 1)),
                      reads=[vT, pt], writes=[ps3])
            kb.op(kb.dve, lambda e: e.tensor_copy(out=o[:, 0:n], in_=ps3[:, 0:n]), reads=[ps3], writes=[o])
            kb.dma(kb.pool, dest, o[:, 0:n], reads=[o])

        for h in range(4):
            q = qp[h % 2]
            q0 = qp0[h % 2]
            kb.dma(kb.sp, q0[:], S.pT[h * 128:(h + 1) * 128, 0:TP], writes=[q0])
            kb.op(kb.pool, lambda e: e.tensor_copy(out=q[:], in_=q0[:]), reads=[q0], writes=[q])
            for i in range(16):
                unit(q, q[:, i * 128:(i + 1) * 128], 128, g.KTp, g.KTp[:, l, h, :], g.Vp,
                     lambda mc, h=h: g.Vp[:, l, mc, h * 128:(h + 1) * 128],
                     S.catT[(12 + h) * 128:(13 + h) * 128, i * 128:(i + 1) * 128])
        kb.dma(kb.sp, qs0[:], S.pT[0:512, TP:TT].rearrange("(h p) t -> p h t", p=128), writes=[qs0])
        kb.op(kb.pool, lambda e: e.tensor_copy(out=qs[:], in_=qs0[:]), reads=[qs0], writes=[qs])
        for s in range(NS):
            ki, vi, kt = kin[s % 2], vin[s % 2], kts[s % 2]
            kb.dma(kb.sp, ki[:], I.cache_k[l, s].rearrange("(mt p) c -> p mt c", p=128), writes=[ki])
            vi0 = vin0[s % 2]
            kb.dma(kb.sp, vi0[:], I.cache_v[l, s].rearrange("(mt p) c -> p mt c", p=128), writes=[vi0])
            kb.op(kb.pool, lambda e: e.tensor_copy(out=vi[:], in_=vi0[:]), reads=[vi0], writes=[vi])
            for hh in range(2):
                ps = next_ps(g)
                for a in range(2):
                    for mt in range(2):
                        h = hh * 2 + a
                        kb.op(kb.pe, lambda e, h=h, mt=mt, a=a: e.transpose(
                            out=ps[:, (a * 2 + mt) * 128:(a * 2 + mt + 1) * 128],
                            in_=ki[:, mt, h * 128:(h + 1) * 128], identity=g.cst[:, C_ID, :]),
                            reads=[ki, g.cst], writes=[ps])
                kb.op(kb.act, lambda e: e.copy(out=kt[:, hh * 2:hh * 2 + 2, :],
                                               in_=ps[:, :].rearrange("p (a b) -> p a b", b=256)),
                      reads=[ps], writes=[kt])
            for h in range(4):
                unit(qs, qs[:, h, s * TS:(s + 1) * TS], TS, kt, kt[:, h, :], vi,
                     lambda mc, h=h, vi=vi: vi[:, mc, h * 128:(h + 1) * 128],
                     S.catT[(12 + h) * 128:(13 + h) * 128, TP + s * TS:TP + (s + 1) * TS])


def phase_outproj(g, l):
    nc, kb, I, O, S = g.nc, g.kb, g.I, g.O, g.S
    w = (I.a_w_out if l == 0 else I.b_w_out).rearrange("(kc p) n -> p kc n", p=128)
    with ExitStack() as es:
        al = lambda name, shape, dt=F32: T(es.enter_context(nc.sbuf_tensor(uname(name), shape, dt)))
        catT = al("op_cat", [128, 16, TT], BF16)
        wb = al("op_wb", [128, 16, D], BF16)
        wf = [al(f"op_wf{i}", [128, 16, 128]) for i in range(2)]
        xt = [al(f"op_x{i}", [128, D]) for i in range(2)]
        kb.dma(kb.sp, catT[:], S.catT.rearrange("(kc p) t -> p kc t", p=128), writes=[catT])
        for pc in range(16):
            f = wf[pc % 2]
            kb.dma(kb.sp, f[:], w[:, :, pc * 128:(pc + 1) * 128], writes=[f])
            ce = kb.pool if pc % 2 == 0 else kb.act
            if ce is kb.pool:
                kb.op(ce, lambda e: e.tensor_copy(out=wb[:, :, pc * 128:(pc + 1) * 128], in_=f[:]), reads=[f], writes=[wb])
            else:
                kb.op(ce, lambda e: e.copy(out=wb[:, :, pc * 128:(pc + 1) * 128], in_=f[:]), reads=[f], writes=[wb])
        for i in range(17):
            x = xt[i % 2]
            kb.dma(kb.sp, x[:], S.xres[i * 128:(i + 1) * 128, :], writes=[x])
            for ct in range(4):
                ps = next_ps(g)
                for kc in range(16):
                    kb.op(kb.pe, lambda e, kc=kc: e.matmul(ps[:, :], lhsT=catT[:, kc, i * 128:(i + 1) * 128],
                                                           rhs=wb[:, kc, ct * 512:(ct + 1) * 512],
                                                           start=(kc == 0), stop=(kc == 15)),
                          reads=[catT, wb], writes=[ps])
                kb.op(kb.dve, lambda e: e.tensor_tensor(out=x[:, ct * 512:(ct + 1) * 512], in0=ps[:, :],
                                                        in1=x[:, ct * 512:(ct + 1) * 512], op=ALU.add),
                      reads=[ps, x], writes=[x])
            kb.dma(kb.pool, S.xres[i * 128:(i + 1) * 128, :], x[:], reads=[x])


FFN_GROUPS = [(0, 768, False), (768, 768, False), (1536, 512, True)]


def phase_ffn(g, l):
    nc, kb, I, O, S = g.nc, g.kb, g.I, g.O, g.S
    wup = I.ffn_w_up[l].rearrange("(kc p) n -> p kc n", p=128)
    wdn = I.ffn_w_down[l].rearrange("(j p) n -> p j n", p=128)
    with ExitStack() as es0:
        al0 = lambda name, shape, dt=F32: T(es0.enter_context(nc.sbuf_tensor(uname(name), shape, dt)))
        carry = al0("ff_carry", [128, 44, 2])
        cvo = al0("ff_cvo", [128, 44, 2 + 2 * NS])
        scA = al0("ff_scA", [128, 44, 2 * NS])
        cw = al0("ff_cw", [128, 3, 44])
        cb = al0("ff_cb", [128, 44])
        kb.dma(kb.sp, cw[:], I.ffn_conv_w[l].rearrange("k (j p) -> p k j", p=128), writes=[cw],
               allow_slow_non_contiguous=True)
        kb.dma(kb.sp, cb[:], I.ffn_conv_b[l:l + 1, :].rearrange("o (j p) -> p (o j)", p=128), writes=[cb],
               allow_slow_non_contiguous=True)
        kb.op(kb.dve, lambda e: e.memset(carry[:], 0.0), writes=[carry])
        with ExitStack() as es:
            al = lambda name, shape, dt=F32: T(es.enter_context(nc.sbuf_tensor(uname(name), shape, dt)))
            sc = al("ff_sc", [2 * NS, DFF])
            kb.dma(kb.sp, sc[:], I.state_conv[l], writes=[sc])
            for j in range(44):
                ps = next_ps(g)
                kb.op(kb.pe, lambda e: e.transpose(out=ps[:, 0:2 * NS], in_=sc[:, j * 128:(j + 1) * 128],
                                                   identity=g.cst[0:2 * NS, C_ID, 0:2 * NS]),
                      reads=[sc, g.cst], writes=[ps])
                kb.op(kb.dve, lambda e: e.tensor_copy(out=scA[:, j, :], in_=ps[:, 0:2 * NS]), reads=[ps], writes=[scA])
        kb.barrier()
        for gi, (p0, pn, has_s) in enumerate(FFN_GROUPS):
            ntok = pn + (TSM if has_s else 0)
            with ExitStack() as esg:
                alg = lambda name, shape, dt=F32: T(esg.enter_context(nc.sbuf_tensor(uname(name), shape, dt)))
                gT = alg("ff_gT", [128, 44, ntok], BF16)
                with ExitStack() as es:
                    al = lambda name, shape, dt=F32: T(es.enter_context(nc.sbuf_tensor(uname(name), shape, dt)))
                    hT2 = al("ff_hT", [128, 16, ntok], BF16)
                    with ExitStack() as es2:
                        al2 = lambda name, shape, dt=F32: T(es2.enter_context(nc.sbuf_tensor(uname(name), shape, dt)))
                        xt = [al2(f"ff_x{i}", [128, D]) for i in range(2)]
                        ht = [al2(f"ff_h{i}", [128, D]) for i in range(2)]
                        sq = al2("ff_sq", [128, D])
                        ssq = al2("ff_ssq", [128, 1])
                        rstd = al2("ff_rstd", [128, 1])
                        gbc = al2("ff_g", [128, D])
                        kb.dma(kb.sp, gbc[:], I.norm_ffn[l:l + 1, :].partition_broadcast(128), writes=[gbc])
                        rows = [p0 + k * 128 for k in range(pn // 128)] + ([TP] if has_s else [])
                        for k, r0 in enumerate(rows):
                            x, h = xt[k % 2], ht[k % 2]
                            kb.dma(kb.sp, x[:], S.xres[r0:r0 + 128, :], writes=[x])
                            r = rms_rstd(g, (sq, ssq, rstd), x, 128, "ff")
                            kb.op(kb.dve, lambda e: e.scalar_tensor_tensor(out=h[:], in0=x[:], scalar=r[:, 0:1],
                                                                           in1=gbc[:], op0=ALU.mult, op1=ALU.mult),
                                  reads=[x, r, gbc], writes=[h])
                            transpose_to_hT(g, h, 128, hT2, k * 128)
                    kb.barrier()
                    wfa = [al(f"ff_wfa{i}", [128, 16, 128]) for i in range(3)]
                    wfv = [al(f"ff_wfv{i}", [128, 16, 128]) for i in range(3)]
                    wba = [al(f"ff_wba{i}", [128, 16, 128], BF16) for i in range(2)]
                    wbv = [al(f"ff_wbv{i}", [128, 16, 128], BF16) for i in range(2)]
                    aext = [al(f"ff_ae{i}", [128, 2 + pn]) for i in range(2)]
                    aexs = [al(f"ff_as{i}", [128, NS, TS + 2]) for i in range(2)]
                    tmp = [al(f"ff_t{i}", [128, 512]) for i in range(2)]
                    tmp2 = [al(f"ff_u{i}", [128, 512]) for i in range(2)]
                    ptiles = tok_tiles_of(pn)
                    tc = [0]
                    for j in range(44):
                        fa, fv, ba, bv = wfa[j % 3], wfv[j % 3], wba[j % 2], wbv[j % 2]
                        ae, asx = aext[j % 2], aexs[j % 2]
                        if gi == 0:
                            kb.dma(kb.sp, fa[:], wup[:, :, j * 128:(j + 1) * 128], writes=[fa])
                            kb.dma(kb.sp, fv[:], wup[:, :, DFF + j * 128:DFF + (j + 1) * 128], writes=[fv])
                            kb.op(kb.act, lambda e: e.copy(out=ba[:], in_=fa[:]), reads=[fa], writes=[ba])
                            kb.op(kb.pool, lambda e: e.tensor_copy(out=bv[:], in_=fv[:]), reads=[fv], writes=[bv])
                            kb.dma(kb.pool, S.wupb[0, j], ba[:, :, :].rearrange("p a b -> p (a b)"), reads=[ba])
                            kb.dma(kb.pool, S.wupb[1, j], bv[:, :, :].rearrange("p a b -> p (a b)"), reads=[bv])
                        else:
                            kb.dma(kb.sp, ba[:, :, :].rearrange("p a b -> p (a b)"), S.wupb[0, j], writes=[ba])
                            kb.dma(kb.sp, bv[:, :, :].rearrange("p a b -> p (a b)"), S.wupb[1, j], writes=[bv])
                        kb.op(kb.dve, lambda e: e.tensor_copy(out=ae[:, 0:2], in_=carry[:, j, :]), reads=[carry], writes=[ae])
                        for (t0, n) in ptiles:
                            psa = next_ps(g)
                            for kc in range(16):
                                kb.op(kb.pe, lambda e, kc=kc: e.matmul(psa[:, 0:n], lhsT=ba[:, kc, :],
                                                                       rhs=hT2[:, kc, t0:t0 + n],
                                                                       start=(kc == 0), stop=(kc == 15)),
                                      reads=[ba, hT2], writes=[psa])
                            psv = next_ps(g)
                            for kc in range(16):
                                kb.op(kb.pe, lambda e, kc=kc: e.matmul(psv[:, 0:n], lhsT=bv[:, kc, :],
                                                                       rhs=hT2[:, kc, t0:t0 + n],
                                                                       start=(kc == 0), stop=(kc == 15)),
                                      reads=[bv, hT2], writes=[psv])
                            kb.op(kb.act, lambda e: e.copy(out=ae[:, 2 + t0:2 + t0 + n], in_=psa[:, 0:n]),
                                  reads=[psa], writes=[ae])
                            t1, t2 = tmp[tc[0] % 2], tmp2[tc[0] % 2]
                            tc[0] += 1
                            kb.op(kb.dve, lambda e: e.tensor_scalar(out=t1[:, 0:n], in0=ae[:, t0:t0 + n],
                                                                    scalar1=cw[:, 0, j:j + 1], scalar2=cb[:, j:j + 1],
                                                                    op0=ALU.mult, op1=ALU.add),
                                  reads=[ae, cw, cb], writes=[t1])
                            for tap in (1, 2):
                                kb.op(kb.dve, lambda e, tap=tap: e.scalar_tensor_tensor(
                                    out=t1[:, 0:n], in0=ae[:, t0 + tap:t0 + tap + n], scalar=cw[:, tap, j:j + 1],
                                    in1=t1[:, 0:n], op0=ALU.mult, op1=ALU.add), reads=[ae, cw, t1], writes=[t1])
                            kb.op(kb.act, lambda e: e.activation(out=t2[:, 0:n], in_=t1[:, 0:n], func=AF.Gelu_apprx_tanh),
                                  reads=[t1], writes=[t2])
                            kb.op(kb.dve, lambda e: e.tensor_tensor(out=gT[:, j, t0:t0 + n], in0=psv[:, 0:n],
                                                                    in1=t2[:, 0:n], op=ALU.mult),
                                  reads=[psv, t2], writes=[gT])
                        if gi < len(FFN_GROUPS) - 1:
                            kb.op(kb.dve, lambda e: e.tensor_copy(out=carry[:, j, :], in_=ae[:, pn:pn + 2]),
                                  reads=[ae], writes=[carry])
                        else:
                            kb.op(kb.dve, lambda e: e.tensor_copy(out=cvo[:, j, 0:2], in_=ae[:, pn:pn + 2]),
                                  reads=[ae], writes=[cvo])
                        if has_s:
                            psa = next_ps(g)
                            for kc in range(16):
                                kb.op(kb.pe, lambda e, kc=kc: e.matmul(psa[:, 0:TSM], lhsT=ba[:, kc, :],
                                                                       rhs=hT2[:, kc, pn:pn + TSM],
                                                                       start=(kc == 0), stop=(kc == 15)),
                                      reads=[ba, hT2], writes=[psa])
                            psv = next_ps(g)
                            for kc in range(16):
                                kb.op(kb.pe, lambda e, kc=kc: e.matmul(psv[:, 0:TSM], lhsT=bv[:, kc, :],
                                                                       rhs=hT2[:, kc, pn:pn + TSM],
                                                                       start=(kc == 0), stop=(kc == 15)),
                                      reads=[bv, hT2], writes=[psv])
                            kb.op(kb.act, lambda e: e.copy(out=asx[:, :, 2:2 + TS],
                                                           in_=psa[:, 0:TSM].rearrange("p (s t) -> p s t", t=TS)),
                                  reads=[psa], writes=[asx])
                            kb.op(kb.dve, lambda e: e.tensor_copy(out=asx[:, :, 0:2],
                                                                  in_=scA[:, j, :].rearrange("p (s t) -> p s t", t=2)),
                                  reads=[scA], writes=[asx])
                            kb.op(kb.dve, lambda e: e.tensor_copy(out=cvo[:, j, 2:2 + 2 * NS].rearrange("p (s t) -> p s t", t=2),
                                                                  in_=asx[:, :, TS:TS + 2]), reads=[asx], writes=[cvo])
                            t1, t2 = tmp[tc[0] % 2], tmp2[tc[0] % 2]
                            tc[0] += 1
                            v3 = lambda tt: tt[:, 0:TSM].rearrange("p (s t) -> p s t", t=TS)
                            kb.op(kb.dve, lambda e: e.tensor_scalar(out=v3(t1), in0=asx[:, :, 0:TS],
                                                                    scalar1=cw[:, 0, j:j + 1], scalar2=cb[:, j:j + 1],
                                                                    op0=ALU.mult, op1=ALU.add),
                                  reads=[asx, cw, cb], writes=[t1])
                            for tap in (1, 2):
                                kb.op(kb.dve, lambda e, tap=tap: e.scalar_tensor_tensor(
                                    out=v3(t1), in0=asx[:, :, tap:tap + TS], scalar=cw[:, tap, j:j + 1],
                                    in1=v3(t1), op0=ALU.mult, op1=ALU.add), reads=[asx, cw, t1], writes=[t1])
                            kb.op(kb.act, lambda e: e.activation(out=t2[:, 0:TSM], in_=t1[:, 0:TSM], func=AF.Gelu_apprx_tanh),
                                  reads=[t1], writes=[t2])
                            kb.op(kb.dve, lambda e: e.tensor_tensor(out=gT[:, j, pn:pn + TSM], in0=psv[:, 0:TSM],
                                                                    in1=t2[:, 0:TSM], op=ALU.mult),
                                  reads=[psv, t2], writes=[gT])
                kb.barrier()
                with ExitStack() as es:
                    al = lambda name, shape, dt=F32: T(es.enter_context(nc.sbuf_tensor(uname(name), shape, dt)))
                    wdbs = [al(f"ff_wdb{i}", [128, 44, 512], BF16) for i in range(2)]
                    wdf = [al(f"ff_wdf{i}", [128, 2, 512]) for i in range(2)]
                    xs = [al(f"ff_xs{i}", [128, 512]) for i in range(3)]
                    rows = [(p0 + k * 128, k * 128) for k in range(pn // 128)] + ([(TP, pn)] if has_s else [])
                    xc = [0]
                    for ct in range(4):
                        wdb = wdbs[ct % 2]
                        if gi > 0:
                            kb.dma(kb.sp, wdb[:, :, :].rearrange("p a b -> p (a b)"), S.wdnb[ct], writes=[wdb])
                        for pc in (range(22) if gi == 0 else []):
                            f = wdf[pc % 2]
                            kb.dma(kb.sp, f[:], wdn[:, pc * 2:(pc + 1) * 2, ct * 512:(ct + 1) * 512], writes=[f])
                            if pc % 3 == 0:
                                kb.op(kb.pool, lambda e: e.tensor_copy(out=wdb[:, pc * 2:(pc + 1) * 2, :], in_=f[:]),
                                      reads=[f], writes=[wdb])
                            else:
                                kb.op(kb.act, lambda e: e.copy(out=wdb[:, pc * 2:(pc + 1) * 2, :], in_=f[:]),
                                      reads=[f], writes=[wdb])
                        if gi == 0:
                            kb.dma(kb.pool, S.wdnb[ct], wdb[:, :, :].rearrange("p a b -> p (a b)"), reads=[wdb])
                        for b0 in range(0, len(rows), 4):
                            batch = rows[b0:b0 + 4]
                            pss = [next_ps(g) for _ in batch]
                            for j in range(44):
                                for (r0, c0), ps in zip(batch, pss):
                                    kb.op(kb.pe, lambda e, c0=c0, ps=ps: e.matmul(ps[:, :], lhsT=gT[:, j, c0:c0 + 128],
                                                                                  rhs=wdb[:, j, :],
                                                                                  start=(j == 0), stop=(j == 43)),
                                          reads=[gT, wdb], writes=[ps])
                            for (r0, c0), ps in zip(batch, pss):
                                x = xs[xc[0] % 3]
                                xc[0] += 1
                                kb.dma(kb.sp, x[:], S.xres[r0:r0 + 128, ct * 512:(ct + 1) * 512], writes=[x])
                                kb.op(kb.dve, lambda e: e.tensor_tensor(out=x[:], in0=ps[:, :], in1=x[:], op=ALU.add),
                                      reads=[ps, x], writes=[x])
                                kb.dma(kb.pool, S.xres[r0:r0 + 128, ct * 512:(ct + 1) * 512], x[:], reads=[x])
            kb.barrier()
        with ExitStack() as es:
            al = lambda name, shape, dt=F32: T(es.enter_context(nc.sbuf_tensor(uname(name), shape, dt)))
            co = al("ff_co", [2 + 2 * NS, DFF])
            nr = 2 + 2 * NS
            for j in range(44):
                ps = next_ps(g)
                kb.op(kb.pe, lambda e: e.transpose(out=ps[0:nr, 0:128], in_=cvo[:, j, :], identity=g.cst[:, C_ID, :]),
                      reads=[cvo, g.cst], writes=[ps])
                kb.op(kb.dve, lambda e: e.tensor_copy(out=co[:, j * 128:(j + 1) * 128], in_=ps[0:nr, 0:128]),
                      reads=[ps], writes=[co])
            kb.dma(kb.pool, O.conv_prompt[l], co[0:2, :], reads=[co])
            kb.dma(kb.pool, O.conv_sample[l], co[2:nr, :], reads=[co])
            kb.barrier()


def phase_final(g):
    nc, kb, I, O, S = g.nc, g.kb, g.I, g.O, g.S
    with ExitStack() as es:
        al = lambda name, shape, dt=F32: T(es.enter_context(nc.sbuf_tensor(uname(name), shape, dt)))
        xt = [al(f"fn_x{i}", [128, D]) for i in range(2)]
        ht = [al(f"fn_h{i}", [128, D]) for i in range(2)]
        sq = al("fn_sq", [128, D])
        ssq = al("fn_ssq", [128, 1])
        rstd = al("fn_rstd", [128, 1])
        gbc = al("fn_g", [128, D])
        kb.dma(kb.sp, gbc[:], I.norm_final[0:1, :].partition_broadcast(128), writes=[gbc])
        for i in range(17):
            x, h = xt[i % 2], ht[i % 2]
            kb.dma(kb.sp, x[:], S.xres[i * 128:(i + 1) * 128, :], writes=[x])
            r = rms_rstd(g, (sq, ssq, rstd), x, 128, "fn")
            kb.op(kb.dve, lambda e: e.scalar_tensor_tensor(out=h[:], in0=x[:], scalar=r[:, 0:1], in1=gbc[:],
                                                           op0=ALU.mult, op1=ALU.mult), reads=[x, r, gbc], writes=[h])
            dst = O.y_prompt[i * 128:(i + 1) * 128, :] if i < 16 else O.y_sample[:, :]
            kb.dma(kb.pool, dst, h[:], reads=[h])


import math
LOG_SCALE = -math.exp(-0.5)
SEGS = [(k * 512, 512, 64, 8, False) for k in range(4)] + [(TP, TSM, 8, NS, True)]


def pvec(g, dst_ap, src, T_dst):
    g.kb.dma(g.kb.sp, dst_ap, src.rearrange("(c p) o -> p (c o)", p=128), writes=[T_dst],
             allow_slow_non_contiguous=True)


def phase_rwkv(g):
    nc, kb, I, O, S = g.nc, g.kb, g.I, g.O, g.S
    dve, act, pe, pool, sp = kb.dve, kb.act, kb.pe, kb.pool, kb.sp
    cst = g.cst
    with ExitStack() as es:
        al = lambda name, shape, dt=F32: T(es.enter_context(nc.sbuf_tensor(uname(name), shape, dt)))
        praw = al("ra_p", [128, TTX])
        dtm = al("ra_d", [128, TT])
        xw = al("ra_xw", [128, TT])
        xa = al("ra_xa", [128, TT])
        xg = al("ra_xg", [128, 2, TT])
        mus = al("ra_mu", [128, 4])
        w2 = al("ra_w2", [96, 1536])
        a2 = al("ra_a2", [96, 1536])
        g2 = al("ra_g2", [128, 2, 1536])
        w0 = al("ra_w0", [128, 12])
        a0 = al("ra_a0", [128, 12])
        stg = [al(f"ra_stg{i}", [128, 512]) for i in range(4)]
        kb.dma(sp, w2[:], I.a_w2[:, :], writes=[w2])
        kb.dma(sp, a2[:], I.a_a2[:, :], writes=[a2])
        kb.dma(sp, g2[:], I.a_g2.rearrange("(kc p) n -> p kc n", p=128), writes=[g2])
        pvec(g, w0[:], I.a_w0, w0)
        pvec(g, a0[:], I.a_a0, a0)
        kb.dma(sp, mus[0:96, 0:1], I.a_mu[4608:4704, :], writes=[mus])
        kb.dma(sp, mus[0:96, 1:2], I.a_mu[4704:4800, :], writes=[mus])
        kb.dma(sp, mus[:, 2:3], I.a_mu[4800:4928, :], writes=[mus])
        kb.dma(sp, mus[:, 3:4], I.a_mu[4928:5056, :], writes=[mus])

        def tshift_full(row0, P, mucol, out_ap, out_T, func):
            kb.dma(sp, praw[0:P, :], S.pT[row0:row0 + P, 0:TTX], writes=[praw])
            p = praw
            kb.op(dve, lambda e: e.tensor_tensor(out=dtm[0:P, 1:TP], in0=p[0:P, 0:TP - 1], in1=p[0:P, 1:TP],
                                                 op=ALU.subtract), reads=[p], writes=[dtm])
            kb.op(dve, lambda e: e.tensor_scalar(out=dtm[0:P, 0:1], in0=p[0:P, 0:1], scalar1=-1.0, scalar2=None,
                                                 op0=ALU.mult), reads=[p], writes=[dtm])
            p3 = p[0:P, TP:TT].rearrange("p (s t) -> p s t", t=TS)
            d3 = dtm[0:P, TP:TT].rearrange("p (s t) -> p s t", t=TS)
            kb.op(dve, lambda e: e.tensor_tensor(out=d3[:, :, 1:TS], in0=p3[:, :, 0:TS - 1], in1=p3[:, :, 1:TS],
                                                 op=ALU.subtract), reads=[p], writes=[dtm])
            kb.op(dve, lambda e: e.tensor_tensor(out=d3[:, :, 0:1],
                                                 in0=p[0:P, TT:TTX].rearrange("p (s o) -> p s o", o=1),
                                                 in1=p3[:, :, 0:1], op=ALU.subtract), reads=[p], writes=[dtm])
            kb.op(dve, lambda e: e.scalar_tensor_tensor(out=out_ap, in0=dtm[0:P, 0:TT], scalar=mus[0:P, mucol:mucol + 1],
                                                        in1=p[0:P, 0:TT], op0=ALU.mult, op1=ALU.add),
                  reads=[dtm, mus, p], writes=[out_T])
            if func is not None:
                kb.op(act, lambda e: e.activation(out=out_ap, in_=out_ap, func=func), reads=[out_T], writes=[out_T])

        tshift_full(5120, 96, 0, xw[0:96, :], xw, AF.Tanh)
        tshift_full(5216, 96, 1, xa[0:96, :], xa, None)
        tshift_full(5312, 128, 2, xg[:, 0, :], xg, AF.Sigmoid)
        tshift_full(5440, 128, 3, xg[:, 1, :], xg, AF.Sigmoid)
        k = 0
        for c12 in range(12):
            cs = slice(c12 * 128, (c12 + 1) * 128)
            for (t0, n) in tok_tiles_of(TT):
                ps = next_ps(g)
                kb.op(pe, lambda e: e.matmul(ps[:, 0:n], lhsT=w2[0:96, cs], rhs=xw[0:96, t0:t0 + n], start=True, stop=True),
                      reads=[w2, xw], writes=[ps])
                s_ = stg[k % 4]; k += 1
                kb.op(act, lambda e: e.activation(out=s_[:, 0:n], in_=ps[:, 0:n], func=AF.Sigmoid,
                                                  bias=w0[:, c12:c12 + 1], scale=1.0), reads=[ps, w0], writes=[s_])
                kb.op(dve, lambda e: e.tensor_scalar(out=s_[:, 0:n], in0=s_[:, 0:n], scalar1=LOG_SCALE, scalar2=None,
                                                     op0=ALU.mult), reads=[s_], writes=[s_])
                kb.dma(pool, S.auxT[0, cs, t0:t0 + n], s_[:, 0:n], reads=[s_])
                ps = next_ps(g)
                kb.op(pe, lambda e: e.matmul(ps[:, 0:n], lhsT=a2[0:96, cs], rhs=xa[0:96, t0:t0 + n], start=True, stop=True),
                      reads=[a2, xa], writes=[ps])
                s_ = stg[k % 4]; k += 1
                kb.op(act, lambda e: e.activation(out=s_[:, 0:n], in_=ps[:, 0:n], func=AF.Sigmoid,
                                                  bias=a0[:, c12:c12 + 1], scale=1.0), reads=[ps, a0], writes=[s_])
                kb.dma(pool, S.auxT[1, cs, t0:t0 + n], s_[:, 0:n], reads=[s_])
                ps = next_ps(g)
                for kc in range(2):
                    kb.op(pe, lambda e, kc=kc: e.matmul(ps[:, 0:n], lhsT=g2[:, kc, cs], rhs=xg[:, kc, t0:t0 + n],
                                                        start=(kc == 0), stop=(kc == 1)), reads=[g2, xg], writes=[ps])
                s_ = stg[k % 4]; k += 1
                kb.op(dve, lambda e: e.tensor_copy(out=s_[:, 0:n], in_=ps[:, 0:n]), reads=[ps], writes=[s_])
                kb.dma(pool, S.auxT[2, cs, t0:t0 + n], s_[:, 0:n], reads=[s_])
    kb.barrier()
    rwkv_scan(g)


def run_interleaved(a, b):
    gens = [x for x in (b, a) if x is not None]
    while gens:
        for x in list(gens):
            try:
                next(x)
            except StopIteration:
                gens.remove(x)


RSEGS = [(k * 256, 256, 64, 4, False) for k in range(8)] + [(TP, TSM, 8, NS, True)]


def rwkv_scan(g):
    nc, kb, I, O, S = g.nc, g.kb, g.I, g.O, g.S
    dve, act, pe, pool, sp = kb.dve, kb.act, kb.pe, kb.pool, kb.sp
    cst = g.cst
    g.ps_n = 7
    psO = g.ps[7]
    Rb = lambda ap: ap
    with ExitStack() as es:
        al = lambda name, shape, dt=F32: T(es.enter_context(nc.sbuf_tensor(uname(name), shape, dt)))
        vec = al("rw_vec", [128, 12, 8])
        for idx, src in enumerate([I.a_mu[0:1536, :], I.a_mu[1536:3072, :], I.a_mu[3072:4608, :], I.a_k_k, I.a_k_a,
                                   I.a_r_k, I.a_ln_w, I.a_ln_b]):
            pvec(g, vec[:, :, idx], src, vec)
        omk = al("rw_omk", [128, 12])
        kb.op(dve, lambda e: e.tensor_scalar(out=omk[:], in0=vec[:, :, 4], scalar1=-1.0, scalar2=1.0,
                                             op0=ALU.mult, op1=ALU.add), reads=[vec], writes=[omk])
        cr = al("rw_cr", [128, 3, 128], BF16)
        idb = al("rw_idb", [128, 128], BF16)
        kb.op(dve, lambda e: e.tensor_copy(out=idb[:], in_=cst[:, C_ID, :]), reads=[cst], writes=[idb])
        for i, cslot in enumerate((C_BLK, C_SEL64, C_SEL8)):
            kb.op(dve, lambda e: e.tensor_copy(out=Rb(cr[:, i, :]), in_=cst[:, cslot, :]), reads=[cst], writes=[cr])
        raw = [al(f"rw_raw{i}", [128, 288]) for i in range(3)]
        names = ["a", "ld", "gt", "d", "r", "k", "v", "kk", "b", "L", "Lm", "LCmL", "bon", "tmp", "eL", "enL", "eLm",
                 "eLC", "ynT", "o1", "tmpr", "d2", "d3", "kp", "ta", "tmpb"]
        Bsets = [{nm: al(f"rw_{nm}{q}", [128, 256], BF16 if nm in ("tmpr", "tmpb") else F32) for nm in names} for q in range(2)]
        ocat = [al(f"rw_ocat{q}", [128, 256], BF16) for q in range(2)]
        LCs = [al(f"rw_LC{q}", [128, 16]) for q in range(2)]
        GCs = [al(f"rw_GC{q}", [128, 16]) for q in range(2)]
        padn = ["Rt", "Kh", "Bh", "KK", "Kb", "Bb", "Vp", "KKr", "Rtr"]
        pdt = lambda nm: F32 if nm in ("KKr", "Rtr") else BF16
        padP = [{nm: al(f"rw_pp{nm}{q}", [128, 4, 128], pdt(nm)) for nm in padn} for q in range(2)]
        padS = [{nm: al(f"rw_ps{nm}{q}", [128, 24, 16], pdt(nm)) for nm in padn} for q in range(2)]
        zt = al("rw_zero", [128, 512])
        kb.op(pool, lambda e: e.memset(zt[:], 0.0), writes=[zt])
        for q in range(2):
            for nm in padn:
                kb.op(pool, lambda e: e.tensor_copy(out=(R if nm in ("KKr", "Rtr") else Rb)(padP[q][nm][:]), in_=zt[:, 0:512].rearrange("p (a b) -> p a b", b=128)),
                      reads=[zt], writes=[padP[q][nm]])
                kb.op(pool, lambda e: e.tensor_copy(out=(R if nm in ("KKr", "Rtr") else Rb)(padS[q][nm][:]), in_=zt[:, 0:384].rearrange("p (a b) -> p a b", b=16)),
                      reads=[zt], writes=[padS[q][nm]])
        un = ["A", "Bm", "AkkT", "ArkT", "ArbT", "Vbd", "Kbd", "Bbd", "AkkV", "P0", "P1", "Aa", "Ab", "Ba", "Bb2", "KKbd",
              "WT", "nU0"]
        UB = [{nm: al(f"rw_u{q}{nm}", [128, 4, 128], F32 if nm in ("WT", "nU0") else BF16) for nm in un} for q in range(2)]
        for q in range(2):
            for nm in un:
                if nm != "nU0":
                    kb.op(pool, lambda e: e.tensor_copy(out=(R if nm == "WT" else Rb)(UB[q][nm][:]), in_=zt[:, 0:512].rearrange("p (a b) -> p a b", b=128)),
                          reads=[zt], writes=[UB[q][nm]])
        X = [al(f"rw_X{i}", [128, 128], BF16) for i in range(2)]
        nU = [al(f"rw_nU{i}", [128, 128], BF16) for i in range(2)]
        Osb = al("rw_O", [128, 4, 128])
        yn = al("rw_yn", [128, 4, 128])
        ynr = al("rw_ynr", [128, 4, 128], BF16)
        junk = al("rw_junk", [128, 4, 128])
        Hs = [al(f"rw_H{i}", [128, 128]) for i in range(3)]
        Hpad = al("rw_Hpad", [128, 128])
        Ht = al("rw_Ht", [128, 128])
        sin1 = al("rw_sin", [128, NS, 64])
        sout1 = al("rw_sout", [128, NS, 64])
        sin = [sin1, sin1]
        sout = [sout1, sout1]
        st = [al(f"rw_st{k}", [128, 4]) for k in range(5)]
        kb.op(pool, lambda e: e.memset(Hpad[:], 0.0), writes=[Hpad])
        hst = {"H": None, "hi": 0, "sc": 0}

        def newH():
            h = Hs[hst["hi"] % 3]
            hst["hi"] += 1
            return h

        def genA(item):
            pr, gs, (t0, ntok, C, nch, is_s), bi, first, last, k = item[:7]
            n = 2 * C
            SL, SU, IU = (C_SL64, C_SU64, C_IU64) if C == 64 else (C_SL8, C_SU8, C_IU8)
            levels = 5 if C == 64 else 2
            Bq = Bsets[gs % 2]
            pad = (padS if is_s else padP)[gs % 2]
            LC, GC = LCs[gs % 2], GCs[gs % 2]
            U = UB[k % 2]
            rs_ = slice(pr * 128, (pr + 1) * 128)
            W = slice(0, ntok)
            if first:
                if is_s:
                    kb.dma(sp, sin[pr % 2][:], I.state_rwkv[:, 2 * pr:2 * pr + 2].rearrange("s h v k -> (h v) s k"),
                           writes=[sin[pr % 2]])
                r, k_, v, kk, b, a, ld, L, Lm, LCmL, bon, tmp, tmpr = (Bq[x] for x in ("r", "k", "v", "kk", "b", "a", "ld", "L",
                                                                                     "Lm", "LCmL", "bon", "tmp", "tmpr"))
                kp, ta = Bq["kp"], Bq["ta"]
                dd = [Bq["d"], Bq["d2"], Bq["d3"]]
                for xi, (nm, row0) in enumerate((("r", 512), ("k", 2048), ("v", 3584))):
                    rw = raw[xi]
                    rows = slice(row0 + pr * 128, row0 + (pr + 1) * 128)
                    if not is_s:
                        if t0 == 0:
                            kb.op(pool, lambda e: e.memset(rw[:, 0:1], 0.0), writes=[rw])
                            kb.dma(sp, rw[:, 1:ntok + 1], S.pT[rows, 0:ntok], writes=[rw])
                        else:
                            kb.dma(sp, rw[:, 0:ntok + 1], S.pT[rows, t0 - 1:t0 + ntok], writes=[rw])
                    else:
                        kb.dma(sp, rw[:, 0:TSM + NS], S.pT[rows, TP:TTX], writes=[rw])
                for ai, nm in ((1, "a"), (0, "ld"), (2, "gt")):
                    kb.dma(sp, Bq[nm][:, W], S.auxT[ai, rs_, t0:t0 + ntok], writes=[Bq[nm]])
                yield
                for xi in range(3):
                    rw, d = raw[xi], dd[xi]
                    if not is_s:
                        kb.op(dve, lambda e: e.tensor_tensor(out=d[:, W], in0=rw[:, 0:ntok], in1=rw[:, 1:ntok + 1],
                                                             op=ALU.subtract), reads=[rw], writes=[d])
                    else:
                        p3 = rw[:, 0:TSM].rearrange("p (s t) -> p s t", t=TS)
                        d3 = d[:, 0:TSM].rearrange("p (s t) -> p s t", t=TS)
                        kb.op(dve, lambda e: e.tensor_tensor(out=d3[:, :, 1:TS], in0=p3[:, :, 0:TS - 1],
                                                             in1=p3[:, :, 1:TS], op=ALU.subtract), reads=[rw], writes=[d])
                        kb.op(pool, lambda e: e.tensor_tensor(out=d3[:, :, 0:1],
                                                              in0=rw[:, TSM:TSM + NS].rearrange("p (s o) -> p s o", o=1),
                                                              in1=p3[:, :, 0:1], op=ALU.subtract), reads=[rw], writes=[d])
                yield
                for c in range(nch):
                    cs = slice(c * C, (c + 1) * C)
                    kb.op(dve, lambda e: e.tensor_tensor_scan(out=L[:, cs], data0=cst[:, C_ONES, 0:C], data1=ld[:, cs],
                                                              initial=0.0, op0=ALU.mult, op1=ALU.add),
                          reads=[cst, ld], writes=[L])
                    if c % 4 == 3:
                        yield
                kb.op(pool, lambda e: e.tensor_scalar(out=ta[:, W], in0=a[:, W], scalar1=vec[:, pr, 4:5],
                                                      scalar2=omk[:, pr:pr + 1], op0=ALU.mult, op1=ALU.add),
                      reads=[a, vec, omk], writes=[ta])
                for xi, nm in enumerate(("r", "k", "v")):
                    rw, d, dst = raw[xi], dd[xi], Bq[nm]
                    src = rw[:, 1:ntok + 1] if not is_s else rw[:, 0:TSM]
                    kb.op(dve, lambda e: e.scalar_tensor_tensor(out=dst[:, W], in0=d[:, W], scalar=vec[:, pr, xi:xi + 1],
                                                                in1=src, op0=ALU.mult, op1=ALU.add),
                          reads=[d, vec, rw], writes=[dst])
                yield
                L3 = L[:, W].rearrange("p (c t) -> p c t", t=C)
                kb.op(dve, lambda e: e.tensor_copy(out=LC[:, 0:nch], in_=L3[:, :, C - 1]), reads=[L], writes=[LC])
                kb.op(pool, lambda e: e.tensor_tensor(out=Lm[:, W], in0=L[:, W], in1=ld[:, W], op=ALU.subtract),
                      reads=[L, ld], writes=[Lm])
                kb.op(act, lambda e: e.activation(out=Bq["eL"][:, W], in_=L[:, W], func=AF.Exp), reads=[L], writes=[Bq["eL"]])
                kb.op(act, lambda e: e.activation(out=Bq["enL"][:, W], in_=L[:, W], func=AF.Exp, scale=-1.0),
                      reads=[L], writes=[Bq["enL"]])
                yield
                kb.op(dve, lambda e: e.tensor_scalar(out=kk[:, W], in0=k_[:, W], scalar1=vec[:, pr, 3:4], scalar2=None,
                                                     op0=ALU.mult), reads=[k_, vec], writes=[kk])
                kb.op(dve, lambda e: e.tensor_tensor(out=LCmL[:, W].rearrange("p (c t) -> p c t", t=C),
                                                     in0=LC[:, 0:nch].unsqueeze(2).to_broadcast([128, nch, C]),
                                                     in1=L3, op=ALU.subtract), reads=[L, LC], writes=[LCmL])
                kb.op(act, lambda e: e.activation(out=GC[:, 0:nch], in_=LC[:, 0:nch], func=AF.Exp), reads=[LC], writes=[GC])
                kb.op(act, lambda e: e.activation(out=Bq["eLm"][:, W], in_=Lm[:, W], func=AF.Exp), reads=[Lm], writes=[Bq["eLm"]])
                yield
                kb.op(dve, lambda e: e.tensor_tensor(out=tmpr[:, W], in0=kk[:, W], in1=kk[:, W], op=ALU.mult),
                      reads=[kk], writes=[tmpr])
                ps_ss = next_ps(g)
                kb.op(pe, lambda e: e.matmul(ps_ss[:, W], lhsT=cr[:, 0, :], rhs=tmpr[:, W], start=True, stop=True),
                      reads=[cr, tmpr], writes=[ps_ss])
                kb.op(dve, lambda e: e.tensor_tensor(out=kp[:, W], in0=k_[:, W], in1=ta[:, W], op=ALU.mult),
                      reads=[k_, ta], writes=[kp])
                kb.op(act, lambda e: e.activation(out=Bq["eLC"][:, W], in_=LCmL[:, W], func=AF.Exp),
                      reads=[LCmL], writes=[Bq["eLC"]])
                yield

                def padmul(dn, an, bn, eng_pair):
                    for h in range(2):
                        hs = slice(h * 64, (h + 1) * 64)
                        o_ap = (R if dn in ("KKr", "Rtr") else Rb)(pad[dn][hs, 0:nch, h * C:(h + 1) * C])
                        i0 = Bq[an][hs, W].rearrange("p (c t) -> p c t", t=C)
                        eng = eng_pair[h]
                        if bn is None:
                            kb.op(eng, lambda e: e.tensor_copy(out=o_ap, in_=i0), reads=[Bq[an]], writes=[pad[dn]])
                        else:
                            i1 = Bq[bn][hs, W].rearrange("p (c t) -> p c t", t=C)
                            kb.op(eng, lambda e: e.tensor_tensor(out=o_ap, in0=i0, in1=i1, op=ALU.mult),
                                  reads=[Bq[an], Bq[bn]], writes=[pad[dn]])
                padmul("Rt", "r", "eL", (dve, pool))
                padmul("Rtr", "r", "eL", (pool, dve))
                yield
                padmul("Vp", "v", None, (pool, pool))
                padmul("Kh", "kp", "enL", (dve, pool))
                yield
                padmul("Kb", "kp", "eLC", (pool, dve))
                tmpb = Bq["tmpb"]
                kb.op(dve, lambda e: e.scalar_tensor_tensor(out=tmpb[:, W], in0=r[:, W], scalar=vec[:, pr, 5:6],
                                                            in1=kp[:, W], op0=ALU.mult, op1=ALU.mult),
                      reads=[r, vec, kp], writes=[tmpb])
                ps_b = next_ps(g)
                kb.op(pe, lambda e: e.matmul(ps_b[:, W], lhsT=cr[:, 0, :], rhs=tmpb[:, W], start=True, stop=True),
                      reads=[cr, tmpb], writes=[ps_b])
                yield
                kb.op(act, lambda e: e.sqrt(out=tmp[:, W], in_=ps_ss[:, W]), reads=[ps_ss], writes=[tmp])
                kb.op(dve, lambda e: e.tensor_tensor(out=bon[:, W], in0=ps_b[:, W], in1=v[:, W], op=ALU.mult),
                      reads=[ps_b, v], writes=[bon])
                kb.op(dve, lambda e: e.tensor_scalar_max(out=tmp[:, W], in0=tmp[:, W], scalar1=1e-12), reads=[tmp], writes=[tmp])
                yield
                kb.op(dve, lambda e: e.reciprocal(out=tmp[:, W], in_=tmp[:, W]), reads=[tmp], writes=[tmp])
                yield
                kb.op(dve, lambda e: e.tensor_tensor(out=kk[:, W], in0=kk[:, W], in1=tmp[:, W], op=ALU.mult),
                      reads=[kk, tmp], writes=[kk])
                yield
                kb.op(dve, lambda e: e.tensor_tensor(out=b[:, W], in0=kk[:, W], in1=a[:, W], op=ALU.mult),
                      reads=[kk, a], writes=[b])
                padmul("KK", "kk", "eLm", (pool, dve))
                yield
                padmul("Bh", "b", "enL", (dve, pool))
                padmul("Bb", "b", "eLC", (pool, dve))
                yield
            units = list(range(bi * 4, bi * 4 + 4))

            def padl(nm, c):
                return pad[nm][:, :, :].rearrange("p c j -> p (c j)")[:, c * n:c * n + 128]

            def mm4(lhs_fn, rhs_fn, wcols, reads):
                ps = next_ps(g)
                for u, c in enumerate(units):
                    kb.op(pe, lambda e: e.matmul(ps[0:128, u * 128:u * 128 + wcols], lhsT=lhs_fn(u, c), rhs=rhs_fn(u, c),
                                                 start=True, stop=True), reads=reads, writes=[ps])
                return ps

            def ps3d(ps, wcols):
                return ps[0:n, 0:512].rearrange("p (u j) -> p u j", j=128)[:, :, 0:wcols]

            for (nm, ln, rn, mk) in (("A", "KK", "Bh", SL), ("Bm", "Bh", "KK", SU), ("AkkT", "Kh", "KK", SU),
                                     ("ArkT", "Kh", "Rt", IU), ("ArbT", "Bh", "Rt", IU)):
                ps = mm4(lambda u, c: Rb(padl(ln, c)), lambda u, c: Rb(pad[rn][:, c, 0:n]), n, [pad[ln], pad[rn]])
                kb.op(dve, lambda e: e.tensor_tensor(out=Rb(U[nm][0:n, :, 0:n]), in0=ps3d(ps, n),
                                                     in1=cst[0:n, mk, 0:n].unsqueeze(1).to_broadcast([n, 4, n]),
                                                     op=ALU.mult), reads=[ps, cst], writes=[U[nm]])
                yield
            for (nm, pn_) in (("Vbd", "Vp"), ("Kbd", "Kb"), ("Bbd", "Bb"), ("KKbd", "KK")):
                ps = next_ps(g)
                psb = ps[:, :].bitcast(BF16)
                for u, c in enumerate(units):
                    kb.op(pe, lambda e: e.transpose(out=psb[0:n, u * 128:(u + 1) * 128], in_=pad[pn_][:, c, 0:n],
                                                    identity=idb[:, :]), reads=[pad[pn_], idb], writes=[ps])
                kb.op(act, lambda e: e.copy(out=U[nm][0:n, :, :],
                                            in_=psb[0:n, 0:512].rearrange("p (u j) -> p u j", j=128)),
                      reads=[ps], writes=[U[nm]])
                yield
            ps = mm4(lambda u, c: Rb(U["AkkT"][0:n, u, 0:128]), lambda u, c: Rb(U["Vbd"][0:n, u, :]), 128, [U["AkkT"], U["Vbd"]])
            kb.op(act, lambda e: e.copy(out=U["AkkV"][0:n, :, :], in_=ps3d(ps, 128)), reads=[ps], writes=[U["AkkV"]])
            kb.op(dve, lambda e: e.tensor_tensor(out=Rb(U["P0"][0:n, :, 0:n]),
                                                 in0=cst[0:n, C_ID, 0:n].unsqueeze(1).to_broadcast([n, 4, n]),
                                                 in1=U["Bm"][0:n, :, 0:n], op=ALU.subtract),
                  reads=[cst, U["Bm"]], writes=[U["P0"]])
            yield
            Aj, Bj, Pj = "A", "Bm", "P0"
            for lev in range(levels):
                An = "Aa" if lev % 2 == 0 else "Ab"
                Bn = "Ba" if lev % 2 == 0 else "Bb2"
                Pn = "P1" if lev % 2 == 0 else "P0"
                ps_a = mm4(lambda u, c: Rb(U[Bj][0:n, u, 0:128]), lambda u, c: Rb(U[Aj][0:n, u, 0:n]), n, [U[Bj], U[Aj]])
                if lev < levels - 1:
                    ps_b = mm4(lambda u, c: Rb(U[Aj][0:n, u, 0:128]), lambda u, c: Rb(U[Bj][0:n, u, 0:n]), n, [U[Bj], U[Aj]])
                kb.op(act, lambda e: e.copy(out=Rb(U[An][0:n, :, 0:n]), in_=ps3d(ps_a, n)), reads=[ps_a], writes=[U[An]])
                if lev < levels - 1:
                    kb.op(dve, lambda e: e.tensor_copy(out=Rb(U[Bn][0:n, :, 0:n]), in_=ps3d(ps_b, n)), reads=[ps_b], writes=[U[Bn]])
                yield
                ps = mm4(lambda u, c: Rb(U[An][0:n, u, 0:128]), lambda u, c: Rb(U[Pj][0:n, u, 0:n]), n, [U[An], U[Pj]])
                kb.op(dve, lambda e: e.tensor_tensor(out=Rb(U[Pn][0:n, :, 0:n]), in0=ps3d(ps, n), in1=U[Pj][0:n, :, 0:n],
                                                     op=ALU.add), reads=[ps, U[Pj]], writes=[U[Pn]])
                yield
                Aj, Bj, Pj = An, Bn, Pn
            item[-1]["P"] = Pj
            ps = mm4(lambda u, c: Rb(U["KKbd"][0:n, u, 0:128]), lambda u, c: Rb(U[Pj][0:n, u, 0:n]), n, [U["KKbd"], U[Pj]])
            kb.op(act, lambda e: e.copy(out=R(U["WT"][:, :, 0:n]),
                                        in_=ps[:, 0:512].rearrange("p (u j) -> p u j", j=128)[:, :, 0:n]),
                  reads=[ps], writes=[U["WT"]])
            yield
            ps = mm4(lambda u, c: Rb(U[Pj][0:n, u, 0:128]), lambda u, c: Rb(U["AkkV"][0:n, u, :]), 128, [U[Pj], U["AkkV"]])
            kb.op(act, lambda e: e.mul(out=U["nU0"][0:n, :, :], in_=ps3d(ps, 128), mul=-1.0), reads=[ps], writes=[U["nU0"]])
            yield

        def genB(item):
            pr, gs, (t0, ntok, C, nch, is_s), bi, first, last, k = item[:7]
            Pj = item[-1]["P"]
            n = 2 * C
            BLKm = C_BLK if C == 64 else C_BLK8
            SELi = 1 if C == 64 else 2
            Bq = Bsets[gs % 2]
            pad = (padS if is_s else padP)[gs % 2]
            GC = GCs[gs % 2]
            U = UB[k % 2]
            rs_ = slice(pr * 128, (pr + 1) * 128)
            W = slice(0, ntok)
            units = list(range(bi * 4, bi * 4 + 4))

            def padl(nm, c):
                return pad[nm][:, :, :].rearrange("p c j -> p (c j)")[:, c * n:c * n + 128]
            ynT = Bq["ynT"]
            if first and (not is_s) and t0 == 0:
                h0 = newH()
                kb.op(dve, lambda e: e.tensor_copy(out=R(h0[:]), in_=zt[:, 0:128]), reads=[zt], writes=[h0])
                hst["H"] = h0
            for u, c in enumerate(units):
                if is_s:
                    for h in range(2):
                        hs = slice(h * 64, (h + 1) * 64)
                        kb.op(dve, lambda e: e.tensor_copy(out=Hpad[hs, hs], in_=sin[pr % 2][hs, c, :]),
                              reads=[sin[pr % 2]], writes=[Hpad])
                    ps = next_ps(g)
                    kb.op(pe, lambda e: e.transpose(out=ps[:, 0:128], in_=Hpad[:, :], identity=cst[:, C_ID, :]),
                          reads=[Hpad, cst], writes=[ps])
                    hn = newH()
                    kb.op(act, lambda e: e.copy(out=R(hn[:, :]), in_=ps[:, 0:128]), reads=[ps], writes=[hn])
                    hst["H"] = hn
                    yield
                Hcur = hst["H"]
                q = hst["sc"] % 2
                hst["sc"] += 1
                Xq, nUq = X[q], nU[q]
                ps4 = next_ps(g)
                kb.op(pe, lambda e: e.matmul(ps4[:, 0:128], lhsT=Rb(U["Kbd"][0:n, u, :]), rhs=Rb(U["Vbd"][0:n, u, :]),
                                             start=True, stop=False), reads=[U["Kbd"], U["Vbd"]], writes=[ps4])
                ps = next_ps(g)
                kb.op(pe, lambda e: e.matmul(ps[0:128, 0:128], lhsT=R(U["WT"][:, u, 0:128]), rhs=R(Hcur[:, :]),
                                             start=True, stop=True), reads=[U["WT"], Hcur], writes=[ps])
                kb.op(dve, lambda e: e.scalar_tensor_tensor(out=Rb(nUq[0:n, :]), in0=ps[0:n, 0:128], scalar=-1.0,
                                                            in1=U["nU0"][0:n, u, :], op0=ALU.mult, op1=ALU.add),
                      reads=[ps, U["nU0"]], writes=[nUq])
                yield
                kb.op(pe, lambda e: e.matmul(ps4[:, 0:128], lhsT=Rb(U["Bbd"][0:n, u, :]), rhs=Rb(nUq[0:n, :]),
                                             start=False, stop=True), reads=[U["Bbd"], nUq], writes=[ps4])
                Hn = newH()
                kb.op(dve, lambda e: e.scalar_tensor_tensor(out=R(Hn[:, :]), in0=Hcur[:, :], scalar=GC[:, c:c + 1],
                                                            in1=ps4[:, 0:128], op0=ALU.mult, op1=ALU.add),
                      reads=[Hcur, GC, ps4], writes=[Hn])
                hst["H"] = Hn
                yield
                oc = slice(u * 128, (u + 1) * 128)
                kb.op(pe, lambda e: e.matmul(psO[0:128, oc], lhsT=R(padl("Rtr", c)), rhs=R(Hcur[:, :]),
                                             start=True, stop=False), reads=[pad["Rtr"], Hcur], writes=[psO])
                kb.op(pe, lambda e: e.matmul(psO[0:128, oc], lhsT=Rb(U["ArkT"][0:n, u, 0:128]), rhs=Rb(U["Vbd"][0:n, u, :]),
                                             start=False, stop=False), reads=[U["ArkT"], U["Vbd"]], writes=[psO])
                kb.op(pe, lambda e: e.matmul(psO[0:128, oc], lhsT=Rb(U["ArbT"][0:n, u, 0:128]), rhs=Rb(nUq[0:n, :]),
                                             start=False, stop=True), reads=[U["ArbT"], nUq], writes=[psO])

                if is_s:
                    ps = next_ps(g)
                    kb.op(pe, lambda e: e.transpose(out=ps[:, 0:128], in_=Hn[:, :], identity=cst[:, C_ID, :]),
                          reads=[Hn, cst], writes=[ps])
                    kb.op(act, lambda e: e.copy(out=Ht[:, :], in_=ps[:, 0:128]), reads=[ps], writes=[Ht])
                    for h in range(2):
                        hs = slice(h * 64, (h + 1) * 64)
                        kb.op(dve, lambda e: e.tensor_copy(out=sout[pr % 2][hs, c, :], in_=Ht[hs, hs]),
                              reads=[Ht], writes=[sout[pr % 2]])
                    yield
            s1, s2, mean, var, msq = st
            kb.op(act, lambda e: e.copy(out=Osb[0:n, :, :], in_=psO[0:n, 0:512].rearrange("p (u j) -> p u j", j=128)),
                  reads=[psO], writes=[Osb])
            kb.op(dve, lambda e: e.tensor_reduce(out=s1[0:n, :], in_=Osb[0:n, :, :], axis=AX.X, op=ALU.add),
                  reads=[Osb], writes=[s1])
            kb.op(act, lambda e: e.activation(out=junk[0:n, :, :], in_=Osb[0:n, :, :], func=AF.Square),
                  reads=[Osb], writes=[junk])
            kb.op(dve, lambda e: e.tensor_reduce(out=s2[0:n, :], in_=junk[0:n, :, :], axis=AX.X, op=ALU.add),
                  reads=[junk], writes=[s2])
            yield
            kb.op(dve, lambda e: e.tensor_scalar(out=mean[0:n, :], in0=s1[0:n, :], scalar1=1.0 / 64, scalar2=None,
                                                 op0=ALU.mult), reads=[s1], writes=[mean])
            kb.op(dve, lambda e: e.tensor_tensor(out=msq[0:n, :], in0=mean[0:n, :], in1=mean[0:n, :], op=ALU.mult),
                  reads=[mean], writes=[msq])
            kb.op(dve, lambda e: e.scalar_tensor_tensor(out=var[0:n, :], in0=s2[0:n, :], scalar=1.0 / 64,
                                                        in1=msq[0:n, :], op0=ALU.mult, op1=ALU.subtract),
                  reads=[s2, msq], writes=[var])
            kb.op(dve, lambda e: e.tensor_scalar_add(out=var[0:n, :], in0=var[0:n, :], scalar1=GN_EPS),
                  reads=[var], writes=[var])
            kb.op(act, lambda e: e.sqrt(out=var[0:n, :], in_=var[0:n, :]), reads=[var], writes=[var])
            kb.op(dve, lambda e: e.reciprocal(out=var[0:n, :], in_=var[0:n, :]), reads=[var], writes=[var])
            yield
            kb.op(dve, lambda e: e.tensor_tensor(out=yn[0:n, :, :], in0=Osb[0:n, :, :],
                                                 in1=mean[0:n, :].unsqueeze(2).to_broadcast([n, 4, 128]), op=ALU.subtract),
                  reads=[Osb, mean], writes=[yn])
            kb.op(dve, lambda e: e.tensor_tensor(out=yn[0:n, :, :], in0=yn[0:n, :, :],
                                                 in1=var[0:n, :].unsqueeze(2).to_broadcast([n, 4, 128]), op=ALU.mult),
                  reads=[yn, var], writes=[yn])
            kb.op(dve, lambda e: e.tensor_tensor(out=Rb(ynr[0:n, :, :]), in0=yn[0:n, :, :],
                                                 in1=cst[0:n, BLKm, :].unsqueeze(1).to_broadcast([n, 4, 128]), op=ALU.mult),
                  reads=[yn, cst], writes=[ynr])
            psF = next_ps(g)
            for u, c in enumerate(units):
                kb.op(pe, lambda e: e.matmul(psF[:, u * C:(u + 1) * C], lhsT=Rb(ynr[0:n, u, :]), rhs=Rb(cr[0:n, SELi, 0:C]),
                                             start=True, stop=True), reads=[ynr, cr], writes=[psF])
            kb.op(act, lambda e: e.copy(out=ynT[:, bi * 4 * C:(bi * 4 + 4) * C], in_=psF[:, 0:4 * C]),
                  reads=[psF], writes=[ynT])
            yield
            if last:
                o1 = Bq["o1"]
                oc_ = ocat[gs % 2]
                kb.op(dve, lambda e: e.tensor_scalar(out=o1[:, W], in0=ynT[:, W], scalar1=vec[:, pr, 6:7],
                                                     scalar2=vec[:, pr, 7:8], op0=ALU.mult, op1=ALU.add),
                      reads=[ynT, vec], writes=[o1])
                kb.op(dve, lambda e: e.tensor_tensor(out=o1[:, W], in0=o1[:, W], in1=Bq["bon"][:, W], op=ALU.add),
                      reads=[o1, Bq["bon"]], writes=[o1])
                kb.op(dve, lambda e: e.tensor_tensor(out=oc_[:, W], in0=o1[:, W], in1=Bq["gt"][:, W], op=ALU.mult),
                      reads=[o1, Bq["gt"]], writes=[oc_])
                kb.dma(pool, S.catT[rs_, t0:t0 + ntok], oc_[:, W], reads=[oc_])
                if (not is_s) and t0 + ntok == TP:
                    Hcur = hst["H"]
                    ps = next_ps(g)
                    kb.op(pe, lambda e: e.transpose(out=ps[:, 0:128], in_=Hcur[:, :], identity=cst[:, C_ID, :]),
                          reads=[Hcur, cst], writes=[ps])
                    kb.op(act, lambda e: e.copy(out=Ht[:, :], in_=ps[:, 0:128]), reads=[ps], writes=[Ht])
                    for h in range(2):
                        hs = slice(h * 64, (h + 1) * 64)
                        kb.dma(pool, O.rwkv_prompt[pr * 128 + h * 64:pr * 128 + (h + 1) * 64, :], Ht[hs, hs], reads=[Ht])
                if is_s:
                    kb.dma(pool, O.rwkv_sample[:, rs_, :].rearrange("s r k -> r s k"), sout[pr % 2][:],
                           reads=[sout[pr % 2]])
                yield

        pending = None
        gs = 0
        k = 0
        for pr in range(12):
            for seg in RSEGS:
                nb = seg[3] // 4
                for bi in range(nb):
                    item = (pr, gs, seg, bi, bi == 0, bi == nb - 1, k, {})
                    run_interleaved(genA(item), pending)
                    pending = genB(item)
                    k += 1
                gs += 1
        run_interleaved(None, pending)
    g.ps_n = 8


HSEGS = [(k * 256, 256, 64, 4, False) for k in range(8)] + [(TP, TSM, 8, NS, True)]


def phase_hgrn(g):
    nc, kb, I, O, S = g.nc, g.kb, g.I, g.O, g.S
    dve, act, pe, pool, sp = kb.dve, kb.act, kb.pe, kb.pool, kb.sp
    cst = g.cst
    g.ps_n = 7
    psO = g.ps[7]
    with ExitStack() as es:
        al = lambda name, shape, dt=F32: T(es.enter_context(nc.sbuf_tensor(uname(name), shape, dt)))
        lbr = al("hg_lbr", [128, 2, 12])
        lb = al("hg_lb", [128, 12])
        oml = al("hg_oml", [128, 12])
        gn = al("hg_gn", [128, 1])
        onesr = al("hg_ones", [128, 128])
        zt = al("hg_zero", [128, 128])
        kb.op(pool, lambda e: e.memset(zt[:], 0.0), writes=[zt])
        kb.op(dve, lambda e: e.tensor_copy(out=R(onesr[:]), in_=cst[:, C_ONES, :]), reads=[cst], writes=[onesr])
        kb.dma(sp, lbr[:], I.b_lb.rearrange("l (c p) -> p l c", p=128), writes=[lbr], allow_slow_non_contiguous=True)
        kb.dma(sp, gn[:], I.b_g_norm[:, :], writes=[gn])
        kb.op(dve, lambda e: e.tensor_tensor(out=lb[:], in0=lbr[:, 1, :], in1=lbr[:, 0, :], op=ALU.subtract),
              reads=[lbr], writes=[lb])
        kb.op(act, lambda e: e.activation(out=lb[:], in_=lb[:], func=AF.Sigmoid), reads=[lb], writes=[lb])
        kb.op(dve, lambda e: e.tensor_scalar(out=oml[:], in0=lb[:], scalar1=-1.0, scalar2=1.0, op0=ALU.mult, op1=ALU.add),
              reads=[lb], writes=[oml])
        names = ["q", "f", "i", "og", "sq", "kk", "lf", "L", "LmM", "LCmL", "e1", "e2", "Qt", "Kt", "Qh", "Kb", "oseg",
                 "t1", "t1r"]
        Bs = [{nm: al(f"hg_{nm}{i}", [128, 384 if nm == "Kt" else 256], BF16 if nm in ("Kt", "Qt") else F32) for nm in names} for i in range(2)]
        for i in range(2):
            for j3 in range(3):
                kb.op(dve, lambda e: e.tensor_copy(out=Bs[i]["Kt"][:, j3 * 128:(j3 + 1) * 128], in_=zt[:]),
                      reads=[zt], writes=[Bs[i]["Kt"]])
        ocat = [al(f"hg_ocat{i}", [128, 256], BF16) for i in range(2)]
        LCs = [al(f"hg_LC{i}", [128, 16]) for i in range(2)]
        MDs = [al(f"hg_MD{i}", [128, 16]) for i in range(2)]
        GCs = [al(f"hg_GC{i}", [128, 16]) for i in range(2)]
        attT = [al(f"hg_att{i}", [64, 4, 64], BF16) for i in range(2)]
        VK = [al(f"hg_vk{i}", [64, 8, 128], BF16) for i in range(2)]
        Ss = [al(f"hg_S{i}", [128, 128]) for i in range(3)]
        sin = [al(f"hg_sin{i}", [128, NS, 128]) for i in range(2)]
        sout = [al(f"hg_sout{i}", [128, NS, 128]) for i in range(2)]
        hst = {"S": None, "si": 0}

        def newS():
            t_ = Ss[hst["si"] % 3]
            hst["si"] += 1
            return t_

        def genA(item):
            hd, gs, (t0, ntok, C, nch, is_s), bi, first, last, k = item[:7]
            W = slice(0, ntok)
            HIU = C_HIU64 if C == 64 else C_HIU8
            mid = (C - 1) // 2
            Bq = Bs[gs % 2]
            LC, MD, GC = LCs[gs % 2], MDs[gs % 2], GCs[gs % 2]
            q, f, iv, og, sq, kk, lf, L, LmM, LCmL, e1, e2, Qt, Kt, Qh, Kb, oseg, t1, t1r = (Bq[x] for x in names)
            if first:
                if is_s:
                    kb.dma(sp, sin[hd % 2][:], I.state_hgrn[:, hd].rearrange("s k v -> k s v"), writes=[sin[hd % 2]])
                for nm, row0 in (("q", 512), ("f", 2048), ("i", 3584), ("og", 5120)):
                    kb.dma(sp, Bq[nm][:, W], S.pT[row0 + hd * 128:row0 + (hd + 1) * 128, t0:t0 + ntok], writes=[Bq[nm]])
                kb.op(act, lambda e: e.activation(out=sq[:, W], in_=q[:, W], func=AF.Silu), reads=[q], writes=[sq])
                kb.op(act, lambda e: e.activation(out=og[:, W], in_=og[:, W], func=AF.Silu), reads=[og], writes=[og])
                kb.op(act, lambda e: e.activation(out=f[:, W], in_=f[:, W], func=AF.Sigmoid), reads=[f], writes=[f])
                yield
                kb.op(dve, lambda e: e.tensor_scalar(out=f[:, W], in0=f[:, W], scalar1=oml[:, hd:hd + 1],
                                                     scalar2=lb[:, hd:hd + 1], op0=ALU.mult, op1=ALU.add),
                      reads=[f, oml, lb], writes=[f])
                kb.op(dve, lambda e: e.tensor_scalar(out=kk[:, W], in0=f[:, W], scalar1=-1.0, scalar2=1.0,
                                                     op0=ALU.mult, op1=ALU.add), reads=[f], writes=[kk])
                kb.op(act, lambda e: e.activation(out=lf[:, W], in_=f[:, W], func=AF.Ln), reads=[f], writes=[lf])
                yield
                for c in range(nch):
                    cs = slice(c * C, (c + 1) * C)
                    kb.op(dve, lambda e: e.tensor_tensor_scan(out=L[:, cs], data0=cst[:, C_ONES, 0:C], data1=lf[:, cs],
                                                              initial=0.0, op0=ALU.mult, op1=ALU.add),
                          reads=[cst, lf], writes=[L])
                    if c % 4 == 3:
                        yield
                L3 = L[:, W].rearrange("p (c t) -> p c t", t=C)
                kb.op(dve, lambda e: e.tensor_copy(out=LC[:, 0:nch], in_=L3[:, :, C - 1]), reads=[L], writes=[LC])
                kb.op(dve, lambda e: e.tensor_copy(out=MD[:, 0:nch], in_=L3[:, :, mid]), reads=[L], writes=[MD])
                kb.op(dve, lambda e: e.tensor_tensor(out=LmM[:, W].rearrange("p (c t) -> p c t", t=C), in0=L3,
                                                     in1=MD[:, 0:nch].unsqueeze(2).to_broadcast([128, nch, C]),
                                                     op=ALU.subtract), reads=[L, MD], writes=[LmM])
                kb.op(dve, lambda e: e.tensor_tensor(out=LCmL[:, W].rearrange("p (c t) -> p c t", t=C),
                                                     in0=LC[:, 0:nch].unsqueeze(2).to_broadcast([128, nch, C]),
                                                     in1=L3, op=ALU.subtract), reads=[L, LC], writes=[LCmL])
                yield
                kb.op(act, lambda e: e.activation(out=GC[:, 0:nch], in_=LC[:, 0:nch], func=AF.Exp), reads=[LC], writes=[GC])
                kb.op(act, lambda e: e.activation(out=e1[:, W], in_=LmM[:, W], func=AF.Exp), reads=[LmM], writes=[e1])
                kb.op(dve, lambda e: e.tensor_tensor(out=Qt[:, W], in0=sq[:, W], in1=e1[:, W], op=ALU.mult),
                      reads=[sq, e1], writes=[Qt])
                kb.op(act, lambda e: e.activation(out=e2[:, W], in_=LmM[:, W], func=AF.Exp, scale=-1.0), reads=[LmM], writes=[e2])
                kb.op(dve, lambda e: e.tensor_tensor(out=Kt[:, W], in0=kk[:, W], in1=e2[:, W], op=ALU.mult),
                      reads=[kk, e2], writes=[Kt])
                yield
                kb.op(act, lambda e: e.activation(out=e1[:, W], in_=L[:, W], func=AF.Exp), reads=[L], writes=[e1])
                kb.op(dve, lambda e: e.tensor_tensor(out=R(Qh[:, W]), in0=sq[:, W], in1=e1[:, W], op=ALU.mult),
                      reads=[sq, e1], writes=[Qh])
                kb.op(act, lambda e: e.activation(out=e2[:, W], in_=LCmL[:, W], func=AF.Exp), reads=[LCmL], writes=[e2])
                kb.op(dve, lambda e: e.tensor_tensor(out=Kb[:, W], in0=kk[:, W], in1=e2[:, W], op=ALU.mult),
                      reads=[kk, e2], writes=[Kb])
                yield
            units = list(range(bi * 4, bi * 4 + 4))
            at, vk = attT[k % 2], VK[k % 2]
            ps = next_ps(g)
            for u, c in enumerate(units):
                cs = slice(c * C, (c + 1) * C)
                kb.op(pe, lambda e: e.matmul(ps[0:128, u * 64:u * 64 + C], lhsT=Kt[:, c * C:c * C + 128], rhs=Qt[:, cs],
                                             start=True, stop=True), reads=[Kt, Qt], writes=[ps])
            kb.op(dve, lambda e: e.tensor_tensor(out=at[0:C, :, 0:C],
                                                 in0=ps[0:C, 0:256].rearrange("p (u j) -> p u j", j=64)[:, :, 0:C],
                                                 in1=cst[0:C, HIU, 0:C].unsqueeze(1).to_broadcast([C, 4, C]), op=ALU.mult),
                  reads=[ps, cst], writes=[at])
            yield
            for half in range(2):
                ps = next_ps(g)
                for uu in range(2):
                    u = half * 2 + uu
                    cs = slice(units[u] * C, (units[u] + 1) * C)
                    kb.op(pe, lambda e: e.transpose(out=ps[0:C, (uu * 2) * 128:(uu * 2 + 1) * 128], in_=iv[:, cs],
                                                    identity=cst[:, C_ID, :]), reads=[iv, cst], writes=[ps])
                    kb.op(pe, lambda e: e.transpose(out=ps[0:C, (uu * 2 + 1) * 128:(uu * 2 + 2) * 128], in_=Kb[:, cs],
                                                    identity=cst[:, C_ID, :]), reads=[Kb, cst], writes=[ps])
                kb.op(act, lambda e: e.copy(out=vk[0:C, half * 4:half * 4 + 4, :],
                                            in_=ps[0:C, 0:512].rearrange("p (u j) -> p u j", j=128)),
                      reads=[ps], writes=[vk])
                yield

        def genB(item):
            hd, gs, (t0, ntok, C, nch, is_s), bi, first, last, k = item[:7]
            W = slice(0, ntok)
            rs_ = slice(hd * 128, (hd + 1) * 128)
            Bq = Bs[gs % 2]
            GC = GCs[gs % 2]
            Qh, oseg, t1, t1r, og = Bq["Qh"], Bq["oseg"], Bq["t1"], Bq["t1r"], Bq["og"]
            units = list(range(bi * 4, bi * 4 + 4))
            at, vk = attT[k % 2], VK[k % 2]
            if first and (not is_s) and t0 == 0:
                s0 = newS()
                kb.op(dve, lambda e: e.tensor_copy(out=R(s0[:]), in_=zt[:]), reads=[zt], writes=[s0])
                hst["S"] = s0
            for u, c in enumerate(units):
                cs = slice(c * C, (c + 1) * C)
                if is_s:
                    sn = newS()
                    kb.op(dve, lambda e: e.tensor_copy(out=R(sn[:, :]), in_=sin[hd % 2][:, c, :]),
                          reads=[sin[hd % 2]], writes=[sn])
                    hst["S"] = sn
                Scur = hst["S"]
                kb.op(pe, lambda e: e.matmul(psO[:, u * C:(u + 1) * C], lhsT=vk[0:C, u * 2, :], rhs=at[0:C, u, 0:C],
                                             start=True, stop=False), reads=[vk, at], writes=[psO])
                kb.op(pe, lambda e: e.matmul(psO[:, u * C:(u + 1) * C], lhsT=R(Scur[:, :]), rhs=R(Qh[:, cs]),
                                             start=False, stop=True), reads=[Scur, Qh], writes=[psO])
                ps = next_ps(g)
                kb.op(pe, lambda e: e.matmul(ps[:, 0:128], lhsT=vk[0:C, u * 2 + 1, :], rhs=vk[0:C, u * 2, :],
                                             start=True, stop=True), reads=[vk], writes=[ps])
                if is_s:
                    kb.op(dve, lambda e: e.scalar_tensor_tensor(out=sout[hd % 2][:, c, :], in0=Scur[:, :],
                                                                scalar=GC[:, c:c + 1], in1=ps[:, 0:128],
                                                                op0=ALU.mult, op1=ALU.add),
                          reads=[Scur, GC, ps], writes=[sout[hd % 2]])
                else:
                    Sn = newS()
                    kb.op(dve, lambda e: e.scalar_tensor_tensor(out=R(Sn[:, :]), in0=Scur[:, :], scalar=GC[:, c:c + 1],
                                                                in1=ps[:, 0:128], op0=ALU.mult, op1=ALU.add),
                          reads=[Scur, GC, ps], writes=[Sn])
                    hst["S"] = Sn
                yield
            kb.op(act, lambda e: e.copy(out=oseg[:, bi * 4 * C:(bi * 4 + 4) * C], in_=psO[:, 0:4 * C]),
                  reads=[psO], writes=[oseg])
            yield
            if last:
                kb.op(dve, lambda e: e.tensor_tensor(out=R(t1r[:, W]), in0=oseg[:, W], in1=oseg[:, W], op=ALU.mult),
                      reads=[oseg], writes=[t1r])
                ps = next_ps(g)
                kb.op(pe, lambda e: e.matmul(ps[:, W], lhsT=R(onesr[:, :]), rhs=R(t1r[:, W]), start=True, stop=True),
                      reads=[onesr, t1r], writes=[ps])
                kb.op(dve, lambda e: e.tensor_scalar(out=t1[:, W], in0=ps[:, W], scalar1=1.0 / 128, scalar2=EPS,
                                                     op0=ALU.mult, op1=ALU.add), reads=[ps], writes=[t1])
                yield
                kb.op(act, lambda e: e.sqrt(out=t1[:, W], in_=t1[:, W]), reads=[t1], writes=[t1])
                kb.op(dve, lambda e: e.reciprocal(out=t1[:, W], in_=t1[:, W]), reads=[t1], writes=[t1])
                kb.op(dve, lambda e: e.scalar_tensor_tensor(out=t1[:, W], in0=oseg[:, W], scalar=gn[:, 0:1], in1=t1[:, W],
                                                            op0=ALU.mult, op1=ALU.mult), reads=[oseg, gn, t1], writes=[t1])
                oc_ = ocat[gs % 2]
                kb.op(dve, lambda e: e.tensor_tensor(out=oc_[:, W], in0=t1[:, W], in1=og[:, W], op=ALU.mult),
                      reads=[t1, og], writes=[oc_])
                kb.dma(pool, S.catT[rs_, t0:t0 + ntok], oc_[:, W], reads=[oc_])
                if (not is_s) and t0 + ntok == TP:
                    kb.dma(pool, O.hgrn_prompt[hd], hst["S"][:, :], reads=[hst["S"]])
                if is_s:
                    kb.dma(pool, O.hgrn_sample[:, hd].rearrange("s k v -> k s v"), sout[hd % 2][:], reads=[sout[hd % 2]])
                yield

        pending = None
        gs = 0
        k = 0
        for hd in range(12):
            for seg in HSEGS:
                nb = seg[3] // 4
                for bi in range(nb):
                    item = (hd, gs, seg, bi, bi == 0, bi == nb - 1, k, {})
                    run_interleaved(genA(item), pending)
                    pending = genB(item)
                    k += 1
                gs += 1
        run_interleaved(None, pending)
    g.ps_n = 8


_CACHE = {}


def _prep_inputs(inp, c):
    b = c % 4
    s0, s1 = NS * c, NS * (c + 1)
    f = lambda a: np.ascontiguousarray(a, dtype=np.float32)
    m = {
        "x_prompt": f(inp["x_prompt"][b]),
        "x_sample": f(inp["x_sample"][s0:s1].reshape(TSM, D)),
        "mem_prompt": f(inp["mem_prompt"][b]),
        "cache_mem_k": f(inp["cache_mem_k"][:, s0:s1].reshape(2, NS, NMEM, 512)),
        "cache_mem_v": f(inp["cache_mem_v"][:, s0:s1].reshape(2, NS, NMEM, 512)),
        "state_rwkv": f(inp["state_rwkv"][0, s0:s1]),
        "state_shift": f(inp["state_shift"][0, s0:s1]),
        "state_hgrn": f(inp["state_hgrn"][0, s0:s1]),
        "state_conv": f(inp["state_conv"][:, s0:s1].reshape(2, NS * 2, DFF)),
        "norm_mix": f(inp["norm_mix"]), "norm_ffn": f(inp["norm_ffn"]),
        "norm_final": f(inp["norm_final"].reshape(1, D)), "mem_norm": f(inp["mem_norm"]),
        "w_mem_kv": f(inp["w_mem_kv"]), "a_w_in": f(inp["a_w_in"][0]),
        "a_mu": f(inp["a_mu"][0].reshape(-1, 1)), "a_w0": f(inp["a_w0"][0].reshape(-1, 1)),
        "a_w2": f(inp["a_w2"][0]), "a_a0": f(inp["a_a0"][0].reshape(-1, 1)), "a_a2": f(inp["a_a2"][0]),
        "a_g2": f(inp["a_g2"][0]), "a_k_k": f(inp["a_k_k"][0].reshape(-1, 1)),
        "a_k_a": f(inp["a_k_a"][0].reshape(-1, 1)), "a_r_k": f(inp["a_r_k"][0].reshape(-1, 1)),
        "a_ln_w": f(inp["a_ln_w"][0].reshape(-1, 1)), "a_ln_b": f(inp["a_ln_b"][0].reshape(-1, 1)),
        "a_w_out": f(inp["a_w_out"][0]), "b_w_in": f(inp["b_w_in"][0]),
        "b_lower_bounds": f(inp["b_lower_bounds"]), "b_g_norm": f(inp["b_g_norm"][0].reshape(-1, 1)),
        "b_w_out": f(inp["b_w_out"][0]), "ffn_w_up": f(inp["ffn_w_up"]),
        "ffn_conv_w": f(inp["ffn_conv_w"]), "ffn_conv_b": f(inp["ffn_conv_b"]),
        "ffn_w_down": f(inp["ffn_w_down"]), "consts": make_consts(),
    }
    return m


def run(inputs, stop_after=None, dbg=False, ncores=8):
    key = (stop_after, dbg)
    if key not in _CACHE:
        _CACHE[key] = build_program(stop_after, dbg)
    nc, kb = _CACHE[key]
    in_maps = [_prep_inputs(inputs, c) for c in range(ncores)]
    res = run_bass_kernel_spmd(nc, in_maps, core_ids=list(range(ncores)))
    return res.results


def kernel(**inputs):
    r = run(inputs)
    f = np.float32
    B, DEC = 4, 128
    y_prompt = np.stack([r[b]["y_prompt"] for b in range(B)]).astype(f)
    y_sample = np.concatenate([r[c]["y_sample"].reshape(NS, TS, D) for c in range(8)], 0).astype(f)
    mem_k = np.stack([r[b]["mem_k_prompt"] for b in range(B)], 1).reshape(2, B, NMEM, 4, 128).astype(f)
    mem_v = np.stack([r[b]["mem_v_prompt"] for b in range(B)], 1).reshape(2, B, NMEM, 4, 128).astype(f)
    rwkv_p = np.stack([r[b]["rwkv_prompt"].reshape(24, 64, 64) for b in range(B)])[None].astype(f)
    rwkv_s = np.concatenate([r[c]["rwkv_sample"].reshape(NS, 24, 64, 64) for c in range(8)], 0)[None].astype(f)
    shift_p = np.stack([r[b]["shift_prompt"].reshape(D) for b in range(B)])[None].astype(f)
    shift_s = np.concatenate([r[c]["shift_sample"] for c in range(8)], 0)[None].astype(f)
    hgrn_p = np.stack([r[b]["hgrn_prompt"] for b in range(B)])[None].astype(f)
    hgrn_s = np.concatenate([r[c]["hgrn_sample"] for c in range(8)], 0)[None].astype(f)
    conv_p = np.stack([r[b]["conv_prompt"] for b in range(B)], 1).astype(f)
    conv_s = np.concatenate([r[c]["conv_sample"].reshape(2, NS, 2, DFF) for c in range(8)], 1).astype(f)
    return (y_prompt, y_sample, mem_k, mem_v, rwkv_p, rwkv_s, shift_p, shift_s, hgrn_p, hgrn_s, conv_p, conv_s)
```

```python
import numpy as np
from contextlib import ExitStack
import concourse.bass as bass
import concourse.mybir as mybir
from concourse.bass_utils import run_bass_kernel_spmd

F32 = mybir.dt.float32
BF16 = mybir.dt.bfloat16
AF = mybir.ActivationFunctionType
ALU = mybir.AluOpType
AX = mybir.AxisListType
F32R = mybir.dt.float32r


def R(ap):
    return ap.bitcast(F32R)

D = 2048
TP = 2048
NS = 16
TS = 8
TSM = NS * TS
TT = TP + TSM
TTX = TT + NS
NMEM = 256
DFF = 5632
NCA = 5568
NCB = 6656
EPS = 1e-6
GN_EPS = 64e-5

C_ID, C_ONES, C_BLK, C_SL64, C_SU64, C_IU64, C_SL8, C_SU8, C_IU8, C_HIU64, C_HIU8, C_SEL64, C_SEL8, C_BLK8 = range(14)
NCONST = 14


def make_consts():
    c = np.zeros((128, NCONST, 128), np.float32)
    c[:, C_ID, :] = np.eye(128)
    c[:, C_ONES, :] = 1.0
    for h in range(2):
        c[64 * h:64 * h + 64, C_BLK, 64 * h:64 * h + 64] = 1.0
    for (C, sl, su, iu) in ((64, C_SL64, C_SU64, C_IU64), (8, C_SL8, C_SU8, C_IU8)):
        n = 2 * C
        for h in range(2):
            for t in range(C):
                for s in range(C):
                    if s < t:
                        c[h * C + t, sl, h * C + s] = 1.0
                        c[h * C + s, su, h * C + t] = 1.0
                    if s <= t:
                        c[h * C + s, iu, h * C + t] = 1.0
    for (C, hiu) in ((64, C_HIU64), (8, C_HIU8)):
        for t in range(C):
            for s in range(t + 1):
                c[s, hiu, t] = 1.0
    for (C, sel) in ((64, C_SEL64), (8, C_SEL8)):
        for h in range(2):
            for t in range(C):
                c[h * C + t, sel, t] = 1.0
    c[0:8, C_BLK8, 0:64] = 1.0
    c[8:16, C_BLK8, 64:128] = 1.0
    return c


class Res:
    __slots__ = ("w", "r")

    def __init__(self):
        self.w = None
        self.r = {}


class T:
    def __init__(self, t):
        self.t = t
        self.res = Res()

    def __getitem__(self, k):
        return self.t[k]


class Eng:
    def __init__(self, kb, name, h, is_pe=False):
        self.kb = kb
        self.name = name
        self.h = h
        self.is_pe = is_pe
        self.seen = {}
        self.own = set()
        self.sem = None
        self.cnt = 0
        self.ring = []
        self.ri = 0
        self.idx = 0

    def tick(self):
        if self.sem is None or self.idx >= 30000:
            self.sem = self.kb.newsem(self.name)
            self.own.add(self.sem)
            self.cnt = 0
            self.idx = 0
            self.epoch = getattr(self, "epoch", -1) + 1
            self.kb.semkey[self.sem] = (self.name, self.epoch)
        self.idx += 1
        if self.kb.needed is None:
            self.cnt = self.idx
            return (self.sem, self.cnt), True
        if (self.name, self.epoch, self.idx) in self.kb.needed:
            self.cnt += 1
            return (self.sem, self.cnt), True
        return (self.sem, self.cnt), False


class KB:
    def __init__(self, nc, needed=None):
        self.nc = nc
        self.needed = needed
        self.waited = set()
        self.semkey = {}
        self.nsem = 0
        self.pe = Eng(self, "pe", nc.tensor, True)
        self.act = Eng(self, "act", nc.scalar)
        self.dve = Eng(self, "dve", nc.vector)
        self.pool = Eng(self, "pool", nc.gpsimd)
        self.sp = Eng(self, "sp", nc.sync)
        self.engs = [self.pe, self.act, self.dve, self.pool, self.sp]
        self.dma_slots = []
        self.nins = 0

    def newsem(self, name):
        self.nsem += 1
        return self.nc.alloc_semaphore(f"s_{name}_{self.nsem}")

    def _deps(self, reads, writes):
        deps = {}

        def add(ev):
            if ev is not None:
                if deps.get(ev[0], 0) < ev[1]:
                    deps[ev[0]] = ev[1]
        for r in reads:
            add(r.res.w)
        for w in writes:
            add(w.res.w)
            for s, v in w.res.r.items():
                add((s, v))
        return deps

    def _waits(self, eng, deps):
        for sem, val in deps.items():
            if eng.is_pe and sem in eng.own:
                continue
            if eng.seen.get(sem, 0) >= val:
                continue
            self._emit_wait(eng, sem, val)

    def _emit_wait(self, eng, sem, val):
        if val <= 0:
            eng.seen[sem] = val
            return
        eng.h.wait_ge(sem, val)
        eng.seen[sem] = val
        k = self.semkey.get(sem)
        if k is not None and self.needed is None:
            self.waited.add((k[0], k[1], val))

    def _mark(self, ev, reads, writes):
        for r in reads:
            if r.res.r.get(ev[0], 0) < ev[1]:
                r.res.r[ev[0]] = ev[1]
        for w in writes:
            w.res.w = ev
            w.res.r = {}

    def op(self, eng, fn, reads=(), writes=()):
        self._waits(eng, self._deps(reads, writes))
        ins = fn(eng.h)
        ev, do_inc = eng.tick()
        if do_inc:
            ins.then_inc(ev[0], 1)
        self._mark(ev, reads, writes)
        self.nins += 1
        eng.total = getattr(eng, "total", 0) + 1
        return ev

    def dma(self, eng, out, in_, reads=(), writes=(), **kw):
        self._waits(eng, self._deps(reads, writes))
        if len(eng.ring) < 12:
            slot = [self.newsem(eng.name + "d"), 0]
            eng.ring.append(slot)
            self.dma_slots.append(slot)
        else:
            slot = eng.ring[eng.ri % len(eng.ring)]
            eng.ri += 1
            if eng.seen.get(slot[0], 0) < slot[1]:
                self._emit_wait(eng, slot[0], slot[1])
        ins = eng.h.dma_start(out=out, in_=in_, **kw)
        slot[1] += 16
        ins.then_inc(slot[0], 16)
        ev = (slot[0], slot[1])
        self._mark(ev, reads, writes)
        self.nins += 1
        return ev

    def barrier(self):
        evs = {}
        for e in self.engs:
            if e.sem is not None and e.cnt > 0:
                evs[e.sem] = e.cnt
        for s in self.dma_slots:
            if s[1] > 0:
                evs[s[0]] = s[1]
        for e in self.engs:
            for sem, val in evs.items():
                if e.seen.get(sem, 0) >= val:
                    continue
                if sem in e.own and e.is_pe:
                    continue
                self._emit_wait(e, sem, val)


class Ctx:
    pass


_UN = [0]


def uname(n):
    _UN[0] += 1
    return f"{n}_{_UN[0]}"


def build_program(stop_after=None, dbg=False):
    _UN[0] = 0
    nc1, kb1 = _build_program(stop_after, dbg, None)
    _UN[0] = 0
    return _build_program(stop_after, dbg, kb1.waited)


def _build_program(stop_after, dbg, needed):
    nc = bass.Bass("TRN2", target_bir_lowering=False)
    kb = KB(nc, needed)
    g = Ctx()
    g.nc, g.kb = nc, kb
    di = lambda n, s: nc.dram_tensor(n, list(s), F32, kind="ExternalInput").ap()
    do = lambda n, s: nc.dram_tensor(n, list(s), F32, kind="ExternalOutput").ap()
    ds = lambda n, s, dt=F32: nc.dram_tensor(n, list(s), dt, kind="Internal").ap()
    I = Ctx()
    g.I = I
    I.x_prompt = di("x_prompt", (TP, D))
    I.x_sample = di("x_sample", (TSM, D))
    I.mem_prompt = di("mem_prompt", (NMEM, D))
    I.cache_k = di("cache_mem_k", (2, NS, NMEM, 512))
    I.cache_v = di("cache_mem_v", (2, NS, NMEM, 512))
    I.state_rwkv = di("state_rwkv", (NS, 24, 64, 64))
    I.state_shift = di("state_shift", (NS, D))
    I.state_hgrn = di("state_hgrn", (NS, 12, 128, 128))
    I.state_conv = di("state_conv", (2, NS * 2, DFF))
    I.norm_mix = di("norm_mix", (2, D))
    I.norm_ffn = di("norm_ffn", (2, D))
    I.norm_final = di("norm_final", (1, D))
    I.mem_norm = di("mem_norm", (2, D))
    I.w_mem_kv = di("w_mem_kv", (2, D, 1024))
    I.a_w_in = di("a_w_in", (D, NCA))
    I.a_mu = di("a_mu", (5056, 1))
    I.a_w0 = di("a_w0", (1536, 1))
    I.a_w2 = di("a_w2", (96, 1536))
    I.a_a0 = di("a_a0", (1536, 1))
    I.a_a2 = di("a_a2", (96, 1536))
    I.a_g2 = di("a_g2", (256, 1536))
    I.a_k_k = di("a_k_k", (1536, 1))
    I.a_k_a = di("a_k_a", (1536, 1))
    I.a_r_k = di("a_r_k", (1536, 1))
    I.a_ln_w = di("a_ln_w", (1536, 1))
    I.a_ln_b = di("a_ln_b", (1536, 1))
    I.a_w_out = di("a_w_out", (D, D))
    I.b_w_in = di("b_w_in", (D, NCB))
    I.b_lb = di("b_lower_bounds", (2, 1536))
    I.b_g_norm = di("b_g_norm", (128, 1))
    I.b_w_out = di("b_w_out", (D, D))
    I.ffn_w_up = di("ffn_w_up", (2, D, 2 * DFF))
    I.ffn_conv_w = di("ffn_conv_w", (2, 3, DFF))
    I.ffn_conv_b = di("ffn_conv_b", (2, DFF))
    I.ffn_w_down = di("ffn_w_down", (2, DFF, D))
    I.consts = di("consts", (128, NCONST, 128))
    O = Ctx()
    g.O = O
    O.y_prompt = do("y_prompt", (TP, D))
    O.y_sample = do("y_sample", (TSM, D))
    O.mem_k = do("mem_k_prompt", (2, NMEM, 512))
    O.mem_v = do("mem_v_prompt", (2, NMEM, 512))
    O.rwkv_prompt = do("rwkv_prompt", (24 * 64, 64))
    O.rwkv_sample = do("rwkv_sample", (NS, 24 * 64, 64))
    O.shift_prompt = do("shift_prompt", (1, D))
    O.shift_sample = do("shift_sample", (NS, D))
    O.hgrn_prompt = do("hgrn_prompt", (12, 128, 128))
    O.hgrn_sample = do("hgrn_sample", (NS, 12, 128, 128))
    O.conv_prompt = do("conv_prompt", (2, 2, DFF))
    O.conv_sample = do("conv_sample", (2, NS * 2, DFF))
    S = Ctx()
    g.S = S
    S.xres = ds("xres", (TT, D))
    S.pT = ds("pT", (NCB, TTX))
    S.auxT = ds("auxT", (3, 1536, TT))
    S.catT = ds("catT", (D, TT), BF16)
    S.wupb = ds("wupb", (2, 44, 128, 16 * 128), BF16)
    S.wdnb = ds("wdnb", (4, 128, 44 * 512), BF16)
    if dbg:
        O.dbg = do("dbg", (NCB, TTX))
        O.dbg2 = do("dbg2", (TT, D))
        O.dbg3 = nc.dram_tensor("dbg3", [D, TT], BF16, kind="ExternalOutput").ap()
        O.dbg4 = do("dbg4", (3, 1536, TT))

    g.cst = T(nc.alloc_sbuf_tensor("cst", [128, NCONST, 128], F32))
    g.ps = [T(nc.alloc_psum_tensor(f"ps{i}", [128, 512], F32)) for i in range(8)]
    g.psi = 0
    g.KTp = T(nc.alloc_sbuf_tensor("KTp", [128, 2, 4, NMEM], BF16))
    g.Vp = T(nc.alloc_sbuf_tensor("Vp", [128, 2, 2, 512], BF16))
    kb.dma(kb.sp, g.cst[:], I.consts[:, :, :], writes=[g.cst])

    phases = [("memkv", phase_memkv)]
    for l in range(2):
        phases.append((f"norm{l}", lambda g, l=l: phase_norm_mix(g, l)))
        phases.append((f"inproj{l}", lambda g, l=l: phase_inproj(g, l)))
        phases.append((f"attn{l}", lambda g, l=l: phase_attn(g, l)))
        phases.append((f"mix{l}", phase_rwkv if l == 0 else phase_hgrn))
        phases.append((f"outproj{l}", lambda g, l=l: phase_outproj(g, l)))
        phases.append((f"ffn{l}", lambda g, l=l: phase_ffn(g, l)))
    phases.append(("final", phase_final))
    kb.marks = []
    for name, fn in phases:
        fn(g)
        kb.barrier()
        kb.marks.append((name, getattr(kb.pe, "total", 0), getattr(kb.dve, "total", 0)))
        if stop_after == name:
            break
    if dbg:
        kb.dma(kb.sp, O.dbg[:, :], S.pT[:, :])
        kb.dma(kb.sp, O.dbg2[:, :], S.xres[:, :])
        kb.dma(kb.sp, O.dbg3[:, :], S.catT[:, :])
        kb.dma(kb.sp, O.dbg4[:, :, :], S.auxT[:, :, :])
    kb.barrier()
    return nc, kb


def next_ps(g):
    p = g.ps[g.psi % getattr(g, "ps_n", 8)]
    g.psi += 1
    return p


def rms_rstd(g, es_tiles, xt, n, tag):
    kb = g.kb
    sq, ssq, rstd = es_tiles
    kb.op(kb.act, lambda e: e.activation(out=sq[0:n, :], in_=xt[0:n, :], func=AF.Square, accum_out=ssq[0:n, :]),
          reads=[xt], writes=[sq, ssq])
    kb.op(kb.dve, lambda e: e.tensor_scalar(out=rstd[0:n, :], in0=ssq[0:n, :], scalar1=1.0 / D, scalar2=EPS,
                                            op0=ALU.mult, op1=ALU.add), reads=[ssq], writes=[rstd])
    kb.op(kb.act, lambda e: e.sqrt(out=rstd[0:n, :], in_=rstd[0:n, :]), reads=[rstd], writes=[rstd])
    kb.op(kb.dve, lambda e: e.reciprocal(out=rstd[0:n, :], in_=rstd[0:n, :]), reads=[rstd], writes=[rstd])
    return rstd


def transpose_to_hT(g, h, n, hT, col0, scale_cols=None):
    kb = g.kb
    for q in range(4):
        ps = next_ps(g)
        for i in range(4):
            kc = q * 4 + i
            kb.op(kb.pe, lambda e, kc=kc, i=i: e.transpose(out=ps[:, i * 128:i * 128 + n],
                                                           in_=h[0:n, kc * 128:(kc + 1) * 128],
                                                           identity=g.cst[0:n, C_ID, 0:n]),
                  reads=[h, g.cst], writes=[ps])
        src = ps[:, :].rearrange("p (a b) -> p a b", b=128)[:, :, 0:n]
        dst = hT[:, q * 4:(q + 1) * 4, col0:col0 + n]
        eng = kb.dve if q % 2 == 0 else kb.act
        if eng is kb.dve:
            kb.op(eng, lambda e: e.tensor_copy(out=dst, in_=src), reads=[ps], writes=[hT])
        else:
            kb.op(eng, lambda e: e.copy(out=dst, in_=src), reads=[ps], writes=[hT])


def phase_memkv(g):
    nc, kb, I, O = g.nc, g.kb, g.I, g.O
    with ExitStack() as es:
        al = lambda name, shape, dt=F32: T(es.enter_context(nc.sbuf_tensor(uname(name), shape, dt)))
        xt = [al(f"mk_x{i}", [128, D]) for i in range(2)]
        sq = al("mk_sq", [128, D])
        ssq = al("mk_ssq", [128, 1])
        rstd = al("mk_rstd", [128, 1])
        mn = al("mk_mn", [128, 2, 16])
        hmT = al("mk_hmT", [128, 16, NMEM], F32)
        hml = [al(f"mk_hml{l}", [128, 16, NMEM], BF16) for l in range(2)]
        wf = [al(f"mk_wf{i}", [128, 16, 256]) for i in range(2)]
        wb = al("mk_wb", [128, 16, 1024], BF16)
        stg = [al(f"mk_stg{i}", [128, 512]) for i in range(2)]
        kb.dma(kb.sp, mn[:], I.mem_norm.rearrange("l (kc p) -> p l kc", p=128), writes=[mn],
               allow_slow_non_contiguous=True)
        for mt in range(2):
            kb.dma(kb.sp, xt[mt][:], I.mem_prompt[mt * 128:(mt + 1) * 128, :], writes=[xt[mt]])
            r = rms_rstd(g, (sq, ssq, rstd), xt[mt], 128, "mk")
            kb.op(kb.act, lambda e: e.activation(out=xt[mt][:], in_=xt[mt][:], func=AF.Copy, scale=r[:, 0:1]),
                  reads=[xt[mt], r], writes=[xt[mt]])
            for q in range(4):
                ps = next_ps(g)
                for i in range(4):
                    kc = q * 4 + i
                    kb.op(kb.pe, lambda e, kc=kc, i=i: e.transpose(out=ps[:, i * 128:(i + 1) * 128],
                                                                   in_=xt[mt][:, kc * 128:(kc + 1) * 128],
                                                                   identity=g.cst[:, C_ID, :]),
                          reads=[xt[mt], g.cst], writes=[ps])
                kb.op(kb.dve, lambda e: e.tensor_copy(out=hmT[:, q * 4:(q + 1) * 4, mt * 128:(mt + 1) * 128],
                                                      in_=ps[:, :].rearrange("p (a b) -> p a b", b=128)),
                      reads=[ps], writes=[hmT])
        for l in range(2):
            for kc in range(16):
                kb.op(kb.dve, lambda e, kc=kc: e.tensor_scalar(out=hml[l][:, kc, :], in0=hmT[:, kc, :],
                                                               scalar1=mn[:, l, kc:kc + 1], scalar2=None,
                                                               op0=ALU.mult), reads=[hmT, mn], writes=[hml[l]])
            wsrc = I.w_mem_kv[l].rearrange("(kc p) n -> p kc n", p=128)
            for pc in range(4):
                w = wf[pc % 2]
                kb.dma(kb.sp, w[:], wsrc[:, :, pc * 256:(pc + 1) * 256], writes=[w])
                kb.op(kb.pool, lambda e, pc=pc, w=w: e.tensor_copy(out=wb[:, :, pc * 256:(pc + 1) * 256], in_=w[:]),
                      reads=[w], writes=[wb])
            for mt in range(2):
                for ct in range(2):
                    ps = next_ps(g)
                    for kc in range(16):
                        kb.op(kb.pe, lambda e, kc=kc: e.matmul(ps[:, :], lhsT=hml[l][:, kc, mt * 128:(mt + 1) * 128],
                                                               rhs=wb[:, kc, ct * 512:(ct + 1) * 512],
                                                               start=(kc == 0), stop=(kc == 15)),
                              reads=[hml[l], wb], writes=[ps])
                    if ct == 0:
                        s = stg[mt % 2]
                        kb.op(kb.act, lambda e: e.copy(out=s[:], in_=ps[:, :]), reads=[ps], writes=[s])
                        kb.dma(kb.pool, O.mem_k[l, mt * 128:(mt + 1) * 128, :], s[:], reads=[s])
                    else:
                        s = stg[mt % 2]
                        kb.op(kb.act, lambda e: e.copy(out=s[:], in_=ps[:, :]), reads=[ps], writes=[s])
                        kb.op(kb.dve, lambda e: e.tensor_copy(out=g.Vp[:, l, mt, :], in_=s[:]), reads=[s], writes=[g.Vp])
                        kb.dma(kb.pool, O.mem_v[l, mt * 128:(mt + 1) * 128, :], s[:], reads=[s])
            for h in range(4):
                ps = next_ps(g)
                for kc in range(16):
                    kb.op(kb.pe, lambda e, kc=kc: e.matmul(ps[:, 0:NMEM], lhsT=wb[:, kc, h * 128:(h + 1) * 128],
                                                           rhs=hml[l][:, kc, :], start=(kc == 0), stop=(kc == 15)),
                          reads=[hml[l], wb], writes=[ps])
                kb.op(kb.dve, lambda e: e.tensor_copy(out=g.KTp[:, l, h, :], in_=ps[:, 0:NMEM]),
                      reads=[ps], writes=[g.KTp])


def phase_norm_mix(g, l):
    nc, kb, I, O, S = g.nc, g.kb, g.I, g.O, g.S
    g.hT_stack = ExitStack()
    ntok = TTX if l == 0 else TT
    g.hT = T(g.hT_stack.enter_context(nc.sbuf_tensor(uname(f"hT{l}"), [128, 16, ntok], BF16)))
    with ExitStack() as es:
        al = lambda name, shape, dt=F32: T(es.enter_context(nc.sbuf_tensor(uname(name), shape, dt)))
        xt = [al(f"nm_x{i}", [128, D]) for i in range(2)]
        ht = [al(f"nm_h{i}", [128, D]) for i in range(2)]
        sq = al("nm_sq", [128, D])
        ssq = al("nm_ssq", [128, 1])
        rstd = al("nm_rstd", [128, 1])
        gbc = al("nm_g", [128, D])
        kb.dma(kb.sp, gbc[:], I.norm_mix[l:l + 1, :].partition_broadcast(128), writes=[gbc])
        for i in range(17):
            x = xt[i % 2]
            h = ht[i % 2]
            if l == 0:
                src = I.x_prompt[i * 128:(i + 1) * 128, :] if i < 16 else I.x_sample[:, :]
            else:
                src = S.xres[i * 128:(i + 1) * 128, :]
            kb.dma(kb.sp, x[:], src, writes=[x])
            if l == 0:
                kb.dma(kb.pool, S.xres[i * 128:(i + 1) * 128, :], x[:], reads=[x])
            r = rms_rstd(g, (sq, ssq, rstd), x, 128, "nm")
            kb.op(kb.dve, lambda e: e.scalar_tensor_tensor(out=h[:], in0=x[:], scalar=r[:, 0:1], in1=gbc[:],
                                                           op0=ALU.mult, op1=ALU.mult),
                  reads=[x, r, gbc], writes=[h])
            if l == 0:
                if i == 15:
                    kb.dma(kb.pool, O.shift_prompt[0:1, :], h[127:128, :], reads=[h])
                if i == 16:
                    kb.dma(kb.pool, O.shift_sample[:, :],
                           h[TS - 1:128:TS, :], reads=[h])
            transpose_to_hT(g, h, 128, g.hT, i * 128)
        if l == 0:
            x = xt[1]
            kb.dma(kb.sp, x[0:NS, :], I.state_shift[:, :], writes=[x])
            transpose_to_hT(g, x, NS, g.hT, TT)


def linear_fm(g, es, wsrc, ncols, KC, actT, tok_tiles, epilogue, tag, wblk=256):
    nc, kb = g.nc, g.kb
    al = lambda name, shape, dt=F32: T(es.enter_context(nc.sbuf_tensor(uname(name), shape, dt)))
    wf = [al(f"{tag}_wf{i}", [128, KC, wblk]) for i in range(2)]
    wb = [al(f"{tag}_wb{i}", [128, KC, wblk], BF16) for i in range(2)]
    nblk = (ncols + wblk - 1) // wblk
    for b in range(nblk):
        c0 = b * wblk
        cw = min(wblk, ncols - c0)
        f = wf[b % 2]
        w = wb[b % 2]
        kb.dma(kb.sp, f[:, :, 0:cw], wsrc[:, :, c0:c0 + cw], writes=[f])
        ceng = kb.pool if b % 2 == 0 else kb.dve
        kb.op(ceng, lambda e: e.tensor_copy(out=w[:, :, 0:cw], in_=f[:, :, 0:cw]), reads=[f], writes=[w])
        for jj in range(0, cw, 128):
            ncj = min(128, cw - jj)
            j = (c0 + jj) // 128
            for ti, (t0, n) in enumerate(tok_tiles):
                ps = next_ps(g)
                for kc in range(KC):
                    kb.op(kb.pe, lambda e, kc=kc: e.matmul(ps[0:ncj, 0:n], lhsT=w[:, kc, jj:jj + ncj],
                                                           rhs=actT[:, kc, t0:t0 + n],
                                                           start=(kc == 0), stop=(kc == KC - 1)),
                          reads=[w, actT], writes=[ps])
                epilogue(j, ncj, ti, t0, n, ps)


def tok_tiles_of(n):
    out = []
    t = 0
    while t < n:
        m = min(512, n - t)
        out.append((t, m))
        t += m
    return out


def phase_inproj(g, l):
    nc, kb, I, O, S = g.nc, g.kb, g.I, g.O, g.S
    ncols = NCA if l == 0 else NCB
    ntok = TTX if l == 0 else TT
    w = (I.a_w_in if l == 0 else I.b_w_in).rearrange("(kc p) n -> p kc n", p=128)
    with ExitStack() as es:
        al = lambda name, shape, dt=F32: T(es.enter_context(nc.sbuf_tensor(uname(name), shape, dt)))
        stg = [al(f"ip_stg{i}", [128, 512]) for i in range(4)]
        cnt = [0]

        def epi(j, ncj, ti, t0, n, ps):
            s = stg[cnt[0] % 4]
            if cnt[0] % 2 == 0:
                kb.op(kb.act, lambda e: e.copy(out=s[0:ncj, 0:n], in_=ps[0:ncj, 0:n]), reads=[ps], writes=[s])
            else:
                kb.op(kb.dve, lambda e: e.tensor_copy(out=s[0:ncj, 0:n], in_=ps[0:ncj, 0:n]), reads=[ps], writes=[s])
            kb.dma(kb.pool, S.pT[j * 128:j * 128 + ncj, t0:t0 + n], s[0:ncj, 0:n], reads=[s])
            cnt[0] += 1
        linear_fm(g, es, w, ncols, 16, g.hT, tok_tiles_of(ntok), epi, f"ip{l}")
    g.hT_stack.close()


def phase_attn(g, l):
    nc, kb, I, O, S = g.nc, g.kb, g.I, g.O, g.S
    scale = 128.0 ** -0.5
    with ExitStack() as es:
        al = lambda name, shape, dt=F32: T(es.enter_context(nc.sbuf_tensor(uname(name), shape, dt)))
        qp0 = [al(f"at_qp0{i}", [128, TP]) for i in range(2)]
        qp = [al(f"at_qp{i}", [128, TP], BF16) for i in range(2)]
        qs0 = al("at_qs0", [128, 4, TSM])
        qs = al("at_qs", [128, 4, TSM], BF16)
        vin0 = [al(f"at_vin0{i}", [128, 2, 512]) for i in range(2)]
        pb = [al(f"at_p{i}", [128, NMEM]) for i in range(3)]
        pT = [al(f"at_pT{i}", [128, 2, 128], BF16) for i in range(3)]
        sm = [[al(f"at_sm{i}_{k}", [128, 1]) for k in range(4)] for i in range(3)]
        ob = [al(f"at_o{i}", [128, 128], BF16) for i in range(3)]
        kin = [al(f"at_kin{i}", [128, 2, 512]) for i in range(2)]
        vin = [al(f"at_vin{i}", [128, 2, 512], BF16) for i in range(2)]
        kts = [al(f"at_kts{i}", [128, 4, NMEM], BF16) for i in range(2)]
        uc = [0]

        def unit(qt, qap, n, ktT, ktap, vT, vap, dest):
            u = uc[0] % 3
            uc[0] += 1
            p, pt, (mx, nb, sme, rs), o = pb[u], pT[u], sm[u], ob[u]
            ps = next_ps(g)
            kb.op(kb.pe, lambda e: e.matmul(ps[0:n, 0:NMEM], lhsT=qap, rhs=ktap, start=True, stop=True),
                  reads=[qt, ktT], writes=[ps])
            kb.op(kb.dve, lambda e: e.tensor_reduce(out=mx[0:n, :], in_=ps[0:n, 0:NMEM], axis=AX.X, op=ALU.max),
                  reads=[ps], writes=[mx])
            kb.op(kb.dve, lambda e: e.tensor_scalar(out=nb[0:n, :], in0=mx[0:n, :], scalar1=-scale, scalar2=None,
                                                    op0=ALU.mult), reads=[mx], writes=[nb])
            yield
            kb.op(kb.act, lambda e: e.activation(out=p[0:n, :], in_=ps[0:n, 0:NMEM], func=AF.Exp, bias=nb[0:n, 0:1],
                                                 scale=scale, accum_out=sme[0:n, :]),
                  reads=[ps, nb], writes=[p, sme])
            yield
            kb.op(kb.dve, lambda e: e.reciprocal(out=rs[0:n, :], in_=sme[0:n, :]), reads=[sme], writes=[rs])
            kb.op(kb.dve, lambda e: e.tensor_scalar(out=p[0:n, :], in0=p[0:n, :], scalar1=rs[0:n, 0:1], scalar2=None,
                                                    op0=ALU.mult), reads=[p, rs], writes=[p])
            yield
            ps2 = next_ps(g)
            for mc in range(2):
                kb.op(kb.pe, lambda e, mc=mc: e.transpose(out=ps2[:, mc * 128:mc * 128 + n],
                                                          in_=p[0:n, mc * 128:(mc + 1) * 128],
                                                          identity=g.cst[0:n, C_ID, 0:n]),
                      reads=[p, g.cst], writes=[ps2])
            kb.op(kb.act, lambda e: e.copy(out=pt[:, :, 0:n],
                                           in_=ps2[:, 0:256].rearrange("p (a b) -> p a b", b=128)[:, :, 0:n]),
                  reads=[ps2], writes=[pt])
            yield
            ps3 = next_ps(g)
            for mc in range(2):
                kb.op(kb.pe, lambda e, mc=mc: e.matmul(ps3[:, 0:n], lhsT=vap(mc), rhs=pt[:, mc, 0:n],
                                                       start=(mc == 0), stop=(mc == 1)),
                      reads=[vT, pt], writes=[ps3])
            kb.op(kb.dve, lambda e: e.tensor_copy(out=o[:, 0:n], in_=ps3[:, 0:n]), reads=[ps3], writes=[o])
            kb.dma(kb.pool, dest, o[:, 0:n], reads=[o])

        pend = []

        def submit(gen):
            pend.append(gen)
            if len(pend) == 2:
                run_interleaved(pend[0], pend[1])
                pend.clear()

        def flush():
            if pend:
                run_interleaved(pend[0], None)
                pend.clear()

        for h in range(4):
            q = qp[h % 2]
            q0 = qp0[h % 2]
            kb.dma(kb.sp, q0[:], S.pT[h * 128:(h + 1) * 128, 0:TP], writes=[q0])
            kb.op(kb.pool, lambda e: e.tensor_copy(out=q[:], in_=q0[:]), reads=[q0], writes=[q])
            for i in range(16):
                submit(unit(q, q[:, i * 128:(i + 1) * 128], 128, g.KTp, g.KTp[:, l, h, :], g.Vp,
                            lambda mc, h=h: g.Vp[:, l, mc, h * 128:(h + 1) * 128],
                            S.catT[(12 + h) * 128:(13 + h) * 128, i * 128:(i + 1) * 128]))
            flush()
        kb.dma(kb.sp, qs0[:], S.pT[0:512, TP:TT].rearrange("(h p) t -> p h t", p=128), writes=[qs0])
        kb.op(kb.pool, lambda e: e.tensor_copy(out=qs[:], in_=qs0[:]), reads=[qs0], writes=[qs])
        for s in range(NS):
            ki, vi, kt = kin[s % 2], vin[s % 2], kts[s % 2]
            kb.dma(kb.sp, ki[:], I.cache_k[l, s].rearrange("(mt p) c -> p mt c", p=128), writes=[ki])
            vi0 = vin0[s % 2]
            kb.dma(kb.sp, vi0[:], I.cache_v[l, s].rearrange("(mt p) c -> p mt c", p=128), writes=[vi0])
            kb.op(kb.pool, lambda e: e.tensor_copy(out=vi[:], in_=vi0[:]), reads=[vi0], writes=[vi])
            for hh in range(2):
                ps = next_ps(g)
                for a in range(2):
                    for mt in range(2):
                        h = hh * 2 + a
                        kb.op(kb.pe, lambda e, h=h, mt=mt, a=a: e.transpose(
                            out=ps[:, (a * 2 + mt) * 128:(a * 2 + mt + 1) * 128],
                            in_=ki[:, mt, h * 128:(h + 1) * 128], identity=g.cst[:, C_ID, :]),
                            reads=[ki, g.cst], writes=[ps])
                kb.op(kb.act, lambda e: e.copy(out=kt[:, hh * 2:hh * 2 + 2, :],
                                               in_=ps[:, :].rearrange("p (a b) -> p a b", b=256)),
                      reads=[ps], writes=[kt])
            for h in range(4):
                submit(unit(qs, qs[:, h, s * TS:(s + 1) * TS], TS, kt, kt[:, h, :], vi,
                            lambda mc, h=h, vi=vi: vi[:, mc, h * 128:(h + 1) * 128],
                            S.catT[(12 + h) * 128:(13 + h) * 128, TP + s * TS:TP + (s + 1) * TS]))
            flush()


def phase_outproj(g, l):
    nc, kb, I, O, S = g.nc, g.kb, g.I, g.O, g.S
    w = (I.a_w_out if l == 0 else I.b_w_out).rearrange("(kc p) n -> p kc n", p=128)
    with ExitStack() as es:
        al = lambda name, shape, dt=F32: T(es.enter_context(nc.sbuf_tensor(uname(name), shape, dt)))
        catT = al("op_cat", [128, 16, TT], BF16)
        wb = al("op_wb", [128, 16, D], BF16)
        wf = [al(f"op_wf{i}", [128, 16, 128]) for i in range(2)]
        xt = [al(f"op_x{i}", [128, D]) for i in range(2)]
        kb.dma(kb.sp, catT[:], S.catT.rearrange("(kc p) t -> p kc t", p=128), writes=[catT])
        for pc in range(16):
            f = wf[pc % 2]
            kb.dma(kb.sp, f[:], w[:, :, pc * 128:(pc + 1) * 128], writes=[f])
            ce = kb.pool if pc % 2 == 0 else kb.act
            if ce is kb.pool:
                kb.op(ce, lambda e: e.tensor_copy(out=wb[:, :, pc * 128:(pc + 1) * 128], in_=f[:]), reads=[f], writes=[wb])
            else:
                kb.op(ce, lambda e: e.copy(out=wb[:, :, pc * 128:(pc + 1) * 128], in_=f[:]), reads=[f], writes=[wb])
        for i in range(17):
            x = xt[i % 2]
            kb.dma(kb.sp, x[:], S.xres[i * 128:(i + 1) * 128, :], writes=[x])
            for ct in range(4):
                ps = next_ps(g)
                for kc in range(16):
                    kb.op(kb.pe, lambda e, kc=kc: e.matmul(ps[:, :], lhsT=catT[:, kc, i * 128:(i + 1) * 128],
                                                           rhs=wb[:, kc, ct * 512:(ct + 1) * 512],
                                                           start=(kc == 0), stop=(kc == 15)),
                          reads=[catT, wb], writes=[ps])
                kb.op(kb.dve, lambda e: e.tensor_tensor(out=x[:, ct * 512:(ct + 1) * 512], in0=ps[:, :],
                                                        in1=x[:, ct * 512:(ct + 1) * 512], op=ALU.add),
                      reads=[ps, x], writes=[x])
            kb.dma(kb.pool, S.xres[i * 128:(i + 1) * 128, :], x[:], reads=[x])


FFN_GROUPS = [(0, 768, False), (768, 768, False), (1536, 512, True)]


def phase_ffn(g, l):
    nc, kb, I, O, S = g.nc, g.kb, g.I, g.O, g.S
    wup = I.ffn_w_up[l].rearrange("(kc p) n -> p kc n", p=128)
    wdn = I.ffn_w_down[l].rearrange("(j p) n -> p j n", p=128)
    with ExitStack() as es0:
        al0 = lambda name, shape, dt=F32: T(es0.enter_context(nc.sbuf_tensor(uname(name), shape, dt)))
        carry = al0("ff_carry", [128, 44, 2])
        cvo = al0("ff_cvo", [128, 44, 2 + 2 * NS])
        scA = al0("ff_scA", [128, 44, 2 * NS])
        cw = al0("ff_cw", [128, 3, 44])
        cb = al0("ff_cb", [128, 44])
        kb.dma(kb.sp, cw[:], I.ffn_conv_w[l].rearrange("k (j p) -> p k j", p=128), writes=[cw],
               allow_slow_non_contiguous=True)
        kb.dma(kb.sp, cb[:], I.ffn_conv_b[l:l + 1, :].rearrange("o (j p) -> p (o j)", p=128), writes=[cb],
               allow_slow_non_contiguous=True)
        kb.op(kb.dve, lambda e: e.memset(carry[:], 0.0), writes=[carry])
        with ExitStack() as es:
            al = lambda name, shape, dt=F32: T(es.enter_context(nc.sbuf_tensor(uname(name), shape, dt)))
            sc = al("ff_sc", [2 * NS, DFF])
            kb.dma(kb.sp, sc[:], I.state_conv[l], writes=[sc])
            for j in range(44):
                ps = next_ps(g)
                kb.op(kb.pe, lambda e: e.transpose(out=ps[:, 0:2 * NS], in_=sc[:, j * 128:(j + 1) * 128],
                                                   identity=g.cst[0:2 * NS, C_ID, 0:2 * NS]),
                      reads=[sc, g.cst], writes=[ps])
                kb.op(kb.dve, lambda e: e.tensor_copy(out=scA[:, j, :], in_=ps[:, 0:2 * NS]), reads=[ps], writes=[scA])
        kb.barrier()
        for gi, (p0, pn, has_s) in enumerate(FFN_GROUPS):
            ntok = pn + (TSM if has_s else 0)
            with ExitStack() as esg:
                alg = lambda name, shape, dt=F32: T(esg.enter_context(nc.sbuf_tensor(uname(name), shape, dt)))
                gT = alg("ff_gT", [128, 44, ntok], BF16)
                with ExitStack() as es:
                    al = lambda name, shape, dt=F32: T(es.enter_context(nc.sbuf_tensor(uname(name), shape, dt)))
                    hT2 = al("ff_hT", [128, 16, ntok], BF16)
                    with ExitStack() as es2:
                        al2 = lambda name, shape, dt=F32: T(es2.enter_context(nc.sbuf_tensor(uname(name), shape, dt)))
                        xt = [al2(f"ff_x{i}", [128, D]) for i in range(2)]
                        ht = [al2(f"ff_h{i}", [128, D]) for i in range(2)]
                        sq = al2("ff_sq", [128, D])
                        ssq = al2("ff_ssq", [128, 1])
                        rstd = al2("ff_rstd", [128, 1])
                        gbc = al2("ff_g", [128, D])
                        kb.dma(kb.sp, gbc[:], I.norm_ffn[l:l + 1, :].partition_broadcast(128), writes=[gbc])
                        rows = [p0 + k * 128 for k in range(pn // 128)] + ([TP] if has_s else [])
                        for k, r0 in enumerate(rows):
                            x, h = xt[k % 2], ht[k % 2]
                            kb.dma(kb.sp, x[:], S.xres[r0:r0 + 128, :], writes=[x])
                            r = rms_rstd(g, (sq, ssq, rstd), x, 128, "ff")
                            kb.op(kb.dve, lambda e: e.scalar_tensor_tensor(out=h[:], in0=x[:], scalar=r[:, 0:1],
                                                                           in1=gbc[:], op0=ALU.mult, op1=ALU.mult),
                                  reads=[x, r, gbc], writes=[h])
                            transpose_to_hT(g, h, 128, hT2, k * 128)
                    kb.barrier()
                    wfa = [al(f"ff_wfa{i}", [128, 16, 128]) for i in range(3)]
                    wfv = [al(f"ff_wfv{i}", [128, 16, 128]) for i in range(3)]
                    wba = [al(f"ff_wba{i}", [128, 16, 128], BF16) for i in range(2)]
                    wbv = [al(f"ff_wbv{i}", [128, 16, 128], BF16) for i in range(2)]
                    aext = [al(f"ff_ae{i}", [128, 2 + pn]) for i in range(2)]
                    aexs = [al(f"ff_as{i}", [128, NS, TS + 2]) for i in range(2)]
                    tmp = [al(f"ff_t{i}", [128, 512]) for i in range(2)]
                    tmp2 = [al(f"ff_u{i}", [128, 512]) for i in range(2)]
                    ptiles = tok_tiles_of(pn)
                    tc = [0]
                    for j in range(44):
                        fa, fv, ba, bv = wfa[j % 3], wfv[j % 3], wba[j % 2], wbv[j % 2]
                        ae, asx = aext[j % 2], aexs[j % 2]
                        if gi == 0:
                            kb.dma(kb.sp, fa[:], wup[:, :, j * 128:(j + 1) * 128], writes=[fa])
                            kb.dma(kb.sp, fv[:], wup[:, :, DFF + j * 128:DFF + (j + 1) * 128], writes=[fv])
                            kb.op(kb.act, lambda e: e.copy(out=ba[:], in_=fa[:]), reads=[fa], writes=[ba])
                            kb.op(kb.pool, lambda e: e.tensor_copy(out=bv[:], in_=fv[:]), reads=[fv], writes=[bv])
                            kb.dma(kb.pool, S.wupb[0, j], ba[:, :, :].rearrange("p a b -> p (a b)"), reads=[ba])
                            kb.dma(kb.pool, S.wupb[1, j], bv[:, :, :].rearrange("p a b -> p (a b)"), reads=[bv])
                        else:
                            kb.dma(kb.sp, ba[:, :, :].rearrange("p a b -> p (a b)"), S.wupb[0, j], writes=[ba])
                            kb.dma(kb.sp, bv[:, :, :].rearrange("p a b -> p (a b)"), S.wupb[1, j], writes=[bv])
                        kb.op(kb.dve, lambda e: e.tensor_copy(out=ae[:, 0:2], in_=carry[:, j, :]), reads=[carry], writes=[ae])
                        for (t0, n) in ptiles:
                            psa = next_ps(g)
                            for kc in range(16):
                                kb.op(kb.pe, lambda e, kc=kc: e.matmul(psa[:, 0:n], lhsT=ba[:, kc, :],
                                                                       rhs=hT2[:, kc, t0:t0 + n],
                                                                       start=(kc == 0), stop=(kc == 15)),
                                      reads=[ba, hT2], writes=[psa])
                            psv = next_ps(g)
                            for kc in range(16):
                                kb.op(kb.pe, lambda e, kc=kc: e.matmul(psv[:, 0:n], lhsT=bv[:, kc, :],
                                                                       rhs=hT2[:, kc, t0:t0 + n],
                                                                       start=(kc == 0), stop=(kc == 15)),
                                      reads=[bv, hT2], writes=[psv])
                            kb.op(kb.act, lambda e: e.copy(out=ae[:, 2 + t0:2 + t0 + n], in_=psa[:, 0:n]),
                                  reads=[psa], writes=[ae])
                            t1, t2 = tmp[tc[0] % 2], tmp2[tc[0] % 2]
                            tc[0] += 1
                            kb.op(kb.dve, lambda e: e.tensor_scalar(out=t1[:, 0:n], in0=ae[:, t0:t0 + n],
                                                                    scalar1=cw[:, 0, j:j + 1], scalar2=cb[:, j:j + 1],
                                                                    op0=ALU.mult, op1=ALU.add),
                                  reads=[ae, cw, cb], writes=[t1])
                            for tap in (1, 2):
                                kb.op(kb.dve, lambda e, tap=tap: e.scalar_tensor_tensor(
                                    out=t1[:, 0:n], in0=ae[:, t0 + tap:t0 + tap + n], scalar=cw[:, tap, j:j + 1],
                                    in1=t1[:, 0:n], op0=ALU.mult, op1=ALU.add), reads=[ae, cw, t1], writes=[t1])
                            kb.op(kb.act, lambda e: e.activation(out=t2[:, 0:n], in_=t1[:, 0:n], func=AF.Gelu_apprx_tanh),
                                  reads=[t1], writes=[t2])
                            kb.op(kb.dve, lambda e: e.tensor_tensor(out=gT[:, j, t0:t0 + n], in0=psv[:, 0:n],
                                                                    in1=t2[:, 0:n], op=ALU.mult),
                                  reads=[psv, t2], writes=[gT])
                        if gi < len(FFN_GROUPS) - 1:
                            kb.op(kb.dve, lambda e: e.tensor_copy(out=carry[:, j, :], in_=ae[:, pn:pn + 2]),
                                  reads=[ae], writes=[carry])
                        else:
                            kb.op(kb.dve, lambda e: e.tensor_copy(out=cvo[:, j, 0:2], in_=ae[:, pn:pn + 2]),
                                  reads=[ae], writes=[cvo])
                        if has_s:
                            psa = next_ps(g)
                            for kc in range(16):
                                kb.op(kb.pe, lambda e, kc=kc: e.matmul(psa[:, 0:TSM], lhsT=ba[:, kc, :],
                                                                       rhs=hT2[:, kc, pn:pn + TSM],
                                                                       start=(kc == 0), stop=(kc == 15)),
                                      reads=[ba, hT2], writes=[psa])
                            psv = next_ps(g)
                            for kc in range(16):
                                kb.op(kb.pe, lambda e, kc=kc: e.matmul(psv[:, 0:TSM], lhsT=bv[:, kc, :],
                                                                       rhs=hT2[:, kc, pn:pn + TSM],
                                                                       start=(kc == 0), stop=(kc == 15)),
                                      reads=[bv, hT2], writes=[psv])
                            kb.op(kb.act, lambda e: e.copy(out=asx[:, :, 2:2 + TS],
                                                           in_=psa[:, 0:TSM].rearrange("p (s t) -> p s t", t=TS)),
                                  reads=[psa], writes=[asx])
                            kb.op(kb.dve, lambda e: e.tensor_copy(out=asx[:, :, 0:2],
                                                                  in_=scA[:, j, :].rearrange("p (s t) -> p s t", t=2)),
                                  reads=[scA], writes=[asx])
                            kb.op(kb.dve, lambda e: e.tensor_copy(out=cvo[:, j, 2:2 + 2 * NS].rearrange("p (s t) -> p s t", t=2),
                                                                  in_=asx[:, :, TS:TS + 2]), reads=[asx], writes=[cvo])
                            t1, t2 = tmp[tc[0] % 2], tmp2[tc[0] % 2]
                            tc[0] += 1
                            v3 = lambda tt: tt[:, 0:TSM].rearrange("p (s t) -> p s t", t=TS)
                            kb.op(kb.dve, lambda e: e.tensor_scalar(out=v3(t1), in0=asx[:, :, 0:TS],
                                                                    scalar1=cw[:, 0, j:j + 1], scalar2=cb[:, j:j + 1],
                                                                    op0=ALU.mult, op1=ALU.add),
                                  reads=[asx, cw, cb], writes=[t1])
                            for tap in (1, 2):
                                kb.op(kb.dve, lambda e, tap=tap: e.scalar_tensor_tensor(
                                    out=v3(t1), in0=asx[:, :, tap:tap + TS], scalar=cw[:, tap, j:j + 1],
                                    in1=v3(t1), op0=ALU.mult, op1=ALU.add), reads=[asx, cw, t1], writes=[t1])
                            kb.op(kb.act, lambda e: e.activation(out=t2[:, 0:TSM], in_=t1[:, 0:TSM], func=AF.Gelu_apprx_tanh),
                                  reads=[t1], writes=[t2])
                            kb.op(kb.dve, lambda e: e.tensor_tensor(out=gT[:, j, pn:pn + TSM], in0=psv[:, 0:TSM],
                                                                    in1=t2[:, 0:TSM], op=ALU.mult),
                                  reads=[psv, t2], writes=[gT])
                kb.barrier()
                with ExitStack() as es:
                    al = lambda name, shape, dt=F32: T(es.enter_context(nc.sbuf_tensor(uname(name), shape, dt)))
                    wdbs = [al(f"ff_wdb{i}", [128, 44, 512], BF16) for i in range(2)]
                    wdf = [al(f"ff_wdf{i}", [128, 2, 512]) for i in range(2)]
                    xs = [al(f"ff_xs{i}", [128, 512]) for i in range(3)]
                    rows = [(p0 + k * 128, k * 128) for k in range(pn // 128)] + ([(TP, pn)] if has_s else [])
                    xc = [0]
                    for ct in range(4):
                        wdb = wdbs[ct % 2]
                        if gi > 0:
                            kb.dma(kb.sp, wdb[:, :, :].rearrange("p a b -> p (a b)"), S.wdnb[ct], writes=[wdb])
                        for pc in (range(22) if gi == 0 else []):
                            f = wdf[pc % 2]
                            kb.dma(kb.sp, f[:], wdn[:, pc * 2:(pc + 1) * 2, ct * 512:(ct + 1) * 512], writes=[f])
                            if pc % 3 == 0:
                                kb.op(kb.pool, lambda e: e.tensor_copy(out=wdb[:, pc * 2:(pc + 1) * 2, :], in_=f[:]),
                                      reads=[f], writes=[wdb])
                            else:
                                kb.op(kb.act, lambda e: e.copy(out=wdb[:, pc * 2:(pc + 1) * 2, :], in_=f[:]),
                                      reads=[f], writes=[wdb])
                        if gi == 0:
                            kb.dma(kb.pool, S.wdnb[ct], wdb[:, :, :].rearrange("p a b -> p (a b)"), reads=[wdb])
                        for b0 in range(0, len(rows), 4):
                            batch = rows[b0:b0 + 4]
                            pss = [next_ps(g) for _ in batch]
                            for j in range(44):
                                for (r0, c0), ps in zip(batch, pss):
                                    kb.op(kb.pe, lambda e, c0=c0, ps=ps: e.matmul(ps[:, :], lhsT=gT[:, j, c0:c0 + 128],
                                                                                  rhs=wdb[:, j, :],
                                                                                  start=(j == 0), stop=(j == 43)),
                                          reads=[gT, wdb], writes=[ps])
                            for (r0, c0), ps in zip(batch, pss):
                                x = xs[xc[0] % 3]
                                xc[0] += 1
                                kb.dma(kb.sp, x[:], S.xres[r0:r0 + 128, ct * 512:(ct + 1) * 512], writes=[x])
                                kb.op(kb.dve, lambda e: e.tensor_tensor(out=x[:], in0=ps[:, :], in1=x[:], op=ALU.add),
                                      reads=[ps, x], writes=[x])
                                kb.dma(kb.pool, S.xres[r0:r0 + 128, ct * 512:(ct + 1) * 512], x[:], reads=[x])
            kb.barrier()
        with ExitStack() as es:
            al = lambda name, shape, dt=F32: T(es.enter_context(nc.sbuf_tensor(uname(name), shape, dt)))
            co = al("ff_co", [2 + 2 * NS, DFF])
            nr = 2 + 2 * NS
            for j in range(44):
                ps = next_ps(g)
                kb.op(kb.pe, lambda e: e.transpose(out=ps[0:nr, 0:128], in_=cvo[:, j, :], identity=g.cst[:, C_ID, :]),
                      reads=[cvo, g.cst], writes=[ps])
                kb.op(kb.dve, lambda e: e.tensor_copy(out=co[:, j * 128:(j + 1) * 128], in_=ps[0:nr, 0:128]),
                      reads=[ps], writes=[co])
            kb.dma(kb.pool, O.conv_prompt[l], co[0:2, :], reads=[co])
            kb.dma(kb.pool, O.conv_sample[l], co[2:nr, :], reads=[co])
            kb.barrier()


def phase_final(g):
    nc, kb, I, O, S = g.nc, g.kb, g.I, g.O, g.S
    with ExitStack() as es:
        al = lambda name, shape, dt=F32: T(es.enter_context(nc.sbuf_tensor(uname(name), shape, dt)))
        xt = [al(f"fn_x{i}", [128, D]) for i in range(2)]
        ht = [al(f"fn_h{i}", [128, D]) for i in range(2)]
        sq = al("fn_sq", [128, D])
        ssq = al("fn_ssq", [128, 1])
        rstd = al("fn_rstd", [128, 1])
        gbc = al("fn_g", [128, D])
        kb.dma(kb.sp, gbc[:], I.norm_final[0:1, :].partition_broadcast(128), writes=[gbc])
        for i in range(17):
            x, h = xt[i % 2], ht[i % 2]
            kb.dma(kb.sp, x[:], S.xres[i * 128:(i + 1) * 128, :], writes=[x])
            r = rms_rstd(g, (sq, ssq, rstd), x, 128, "fn")
            kb.op(kb.dve, lambda e: e.scalar_tensor_tensor(out=h[:], in0=x[:], scalar=r[:, 0:1], in1=gbc[:],
                                                           op0=ALU.mult, op1=ALU.mult), reads=[x, r, gbc], writes=[h])
            dst = O.y_prompt[i * 128:(i + 1) * 128, :] if i < 16 else O.y_sample[:, :]
            kb.dma(kb.pool, dst, h[:], reads=[h])


import math
LOG_SCALE = -math.exp(-0.5)
SEGS = [(k * 512, 512, 64, 8, False) for k in range(4)] + [(TP, TSM, 8, NS, True)]


def pvec(g, dst_ap, src, T_dst):
    g.kb.dma(g.kb.sp, dst_ap, src.rearrange("(c p) o -> p (c o)", p=128), writes=[T_dst],
             allow_slow_non_contiguous=True)


def phase_rwkv(g):
    nc, kb, I, O, S = g.nc, g.kb, g.I, g.O, g.S
    dve, act, pe, pool, sp = kb.dve, kb.act, kb.pe, kb.pool, kb.sp
    cst = g.cst
    with ExitStack() as es:
        al = lambda name, shape, dt=F32: T(es.enter_context(nc.sbuf_tensor(uname(name), shape, dt)))
        praw = al("ra_p", [128, TTX])
        dtm = al("ra_d", [128, TT])
        xw = al("ra_xw", [128, TT])
        xa = al("ra_xa", [128, TT])
        xg = al("ra_xg", [128, 2, TT])
        mus = al("ra_mu", [128, 4])
        w2 = al("ra_w2", [96, 1536])
        a2 = al("ra_a2", [96, 1536])
        g2 = al("ra_g2", [128, 2, 1536])
        w0 = al("ra_w0", [128, 12])
        a0 = al("ra_a0", [128, 12])
        stg = [al(f"ra_stg{i}", [128, 512]) for i in range(4)]
        kb.dma(sp, w2[:], I.a_w2[:, :], writes=[w2])
        kb.dma(sp, a2[:], I.a_a2[:, :], writes=[a2])
        kb.dma(sp, g2[:], I.a_g2.rearrange("(kc p) n -> p kc n", p=128), writes=[g2])
        pvec(g, w0[:], I.a_w0, w0)
        pvec(g, a0[:], I.a_a0, a0)
        kb.dma(sp, mus[0:96, 0:1], I.a_mu[4608:4704, :], writes=[mus])
        kb.dma(sp, mus[0:96, 1:2], I.a_mu[4704:4800, :], writes=[mus])
        kb.dma(sp, mus[:, 2:3], I.a_mu[4800:4928, :], writes=[mus])
        kb.dma(sp, mus[:, 3:4], I.a_mu[4928:5056, :], writes=[mus])

        def tshift_full(row0, P, mucol, out_ap, out_T, func):
            kb.dma(sp, praw[0:P, :], S.pT[row0:row0 + P, 0:TTX], writes=[praw])
            p = praw
            kb.op(dve, lambda e: e.tensor_tensor(out=dtm[0:P, 1:TP], in0=p[0:P, 0:TP - 1], in1=p[0:P, 1:TP],
                                                 op=ALU.subtract), reads=[p], writes=[dtm])
            kb.op(dve, lambda e: e.tensor_scalar(out=dtm[0:P, 0:1], in0=p[0:P, 0:1], scalar1=-1.0, scalar2=None,
                                                 op0=ALU.mult), reads=[p], writes=[dtm])
            p3 = p[0:P, TP:TT].rearrange("p (s t) -> p s t", t=TS)
            d3 = dtm[0:P, TP:TT].rearrange("p (s t) -> p s t", t=TS)
            kb.op(dve, lambda e: e.tensor_tensor(out=d3[:, :, 1:TS], in0=p3[:, :, 0:TS - 1], in1=p3[:, :, 1:TS],
                                                 op=ALU.subtract), reads=[p], writes=[dtm])
            kb.op(dve, lambda e: e.tensor_tensor(out=d3[:, :, 0:1],
                                                 in0=p[0:P, TT:TTX].rearrange("p (s o) -> p s o", o=1),
                                                 in1=p3[:, :, 0:1], op=ALU.subtract), reads=[p], writes=[dtm])
            kb.op(dve, lambda e: e.scalar_tensor_tensor(out=out_ap, in0=dtm[0:P, 0:TT], scalar=mus[0:P, mucol:mucol + 1],
                                                        in1=p[0:P, 0:TT], op0=ALU.mult, op1=ALU.add),
                  reads=[dtm, mus, p], writes=[out_T])
            if func is not None:
                kb.op(act, lambda e: e.activation(out=out_ap, in_=out_ap, func=func), reads=[out_T], writes=[out_T])

        tshift_full(5120, 96, 0, xw[0:96, :], xw, AF.Tanh)
        tshift_full(5216, 96, 1, xa[0:96, :], xa, None)
        tshift_full(5312, 128, 2, xg[:, 0, :], xg, AF.Sigmoid)
        tshift_full(5440, 128, 3, xg[:, 1, :], xg, AF.Sigmoid)
        k = 0
        for c12 in range(12):
            cs = slice(c12 * 128, (c12 + 1) * 128)
            for (t0, n) in tok_tiles_of(TT):
                ps = next_ps(g)
                kb.op(pe, lambda e: e.matmul(ps[:, 0:n], lhsT=w2[0:96, cs], rhs=xw[0:96, t0:t0 + n], start=True, stop=True),
                      reads=[w2, xw], writes=[ps])
                s_ = stg[k % 4]; k += 1
                kb.op(act, lambda e: e.activation(out=s_[:, 0:n], in_=ps[:, 0:n], func=AF.Sigmoid,
                                                  bias=w0[:, c12:c12 + 1], scale=1.0), reads=[ps, w0], writes=[s_])
                kb.op(dve, lambda e: e.tensor_scalar(out=s_[:, 0:n], in0=s_[:, 0:n], scalar1=LOG_SCALE, scalar2=None,
                                                     op0=ALU.mult), reads=[s_], writes=[s_])
                kb.dma(pool, S.auxT[0, cs, t0:t0 + n], s_[:, 0:n], reads=[s_])
                ps = next_ps(g)
                kb.op(pe, lambda e: e.matmul(ps[:, 0:n], lhsT=a2[0:96, cs], rhs=xa[0:96, t0:t0 + n], start=True, stop=True),
                      reads=[a2, xa], writes=[ps])
                s_ = stg[k % 4]; k += 1
                kb.op(act, lambda e: e.activation(out=s_[:, 0:n], in_=ps[:, 0:n], func=AF.Sigmoid,
                                                  bias=a0[:, c12:c12 + 1], scale=1.0), reads=[ps, a0], writes=[s_])
                kb.dma(pool, S.auxT[1, cs, t0:t0 + n], s_[:, 0:n], reads=[s_])
                ps = next_ps(g)
                for kc in range(2):
                    kb.op(pe, lambda e, kc=kc: e.matmul(ps[:, 0:n], lhsT=g2[:, kc, cs], rhs=xg[:, kc, t0:t0 + n],
                                                        start=(kc == 0), stop=(kc == 1)), reads=[g2, xg], writes=[ps])
                s_ = stg[k % 4]; k += 1
                kb.op(dve, lambda e: e.tensor_copy(out=s_[:, 0:n], in_=ps[:, 0:n]), reads=[ps], writes=[s_])
                kb.dma(pool, S.auxT[2, cs, t0:t0 + n], s_[:, 0:n], reads=[s_])
    kb.barrier()
    rwkv_scan(g)


def run_interleaved(a, b):
    gens = [x for x in (b, a) if x is not None]
    while gens:
        for x in list(gens):
            try:
                next(x)
            except StopIteration:
                gens.remove(x)


RSEGS = [(k * 256, 256, 64, 4, False) for k in range(8)] + [(TP, TSM, 8, NS, True)]


def rwkv_scan(g):
    nc, kb, I, O, S = g.nc, g.kb, g.I, g.O, g.S
    dve, act, pe, pool, sp = kb.dve, kb.act, kb.pe, kb.pool, kb.sp
    cst = g.cst
    g.ps_n = 7
    psO = g.ps[7]
    Rb = lambda ap: ap
    with ExitStack() as es:
        al = lambda name, shape, dt=F32: T(es.enter_context(nc.sbuf_tensor(uname(name), shape, dt)))
        vec = al("rw_vec", [128, 12, 8])
        for idx, src in enumerate([I.a_mu[0:1536, :], I.a_mu[1536:3072, :], I.a_mu[3072:4608, :], I.a_k_k, I.a_k_a,
                                   I.a_r_k, I.a_ln_w, I.a_ln_b]):
            pvec(g, vec[:, :, idx], src, vec)
        omk = al("rw_omk", [128, 12])
        kb.op(dve, lambda e: e.tensor_scalar(out=omk[:], in0=vec[:, :, 4], scalar1=-1.0, scalar2=1.0,
                                             op0=ALU.mult, op1=ALU.add), reads=[vec], writes=[omk])
        cr = al("rw_cr", [128, 3, 128], BF16)
        idb = al("rw_idb", [128, 128], BF16)
        kb.op(dve, lambda e: e.tensor_copy(out=idb[:], in_=cst[:, C_ID, :]), reads=[cst], writes=[idb])
        for i, cslot in enumerate((C_BLK, C_SEL64, C_SEL8)):
            kb.op(dve, lambda e: e.tensor_copy(out=Rb(cr[:, i, :]), in_=cst[:, cslot, :]), reads=[cst], writes=[cr])
        raw = [al(f"rw_raw{i}", [128, 288]) for i in range(3)]
        names = ["a", "ld", "gt", "d", "r", "k", "v", "kk", "b", "L", "Lm", "LCmL", "bon", "tmp", "eL", "enL", "eLm",
                 "eLC", "ynT", "o1", "tmpr", "d2", "d3", "kp", "ta", "tmpb"]
        Bsets = [{nm: al(f"rw_{nm}{q}", [128, 256], BF16 if nm in ("tmpr", "tmpb") else F32) for nm in names} for q in range(2)]
        ocat = [al(f"rw_ocat{q}", [128, 256], BF16) for q in range(2)]
        LCs = [al(f"rw_LC{q}", [128, 16]) for q in range(2)]
        GCs = [al(f"rw_GC{q}", [128, 16]) for q in range(2)]
        padn = ["Rt", "Kh", "Bh", "KK", "Kb", "Bb", "Vp", "KKr", "Rtr"]
        pdt = lambda nm: F32 if nm in ("KKr", "Rtr") else BF16
        padP = [{nm: al(f"rw_pp{nm}{q}", [128, 4, 128], pdt(nm)) for nm in padn} for q in range(2)]
        padS = [{nm: al(f"rw_ps{nm}{q}", [128, 24, 16], pdt(nm)) for nm in padn} for q in range(2)]
        zt = al("rw_zero", [128, 512])
        kb.op(pool, lambda e: e.memset(zt[:], 0.0), writes=[zt])
        for q in range(2):
            for nm in padn:
                kb.op(pool, lambda e: e.tensor_copy(out=(R if nm in ("KKr", "Rtr") else Rb)(padP[q][nm][:]), in_=zt[:, 0:512].rearrange("p (a b) -> p a b", b=128)),
                      reads=[zt], writes=[padP[q][nm]])
                kb.op(pool, lambda e: e.tensor_copy(out=(R if nm in ("KKr", "Rtr") else Rb)(padS[q][nm][:]), in_=zt[:, 0:384].rearrange("p (a b) -> p a b", b=16)),
                      reads=[zt], writes=[padS[q][nm]])
        un = ["A", "Bm", "AkkT", "ArkT", "ArbT", "Vbd", "Kbd", "Bbd", "AkkV", "P0", "P1", "Aa", "Ab", "Ba", "Bb2", "KKbd",
              "WT", "nU0"]
        UB = [{nm: al(f"rw_u{q}{nm}", [128, 4, 128], F32 if nm in ("WT", "nU0") else BF16) for nm in un} for q in range(2)]
        for q in range(2):
            for nm in un:
                if nm != "nU0":
                    kb.op(pool, lambda e: e.tensor_copy(out=(R if nm == "WT" else Rb)(UB[q][nm][:]), in_=zt[:, 0:512].rearrange("p (a b) -> p a b", b=128)),
                          reads=[zt], writes=[UB[q][nm]])
        X = [al(f"rw_X{i}", [128, 128], BF16) for i in range(2)]
        nU = [al(f"rw_nU{i}", [128, 128], BF16) for i in range(2)]
        Osb = al("rw_O", [128, 4, 128])
        yn = al("rw_yn", [128, 4, 128])
        ynr = al("rw_ynr", [128, 4, 128], BF16)
        junk = al("rw_junk", [128, 4, 128])
        Hs = [al(f"rw_H{i}", [128, 128]) for i in range(3)]
        Hpad = al("rw_Hpad", [128, 128])
        Ht = al("rw_Ht", [128, 128])
        sin1 = al("rw_sin", [128, NS, 64])
        sout1 = al("rw_sout", [128, NS, 64])
        sin = [sin1, sin1]
        sout = [sout1, sout1]
        st = [al(f"rw_st{k}", [128, 4]) for k in range(5)]
        kb.op(pool, lambda e: e.memset(Hpad[:], 0.0), writes=[Hpad])
        hst = {"H": None, "hi": 0, "sc": 0}

        def newH():
            h = Hs[hst["hi"] % 3]
            hst["hi"] += 1
            return h

        def genA(item):
            pr, gs, (t0, ntok, C, nch, is_s), bi, first, last, k = item[:7]
            n = 2 * C
            SL, SU, IU = (C_SL64, C_SU64, C_IU64) if C == 64 else (C_SL8, C_SU8, C_IU8)
            levels = 5 if C == 64 else 2
            Bq = Bsets[gs % 2]
            pad = (padS if is_s else padP)[gs % 2]
            LC, GC = LCs[gs % 2], GCs[gs % 2]
            U = UB[k % 2]
            rs_ = slice(pr * 128, (pr + 1) * 128)
            W = slice(0, ntok)
            if first:
                if is_s:
                    kb.dma(sp, sin[pr % 2][:], I.state_rwkv[:, 2 * pr:2 * pr + 2].rearrange("s h v k -> (h v) s k"),
                           writes=[sin[pr % 2]])
                r, k_, v, kk, b, a, ld, L, Lm, LCmL, bon, tmp, tmpr = (Bq[x] for x in ("r", "k", "v", "kk", "b", "a", "ld", "L",
                                                                                     "Lm", "LCmL", "bon", "tmp", "tmpr"))
                kp, ta = Bq["kp"], Bq["ta"]
                dd = [Bq["d"], Bq["d2"], Bq["d3"]]
                for xi, (nm, row0) in enumerate((("r", 512), ("k", 2048), ("v", 3584))):
                    rw = raw[xi]
                    rows = slice(row0 + pr * 128, row0 + (pr + 1) * 128)
                    if not is_s:
                        if t0 == 0:
                            kb.op(pool, lambda e: e.memset(rw[:, 0:1], 0.0), writes=[rw])
                            kb.dma(sp, rw[:, 1:ntok + 1], S.pT[rows, 0:ntok], writes=[rw])
                        else:
                            kb.dma(sp, rw[:, 0:ntok + 1], S.pT[rows, t0 - 1:t0 + ntok], writes=[rw])
                    else:
                        kb.dma(sp, rw[:, 0:TSM + NS], S.pT[rows, TP:TTX], writes=[rw])
                for ai, nm in ((1, "a"), (0, "ld"), (2, "gt")):
                    kb.dma(sp, Bq[nm][:, W], S.auxT[ai, rs_, t0:t0 + ntok], writes=[Bq[nm]])
                yield
                for xi in range(3):
                    rw, d = raw[xi], dd[xi]
                    if not is_s:
                        kb.op(dve, lambda e: e.tensor_tensor(out=d[:, W], in0=rw[:, 0:ntok], in1=rw[:, 1:ntok + 1],
                                                             op=ALU.subtract), reads=[rw], writes=[d])
                    else:
                        p3 = rw[:, 0:TSM].rearrange("p (s t) -> p s t", t=TS)
                        d3 = d[:, 0:TSM].rearrange("p (s t) -> p s t", t=TS)
                        kb.op(dve, lambda e: e.tensor_tensor(out=d3[:, :, 1:TS], in0=p3[:, :, 0:TS - 1],
                                                             in1=p3[:, :, 1:TS], op=ALU.subtract), reads=[rw], writes=[d])
                        kb.op(pool, lambda e: e.tensor_tensor(out=d3[:, :, 0:1],
                                                              in0=rw[:, TSM:TSM + NS].rearrange("p (s o) -> p s o", o=1),
                                                              in1=p3[:, :, 0:1], op=ALU.subtract), reads=[rw], writes=[d])
                yield
                for c in range(nch):
                    cs = slice(c * C, (c + 1) * C)
                    kb.op(dve, lambda e: e.tensor_tensor_scan(out=L[:, cs], data0=cst[:, C_ONES, 0:C], data1=ld[:, cs],
                                                              initial=0.0, op0=ALU.mult, op1=ALU.add),
                          reads=[cst, ld], writes=[L])
                    if c % 4 == 3:
                        yield
                kb.op(pool, lambda e: e.tensor_scalar(out=ta[:, W], in0=a[:, W], scalar1=vec[:, pr, 4:5],
                                                      scalar2=omk[:, pr:pr + 1], op0=ALU.mult, op1=ALU.add),
                      reads=[a, vec, omk], writes=[ta])
                for xi, nm in enumerate(("r", "k", "v")):
                    rw, d, dst = raw[xi], dd[xi], Bq[nm]
                    src = rw[:, 1:ntok + 1] if not is_s else rw[:, 0:TSM]
                    kb.op(dve, lambda e: e.scalar_tensor_tensor(out=dst[:, W], in0=d[:, W], scalar=vec[:, pr, xi:xi + 1],
                                                                in1=src, op0=ALU.mult, op1=ALU.add),
                          reads=[d, vec, rw], writes=[dst])
                yield
                L3 = L[:, W].rearrange("p (c t) -> p c t", t=C)
                kb.op(dve, lambda e: e.tensor_copy(out=LC[:, 0:nch], in_=L3[:, :, C - 1]), reads=[L], writes=[LC])
                kb.op(pool, lambda e: e.tensor_tensor(out=Lm[:, W], in0=L[:, W], in1=ld[:, W], op=ALU.subtract),
                      reads=[L, ld], writes=[Lm])
                kb.op(act, lambda e: e.activation(out=Bq["eL"][:, W], in_=L[:, W], func=AF.Exp), reads=[L], writes=[Bq["eL"]])
                kb.op(act, lambda e: e.activation(out=Bq["enL"][:, W], in_=L[:, W], func=AF.Exp, scale=-1.0),
                      reads=[L], writes=[Bq["enL"]])
                yield
                kb.op(dve, lambda e: e.tensor_scalar(out=kk[:, W], in0=k_[:, W], scalar1=vec[:, pr, 3:4], scalar2=None,
                                                     op0=ALU.mult), reads=[k_, vec], writes=[kk])
                kb.op(dve, lambda e: e.tensor_tensor(out=LCmL[:, W].rearrange("p (c t) -> p c t", t=C),
                                                     in0=LC[:, 0:nch].unsqueeze(2).to_broadcast([128, nch, C]),
                                                     in1=L3, op=ALU.subtract), reads=[L, LC], writes=[LCmL])
                kb.op(act, lambda e: e.activation(out=GC[:, 0:nch], in_=LC[:, 0:nch], func=AF.Exp), reads=[LC], writes=[GC])
                kb.op(act, lambda e: e.activation(out=Bq["eLm"][:, W], in_=Lm[:, W], func=AF.Exp), reads=[Lm], writes=[Bq["eLm"]])
                yield
                kb.op(dve, lambda e: e.tensor_tensor(out=tmpr[:, W], in0=kk[:, W], in1=kk[:, W], op=ALU.mult),
                      reads=[kk], writes=[tmpr])
                ps_ss = next_ps(g)
                kb.op(pe, lambda e: e.matmul(ps_ss[:, W], lhsT=cr[:, 0, :], rhs=tmpr[:, W], start=True, stop=True),
                      reads=[cr, tmpr], writes=[ps_ss])
                kb.op(dve, lambda e: e.tensor_tensor(out=kp[:, W], in0=k_[:, W], in1=ta[:, W], op=ALU.mult),
                      reads=[k_, ta], writes=[kp])
                kb.op(act, lambda e: e.activation(out=Bq["eLC"][:, W], in_=LCmL[:, W], func=AF.Exp),
                      reads=[LCmL], writes=[Bq["eLC"]])
                yield

                def padmul(dn, an, bn, eng_pair):
                    for h in range(2):
                        hs = slice(h * 64, (h + 1) * 64)
                        o_ap = (R if dn in ("KKr", "Rtr") else Rb)(pad[dn][hs, 0:nch, h * C:(h + 1) * C])
                        i0 = Bq[an][hs, W].rearrange("p (c t) -> p c t", t=C)
                        eng = eng_pair[h]
                        if bn is None:
                            kb.op(eng, lambda e: e.tensor_copy(out=o_ap, in_=i0), reads=[Bq[an]], writes=[pad[dn]])
                        else:
                            i1 = Bq[bn][hs, W].rearrange("p (c t) -> p c t", t=C)
                            kb.op(eng, lambda e: e.tensor_tensor(out=o_ap, in0=i0, in1=i1, op=ALU.mult),
                                  reads=[Bq[an], Bq[bn]], writes=[pad[dn]])
                padmul("Rt", "r", "eL", (dve, pool))
                padmul("Rtr", "r", "eL", (pool, dve))
                yield
                padmul("Vp", "v", None, (pool, pool))
                padmul("Kh", "kp", "enL", (dve, pool))
                yield
                padmul("Kb", "kp", "eLC", (pool, dve))
                tmpb = Bq["tmpb"]
                kb.op(dve, lambda e: e.scalar_tensor_tensor(out=tmpb[:, W], in0=r[:, W], scalar=vec[:, pr, 5:6],
                                                            in1=kp[:, W], op0=ALU.mult, op1=ALU.mult),
                      reads=[r, vec, kp], writes=[tmpb])
                ps_b = next_ps(g)
                kb.op(pe, lambda e: e.matmul(ps_b[:, W], lhsT=cr[:, 0, :], rhs=tmpb[:, W], start=True, stop=True),
                      reads=[cr, tmpb], writes=[ps_b])
                yield
                kb.op(act, lambda e: e.sqrt(out=tmp[:, W], in_=ps_ss[:, W]), reads=[ps_ss], writes=[tmp])
                kb.op(dve, lambda e: e.tensor_tensor(out=bon[:, W], in0=ps_b[:, W], in1=v[:, W], op=ALU.mult),
                      reads=[ps_b, v], writes=[bon])
                kb.op(dve, lambda e: e.tensor_scalar_max(out=tmp[:, W], in0=tmp[:, W], scalar1=1e-12), reads=[tmp], writes=[tmp])
                yield
                kb.op(dve, lambda e: e.reciprocal(out=tmp[:, W], in_=tmp[:, W]), reads=[tmp], writes=[tmp])
                yield
                kb.op(dve, lambda e: e.tensor_tensor(out=kk[:, W], in0=kk[:, W], in1=tmp[:, W], op=ALU.mult),
                      reads=[kk, tmp], writes=[kk])
                yield
                kb.op(dve, lambda e: e.tensor_tensor(out=b[:, W], in0=kk[:, W], in1=a[:, W], op=ALU.mult),
                      reads=[kk, a], writes=[b])
                padmul("KK", "kk", "eLm", (pool, dve))
                yield
                padmul("Bh", "b", "enL", (dve, pool))
                padmul("Bb", "b", "eLC", (pool, dve))
                yield
            units = list(range(bi * 4, bi * 4 + 4))

            def padl(nm, c):
                return pad[nm][:, :, :].rearrange("p c j -> p (c j)")[:, c * n:c * n + 128]

            def mm4(lhs_fn, rhs_fn, wcols, reads):
                ps = next_ps(g)
                for u, c in enumerate(units):
                    kb.op(pe, lambda e: e.matmul(ps[0:128, u * 128:u * 128 + wcols], lhsT=lhs_fn(u, c), rhs=rhs_fn(u, c),
                                                 start=True, stop=True), reads=reads, writes=[ps])
                return ps

            def ps3d(ps, wcols):
                return ps[0:n, 0:512].rearrange("p (u j) -> p u j", j=128)[:, :, 0:wcols]

            for (nm, ln, rn, mk) in (("A", "KK", "Bh", SL), ("Bm", "Bh", "KK", SU), ("AkkT", "Kh", "KK", SU),
                                     ("ArkT", "Kh", "Rt", IU), ("ArbT", "Bh", "Rt", IU)):
                ps = mm4(lambda u, c: Rb(padl(ln, c)), lambda u, c: Rb(pad[rn][:, c, 0:n]), n, [pad[ln], pad[rn]])
                kb.op(dve, lambda e: e.tensor_tensor(out=Rb(U[nm][0:n, :, 0:n]), in0=ps3d(ps, n),
                                                     in1=cst[0:n, mk, 0:n].unsqueeze(1).to_broadcast([n, 4, n]),
                                                     op=ALU.mult), reads=[ps, cst], writes=[U[nm]])
                yield
            for (nm, pn_) in (("Vbd", "Vp"), ("Kbd", "Kb"), ("Bbd", "Bb"), ("KKbd", "KK")):
                ps = next_ps(g)
                psb = ps[:, :].bitcast(BF16)
                for u, c in enumerate(units):
                    kb.op(pe, lambda e: e.transpose(out=psb[0:n, u * 128:(u + 1) * 128], in_=pad[pn_][:, c, 0:n],
                                                    identity=idb[:, :]), reads=[pad[pn_], idb], writes=[ps])
                kb.op(act, lambda e: e.copy(out=U[nm][0:n, :, :],
                                            in_=psb[0:n, 0:512].rearrange("p (u j) -> p u j", j=128)),
                      reads=[ps], writes=[U[nm]])
                yield
            ps = mm4(lambda u, c: Rb(U["AkkT"][0:n, u, 0:128]), lambda u, c: Rb(U["Vbd"][0:n, u, :]), 128, [U["AkkT"], U["Vbd"]])
            kb.op(act, lambda e: e.copy(out=U["AkkV"][0:n, :, :], in_=ps3d(ps, 128)), reads=[ps], writes=[U["AkkV"]])
            kb.op(dve, lambda e: e.tensor_tensor(out=Rb(U["P0"][0:n, :, 0:n]),
                                                 in0=cst[0:n, C_ID, 0:n].unsqueeze(1).to_broadcast([n, 4, n]),
                                                 in1=U["Bm"][0:n, :, 0:n], op=ALU.subtract),
                  reads=[cst, U["Bm"]], writes=[U["P0"]])
            yield
            Aj, Bj, Pj = "A", "Bm", "P0"
            for lev in range(levels):
                An = "Aa" if lev % 2 == 0 else "Ab"
                Bn = "Ba" if lev % 2 == 0 else "Bb2"
                Pn = "P1" if lev % 2 == 0 else "P0"
                ps_a = mm4(lambda u, c: Rb(U[Bj][0:n, u, 0:128]), lambda u, c: Rb(U[Aj][0:n, u, 0:n]), n, [U[Bj], U[Aj]])
                if lev < levels - 1:
                    ps_b = mm4(lambda u, c: Rb(U[Aj][0:n, u, 0:128]), lambda u, c: Rb(U[Bj][0:n, u, 0:n]), n, [U[Bj], U[Aj]])
                kb.op(act, lambda e: e.copy(out=Rb(U[An][0:n, :, 0:n]), in_=ps3d(ps_a, n)), reads=[ps_a], writes=[U[An]])
                if lev < levels - 1:
                    kb.op(dve, lambda e: e.tensor_copy(out=Rb(U[Bn][0:n, :, 0:n]), in_=ps3d(ps_b, n)), reads=[ps_b], writes=[U[Bn]])
                yield
                ps = mm4(lambda u, c: Rb(U[An][0:n, u, 0:128]), lambda u, c: Rb(U[Pj][0:n, u, 0:n]), n, [U[An], U[Pj]])
                kb.op(dve, lambda e: e.tensor_tensor(out=Rb(U[Pn][0:n, :, 0:n]), in0=ps3d(ps, n), in1=U[Pj][0:n, :, 0:n],
                                                     op=ALU.add), reads=[ps, U[Pj]], writes=[U[Pn]])
                yield
                Aj, Bj, Pj = An, Bn, Pn
            item[-1]["P"] = Pj
            ps = mm4(lambda u, c: Rb(U["KKbd"][0:n, u, 0:128]), lambda u, c: Rb(U[Pj][0:n, u, 0:n]), n, [U["KKbd"], U[Pj]])
            kb.op(act, lambda e: e.copy(out=R(U["WT"][:, :, 0:n]),
                                        in_=ps[:, 0:512].rearrange("p (u j) -> p u j", j=128)[:, :, 0:n]),
                  reads=[ps], writes=[U["WT"]])
            yield
            ps = mm4(lambda u, c: Rb(U[Pj][0:n, u, 0:128]), lambda u, c: Rb(U["AkkV"][0:n, u, :]), 128, [U[Pj], U["AkkV"]])
            kb.op(act, lambda e: e.mul(out=U["nU0"][0:n, :, :], in_=ps3d(ps, 128), mul=-1.0), reads=[ps], writes=[U["nU0"]])
            yield

        def genB(item):
            pr, gs, (t0, ntok, C, nch, is_s), bi, first, last, k = item[:7]
            Pj = item[-1]["P"]
            n = 2 * C
            BLKm = C_BLK if C == 64 else C_BLK8
            SELi = 1 if C == 64 else 2
            Bq = Bsets[gs % 2]
            pad = (padS if is_s else padP)[gs % 2]
            GC = GCs[gs % 2]
            U = UB[k % 2]
            rs_ = slice(pr * 128, (pr + 1) * 128)
            W = slice(0, ntok)
            units = list(range(bi * 4, bi * 4 + 4))

            def padl(nm, c):
                return pad[nm][:, :, :].rearrange("p c j -> p (c j)")[:, c * n:c * n + 128]
            ynT = Bq["ynT"]
            if first and (not is_s) and t0 == 0:
                h0 = newH()
                kb.op(dve, lambda e: e.tensor_copy(out=R(h0[:]), in_=zt[:, 0:128]), reads=[zt], writes=[h0])
                hst["H"] = h0
            for u, c in enumerate(units):
                if is_s:
                    for h in range(2):
                        hs = slice(h * 64, (h + 1) * 64)
                        kb.op(dve, lambda e: e.tensor_copy(out=Hpad[hs, hs], in_=sin[pr % 2][hs, c, :]),
                              reads=[sin[pr % 2]], writes=[Hpad])
                    ps = next_ps(g)
                    kb.op(pe, lambda e: e.transpose(out=ps[:, 0:128], in_=Hpad[:, :], identity=cst[:, C_ID, :]),
                          reads=[Hpad, cst], writes=[ps])
                    hn = newH()
                    kb.op(act, lambda e: e.copy(out=R(hn[:, :]), in_=ps[:, 0:128]), reads=[ps], writes=[hn])
                    hst["H"] = hn
                    yield
                Hcur = hst["H"]
                q = hst["sc"] % 2
                hst["sc"] += 1
                Xq, nUq = X[q], nU[q]
                ps4 = next_ps(g)
                kb.op(pe, lambda e: e.matmul(ps4[:, 0:128], lhsT=Rb(U["Kbd"][0:n, u, :]), rhs=Rb(U["Vbd"][0:n, u, :]),
                                             start=True, stop=False), reads=[U["Kbd"], U["Vbd"]], writes=[ps4])
                ps = next_ps(g)
                kb.op(pe, lambda e: e.matmul(ps[0:128, 0:128], lhsT=R(U["WT"][:, u, 0:128]), rhs=R(Hcur[:, :]),
                                             start=True, stop=True), reads=[U["WT"], Hcur], writes=[ps])
                kb.op(dve, lambda e: e.scalar_tensor_tensor(out=Rb(nUq[0:n, :]), in0=ps[0:n, 0:128], scalar=-1.0,
                                                            in1=U["nU0"][0:n, u, :], op0=ALU.mult, op1=ALU.add),
                      reads=[ps, U["nU0"]], writes=[nUq])
                yield
                kb.op(pe, lambda e: e.matmul(ps4[:, 0:128], lhsT=Rb(U["Bbd"][0:n, u, :]), rhs=Rb(nUq[0:n, :]),
                                             start=False, stop=True), reads=[U["Bbd"], nUq], writes=[ps4])
                Hn = newH()
                kb.op(dve, lambda e: e.scalar_tensor_tensor(out=R(Hn[:, :]), in0=Hcur[:, :], scalar=GC[:, c:c + 1],
                                                            in1=ps4[:, 0:128], op0=ALU.mult, op1=ALU.add),
                      reads=[Hcur, GC, ps4], writes=[Hn])
                hst["H"] = Hn
                yield
                oc = slice(u * 128, (u + 1) * 128)
                kb.op(pe, lambda e: e.matmul(psO[0:128, oc], lhsT=R(padl("Rtr", c)), rhs=R(Hcur[:, :]),
                                             start=True, stop=False), reads=[pad["Rtr"], Hcur], writes=[psO])
                kb.op(pe, lambda e: e.matmul(psO[0:128, oc], lhsT=Rb(U["ArkT"][0:n, u, 0:128]), rhs=Rb(U["Vbd"][0:n, u, :]),
                                             start=False, stop=False), reads=[U["ArkT"], U["Vbd"]], writes=[psO])
                kb.op(pe, lambda e: e.matmul(psO[0:128, oc], lhsT=Rb(U["ArbT"][0:n, u, 0:128]), rhs=Rb(nUq[0:n, :]),
                                             start=False, stop=True), reads=[U["ArbT"], nUq], writes=[psO])

                if is_s:
                    ps = next_ps(g)
                    kb.op(pe, lambda e: e.transpose(out=ps[:, 0:128], in_=Hn[:, :], identity=cst[:, C_ID, :]),
                          reads=[Hn, cst], writes=[ps])
                    kb.op(act, lambda e: e.copy(out=Ht[:, :], in_=ps[:, 0:128]), reads=[ps], writes=[Ht])
                    for h in range(2):
                        hs = slice(h * 64, (h + 1) * 64)
                        kb.op(dve, lambda e: e.tensor_copy(out=sout[pr % 2][hs, c, :], in_=Ht[hs, hs]),
                              reads=[Ht], writes=[sout[pr % 2]])
                    yield
            s1, s2, mean, var, msq = st
            kb.op(act, lambda e: e.copy(out=Osb[0:n, :, :], in_=psO[0:n, 0:512].rearrange("p (u j) -> p u j", j=128)),
                  reads=[psO], writes=[Osb])
            kb.op(dve, lambda e: e.tensor_reduce(out=s1[0:n, :], in_=Osb[0:n, :, :], axis=AX.X, op=ALU.add),
                  reads=[Osb], writes=[s1])
            kb.op(act, lambda e: e.activation(out=junk[0:n, :, :], in_=Osb[0:n, :, :], func=AF.Square),
                  reads=[Osb], writes=[junk])
            kb.op(dve, lambda e: e.tensor_reduce(out=s2[0:n, :], in_=junk[0:n, :, :], axis=AX.X, op=ALU.add),
                  reads=[junk], writes=[s2])
            yield
            kb.op(dve, lambda e: e.tensor_scalar(out=mean[0:n, :], in0=s1[0:n, :], scalar1=1.0 / 64, scalar2=None,
                                                 op0=ALU.mult), reads=[s1], writes=[mean])
            kb.op(dve, lambda e: e.tensor_tensor(out=msq[0:n, :], in0=mean[0:n, :], in1=mean[0:n, :], op=ALU.mult),
                  reads=[mean], writes=[msq])
            kb.op(dve, lambda e: e.scalar_tensor_tensor(out=var[0:n, :], in0=s2[0:n, :], scalar=1.0 / 64,
                                                        in1=msq[0:n, :], op0=ALU.mult, op1=ALU.subtract),
                  reads=[s2, msq], writes=[var])
            kb.op(dve, lambda e: e.tensor_scalar_add(out=var[0:n, :], in0=var[0:n, :], scalar1=GN_EPS),
                  reads=[var], writes=[var])
            kb.op(act, lambda e: e.sqrt(out=var[0:n, :], in_=var[0:n, :]), reads=[var], writes=[var])
            kb.op(dve, lambda e: e.reciprocal(out=var[0:n, :], in_=var[0:n, :]), reads=[var], writes=[var])
            yield
            kb.op(dve, lambda e: e.tensor_tensor(out=yn[0:n, :, :], in0=Osb[0:n, :, :],
                                                 in1=mean[0:n, :].unsqueeze(2).to_broadcast([n, 4, 128]), op=ALU.subtract),
                  reads=[Osb, mean], writes=[yn])
            kb.op(dve, lambda e: e.tensor_tensor(out=yn[0:n, :, :], in0=yn[0:n, :, :],
                                                 in1=var[0:n, :].unsqueeze(2).to_broadcast([n, 4, 128]), op=ALU.mult),
                  reads=[yn, var], writes=[yn])
            kb.op(dve, lambda e: e.tensor_tensor(out=Rb(ynr[0:n, :, :]), in0=yn[0:n, :, :],
                                                 in1=cst[0:n, BLKm, :].unsqueeze(1).to_broadcast([n, 4, 128]), op=ALU.mult),
                  reads=[yn, cst], writes=[ynr])
            psF = next_ps(g)
            for u, c in enumerate(units):
                kb.op(pe, lambda e: e.matmul(psF[:, u * C:(u + 1) * C], lhsT=Rb(ynr[0:n, u, :]), rhs=Rb(cr[0:n, SELi, 0:C]),
                                             start=True, stop=True), reads=[ynr, cr], writes=[psF])
            kb.op(act, lambda e: e.copy(out=ynT[:, bi * 4 * C:(bi * 4 + 4) * C], in_=psF[:, 0:4 * C]),
                  reads=[psF], writes=[ynT])
            yield
            if last:
                o1 = Bq["o1"]
                oc_ = ocat[gs % 2]
                kb.op(dve, lambda e: e.tensor_scalar(out=o1[:, W], in0=ynT[:, W], scalar1=vec[:, pr, 6:7],
                                                     scalar2=vec[:, pr, 7:8], op0=ALU.mult, op1=ALU.add),
                      reads=[ynT, vec], writes=[o1])
                kb.op(dve, lambda e: e.tensor_tensor(out=o1[:, W], in0=o1[:, W], in1=Bq["bon"][:, W], op=ALU.add),
                      reads=[o1, Bq["bon"]], writes=[o1])
                kb.op(dve, lambda e: e.tensor_tensor(out=oc_[:, W], in0=o1[:, W], in1=Bq["gt"][:, W], op=ALU.mult),
                      reads=[o1, Bq["gt"]], writes=[oc_])
                kb.dma(pool, S.catT[rs_, t0:t0 + ntok], oc_[:, W], reads=[oc_])
                if (not is_s) and t0 + ntok == TP:
                    Hcur = hst["H"]
                    ps = next_ps(g)
                    kb.op(pe, lambda e: e.transpose(out=ps[:, 0:128], in_=Hcur[:, :], identity=cst[:, C_ID, :]),
                          reads=[Hcur, cst], writes=[ps])
                    kb.op(act, lambda e: e.copy(out=Ht[:, :], in_=ps[:, 0:128]), reads=[ps], writes=[Ht])
                    for h in range(2):
                        hs = slice(h * 64, (h + 1) * 64)
                        kb.dma(pool, O.rwkv_prompt[pr * 128 + h * 64:pr * 128 + (h + 1) * 64, :], Ht[hs, hs], reads=[Ht])
                if is_s:
                    kb.dma(pool, O.rwkv_sample[:, rs_, :].rearrange("s r k -> r s k"), sout[pr % 2][:],
                           reads=[sout[pr % 2]])
                yield

        pending = None
        gs = 0
        k = 0
        for pr in range(12):
            for seg in RSEGS:
                nb = seg[3] // 4
                for bi in range(nb):
                    item = (pr, gs, seg, bi, bi == 0, bi == nb - 1, k, {})
                    run_interleaved(genA(item), pending)
                    pending = genB(item)
                    k += 1
                gs += 1
        run_interleaved(None, pending)
    g.ps_n = 8


HSEGS = [(k * 256, 256, 64, 4, False) for k in range(8)] + [(TP, TSM, 8, NS, True)]


def phase_hgrn(g):
    nc, kb, I, O, S = g.nc, g.kb, g.I, g.O, g.S
    dve, act, pe, pool, sp = kb.dve, kb.act, kb.pe, kb.pool, kb.sp
    cst = g.cst
    g.ps_n = 7
    psO = g.ps[7]
    with ExitStack() as es:
        al = lambda name, shape, dt=F32: T(es.enter_context(nc.sbuf_tensor(uname(name), shape, dt)))
        lbr = al("hg_lbr", [128, 2, 12])
        lb = al("hg_lb", [128, 12])
        oml = al("hg_oml", [128, 12])
        gn = al("hg_gn", [128, 1])
        onesr = al("hg_ones", [128, 128])
        zt = al("hg_zero", [128, 128])
        kb.op(pool, lambda e: e.memset(zt[:], 0.0), writes=[zt])
        kb.op(dve, lambda e: e.tensor_copy(out=R(onesr[:]), in_=cst[:, C_ONES, :]), reads=[cst], writes=[onesr])
        kb.dma(sp, lbr[:], I.b_lb.rearrange("l (c p) -> p l c", p=128), writes=[lbr], allow_slow_non_contiguous=True)
        kb.dma(sp, gn[:], I.b_g_norm[:, :], writes=[gn])
        kb.op(dve, lambda e: e.tensor_tensor(out=lb[:], in0=lbr[:, 1, :], in1=lbr[:, 0, :], op=ALU.subtract),
              reads=[lbr], writes=[lb])
        kb.op(act, lambda e: e.activation(out=lb[:], in_=lb[:], func=AF.Sigmoid), reads=[lb], writes=[lb])
        kb.op(dve, lambda e: e.tensor_scalar(out=oml[:], in0=lb[:], scalar1=-1.0, scalar2=1.0, op0=ALU.mult, op1=ALU.add),
              reads=[lb], writes=[oml])
        names = ["q", "f", "i", "og", "sq", "kk", "lf", "L", "LmM", "LCmL", "e1", "e2", "Qt", "Kt", "Qh", "Kb", "oseg",
                 "t1", "t1r"]
        Bs = [{nm: al(f"hg_{nm}{i}", [128, 384 if nm == "Kt" else 256], BF16 if nm in ("Kt", "Qt") else F32) for nm in names} for i in range(2)]
        for i in range(2):
            for j3 in range(3):
                kb.op(dve, lambda e: e.tensor_copy(out=Bs[i]["Kt"][:, j3 * 128:(j3 + 1) * 128], in_=zt[:]),
                      reads=[zt], writes=[Bs[i]["Kt"]])
        ocat = [al(f"hg_ocat{i}", [128, 256], BF16) for i in range(2)]
        LCs = [al(f"hg_LC{i}", [128, 16]) for i in range(2)]
        MDs = [al(f"hg_MD{i}", [128, 16]) for i in range(2)]
        GCs = [al(f"hg_GC{i}", [128, 16]) for i in range(2)]
        attT = [al(f"hg_att{i}", [64, 4, 64], BF16) for i in range(2)]
        VK = [al(f"hg_vk{i}", [64, 8, 128], BF16) for i in range(2)]
        Ss = [al(f"hg_S{i}", [128, 128]) for i in range(3)]
        sin = [al(f"hg_sin{i}", [128, NS, 128]) for i in range(2)]
        sout = [al(f"hg_sout{i}", [128, NS, 128]) for i in range(2)]
        hst = {"S": None, "si": 0}

        def newS():
            t_ = Ss[hst["si"] % 3]
            hst["si"] += 1
            return t_

        def genA(item):
            hd, gs, (t0, ntok, C, nch, is_s), bi, first, last, k = item[:7]
            W = slice(0, ntok)
            HIU = C_HIU64 if C == 64 else C_HIU8
            mid = (C - 1) // 2
            Bq = Bs[gs % 2]
            LC, MD, GC = LCs[gs % 2], MDs[gs % 2], GCs[gs % 2]
            q, f, iv, og, sq, kk, lf, L, LmM, LCmL, e1, e2, Qt, Kt, Qh, Kb, oseg, t1, t1r = (Bq[x] for x in names)
            if first:
                if is_s:
                    kb.dma(sp, sin[hd % 2][:], I.state_hgrn[:, hd].rearrange("s k v -> k s v"), writes=[sin[hd % 2]])
                for nm, row0 in (("q", 512), ("f", 2048), ("i", 3584), ("og", 5120)):
                    kb.dma(sp, Bq[nm][:, W], S.pT[row0 + hd * 128:row0 + (hd + 1) * 128, t0:t0 + ntok], writes=[Bq[nm]])
                kb.op(act, lambda e: e.activation(out=sq[:, W], in_=q[:, W], func=AF.Silu), reads=[q], writes=[sq])
                kb.op(act, lambda e: e.activation(out=og[:, W], in_=og[:, W], func=AF.Silu), reads=[og], writes=[og])
                kb.op(act, lambda e: e.activation(out=f[:, W], in_=f[:, W], func=AF.Sigmoid), reads=[f], writes=[f])
                yield
                kb.op(dve, lambda e: e.tensor_scalar(out=f[:, W], in0=f[:, W], scalar1=oml[:, hd:hd + 1],
                                                     scalar2=lb[:, hd:hd + 1], op0=ALU.mult, op1=ALU.add),
                      reads=[f, oml, lb], writes=[f])
                kb.op(dve, lambda e: e.tensor_scalar(out=kk[:, W], in0=f[:, W], scalar1=-1.0, scalar2=1.0,
                                                     op0=ALU.mult, op1=ALU.add), reads=[f], writes=[kk])
                kb.op(act, lambda e: e.activation(out=lf[:, W], in_=f[:, W], func=AF.Ln), reads=[f], writes=[lf])
                yield
                for c in range(nch):
                    cs = slice(c * C, (c + 1) * C)
                    kb.op(dve, lambda e: e.tensor_tensor_scan(out=L[:, cs], data0=cst[:, C_ONES, 0:C], data1=lf[:, cs],
                                                              initial=0.0, op0=ALU.mult, op1=ALU.add),
                          reads=[cst, lf], writes=[L])
                    if c % 4 == 3:
                        yield
                L3 = L[:, W].rearrange("p (c t) -> p c t", t=C)
                kb.op(dve, lambda e: e.tensor_copy(out=LC[:, 0:nch], in_=L3[:, :, C - 1]), reads=[L], writes=[LC])
                kb.op(dve, lambda e: e.tensor_copy(out=MD[:, 0:nch], in_=L3[:, :, mid]), reads=[L], writes=[MD])
                kb.op(dve, lambda e: e.tensor_tensor(out=LmM[:, W].rearrange("p (c t) -> p c t", t=C), in0=L3,
                                                     in1=MD[:, 0:nch].unsqueeze(2).to_broadcast([128, nch, C]),
                                                     op=ALU.subtract), reads=[L, MD], writes=[LmM])
                kb.op(dve, lambda e: e.tensor_tensor(out=LCmL[:, W].rearrange("p (c t) -> p c t", t=C),
                                                     in0=LC[:, 0:nch].unsqueeze(2).to_broadcast([128, nch, C]),
                                                     in1=L3, op=ALU.subtract), reads=[L, LC], writes=[LCmL])
                yield
                kb.op(act, lambda e: e.activation(out=GC[:, 0:nch], in_=LC[:, 0:nch], func=AF.Exp), reads=[LC], writes=[GC])
                kb.op(act, lambda e: e.activation(out=e1[:, W], in_=LmM[:, W], func=AF.Exp), reads=[LmM], writes=[e1])
                kb.op(dve, lambda e: e.tensor_tensor(out=Qt[:, W], in0=sq[:, W], in1=e1[:, W], op=ALU.mult),
                      reads=[sq, e1], writes=[Qt])
                kb.op(act, lambda e: e.activation(out=e2[:, W], in_=LmM[:, W], func=AF.Exp, scale=-1.0), reads=[LmM], writes=[e2])
                kb.op(dve, lambda e: e.tensor_tensor(out=Kt[:, W], in0=kk[:, W], in1=e2[:, W], op=ALU.mult),
                      reads=[kk, e2], writes=[Kt])
                yield
                kb.op(act, lambda e: e.activation(out=e1[:, W], in_=L[:, W], func=AF.Exp), reads=[L], writes=[e1])
                kb.op(dve, lambda e: e.tensor_tensor(out=R(Qh[:, W]), in0=sq[:, W], in1=e1[:, W], op=ALU.mult),
                      reads=[sq, e1], writes=[Qh])
                kb.op(act, lambda e: e.activation(out=e2[:, W], in_=LCmL[:, W], func=AF.Exp), reads=[LCmL], writes=[e2])
                kb.op(dve, lambda e: e.tensor_tensor(out=Kb[:, W], in0=kk[:, W], in1=e2[:, W], op=ALU.mult),
                      reads=[kk, e2], writes=[Kb])
                yield
            units = list(range(bi * 4, bi * 4 + 4))
            at, vk = attT[k % 2], VK[k % 2]
            ps = next_ps(g)
            for u, c in enumerate(units):
                cs = slice(c * C, (c + 1) * C)
                kb.op(pe, lambda e: e.matmul(ps[0:128, u * 64:u * 64 + C], lhsT=Kt[:, c * C:c * C + 128], rhs=Qt[:, cs],
                                             start=True, stop=True), reads=[Kt, Qt], writes=[ps])
            kb.op(dve, lambda e: e.tensor_tensor(out=at[0:C, :, 0:C],
                                                 in0=ps[0:C, 0:256].rearrange("p (u j) -> p u j", j=64)[:, :, 0:C],
                                                 in1=cst[0:C, HIU, 0:C].unsqueeze(1).to_broadcast([C, 4, C]), op=ALU.mult),
                  reads=[ps, cst], writes=[at])
            yield
            for half in range(2):
                ps = next_ps(g)
                for uu in range(2):
                    u = half * 2 + uu
                    cs = slice(units[u] * C, (units[u] + 1) * C)
                    kb.op(pe, lambda e: e.transpose(out=ps[0:C, (uu * 2) * 128:(uu * 2 + 1) * 128], in_=iv[:, cs],
                                                    identity=cst[:, C_ID, :]), reads=[iv, cst], writes=[ps])
                    kb.op(pe, lambda e: e.transpose(out=ps[0:C, (uu * 2 + 1) * 128:(uu * 2 + 2) * 128], in_=Kb[:, cs],
                                                    identity=cst[:, C_ID, :]), reads=[Kb, cst], writes=[ps])
                kb.op(act, lambda e: e.copy(out=vk[0:C, half * 4:half * 4 + 4, :],
                                            in_=ps[0:C, 0:512].rearrange("p (u j) -> p u j", j=128)),
                      reads=[ps], writes=[vk])
                yield

        def genB(item):
            hd, gs, (t0, ntok, C, nch, is_s), bi, first, last, k = item[:7]
            W = slice(0, ntok)
            rs_ = slice(hd * 128, (hd + 1) * 128)
            Bq = Bs[gs % 2]
            GC = GCs[gs % 2]
            Qh, oseg, t1, t1r, og = Bq["Qh"], Bq["oseg"], Bq["t1"], Bq["t1r"], Bq["og"]
            units = list(range(bi * 4, bi * 4 + 4))
            at, vk = attT[k % 2], VK[k % 2]
            if first and (not is_s) and t0 == 0:
                s0 = newS()
                kb.op(dve, lambda e: e.tensor_copy(out=R(s0[:]), in_=zt[:]), reads=[zt], writes=[s0])
                hst["S"] = s0
            for u, c in enumerate(units):
                cs = slice(c * C, (c + 1) * C)
                if is_s:
                    sn = newS()
                    kb.op(dve, lambda e: e.tensor_copy(out=R(sn[:, :]), in_=sin[hd % 2][:, c, :]),
                          reads=[sin[hd % 2]], writes=[sn])
                    hst["S"] = sn
                Scur = hst["S"]
                kb.op(pe, lambda e: e.matmul(psO[:, u * C:(u + 1) * C], lhsT=vk[0:C, u * 2, :], rhs=at[0:C, u, 0:C],
                                             start=True, stop=False), reads=[vk, at], writes=[psO])
                kb.op(pe, lambda e: e.matmul(psO[:, u * C:(u + 1) * C], lhsT=R(Scur[:, :]), rhs=R(Qh[:, cs]),
                                             start=False, stop=True), reads=[Scur, Qh], writes=[psO])
                ps = next_ps(g)
                kb.op(pe, lambda e: e.matmul(ps[:, 0:128], lhsT=vk[0:C, u * 2 + 1, :], rhs=vk[0:C, u * 2, :],
                                             start=True, stop=True), reads=[vk], writes=[ps])
                if is_s:
                    kb.op(dve, lambda e: e.scalar_tensor_tensor(out=sout[hd % 2][:, c, :], in0=Scur[:, :],
                                                                scalar=GC[:, c:c + 1], in1=ps[:, 0:128],
                                                                op0=ALU.mult, op1=ALU.add),
                          reads=[Scur, GC, ps], writes=[sout[hd % 2]])
                else:
                    Sn = newS()
                    kb.op(dve, lambda e: e.scalar_tensor_tensor(out=R(Sn[:, :]), in0=Scur[:, :], scalar=GC[:, c:c + 1],
                                                                in1=ps[:, 0:128], op0=ALU.mult, op1=ALU.add),
                          reads=[Scur, GC, ps], writes=[Sn])
                    hst["S"] = Sn
                yield
            kb.op(act, lambda e: e.copy(out=oseg[:, bi * 4 * C:(bi * 4 + 4) * C], in_=psO[:, 0:4 * C]),
                  reads=[psO], writes=[oseg])
            yield
            if last:
                kb.op(dve, lambda e: e.tensor_tensor(out=R(t1r[:, W]), in0=oseg[:, W], in1=oseg[:, W], op=ALU.mult),
                      reads=[oseg], writes=[t1r])
                ps = next_ps(g)
                kb.op(pe, lambda e: e.matmul(ps[:, W], lhsT=R(onesr[:, :]), rhs=R(t1r[:, W]), start=True, stop=True),
                      reads=[onesr, t1r], writes=[ps])
                kb.op(dve, lambda e: e.tensor_scalar(out=t1[:, W], in0=ps[:, W], scalar1=1.0 / 128, scalar2=EPS,
                                                     op0=ALU.mult, op1=ALU.add), reads=[ps], writes=[t1])
                yield
                kb.op(act, lambda e: e.sqrt(out=t1[:, W], in_=t1[:, W]), reads=[t1], writes=[t1])
                kb.op(dve, lambda e: e.reciprocal(out=t1[:, W], in_=t1[:, W]), reads=[t1], writes=[t1])
                kb.op(dve, lambda e: e.scalar_tensor_tensor(out=t1[:, W], in0=oseg[:, W], scalar=gn[:, 0:1], in1=t1[:, W],
                                                            op0=ALU.mult, op1=ALU.mult), reads=[oseg, gn, t1], writes=[t1])
                oc_ = ocat[gs % 2]
                kb.op(dve, lambda e: e.tensor_tensor(out=oc_[:, W], in0=t1[:, W], in1=og[:, W], op=ALU.mult),
                      reads=[t1, og], writes=[oc_])
                kb.dma(pool, S.catT[rs_, t0:t0 + ntok], oc_[:, W], reads=[oc_])
                if (not is_s) and t0 + ntok == TP:
                    kb.dma(pool, O.hgrn_prompt[hd], hst["S"][:, :], reads=[hst["S"]])
                if is_s:
                    kb.dma(pool, O.hgrn_sample[:, hd].rearrange("s k v -> k s v"), sout[hd % 2][:], reads=[sout[hd % 2]])
                yield

        pending = None
        gs = 0
        k = 0
        for hd in range(12):
            for seg in HSEGS:
                nb = seg[3] // 4
                for bi in range(nb):
                    item = (hd, gs, seg, bi, bi == 0, bi == nb - 1, k, {})
                    run_interleaved(genA(item), pending)
                    pending = genB(item)
                    k += 1
                gs += 1
        run_interleaved(None, pending)
    g.ps_n = 8


_CACHE = {}


def _prep_inputs(inp, c):
    b = c % 4
    s0, s1 = NS * c, NS * (c + 1)
    f = lambda a: np.ascontiguousarray(a, dtype=np.float32)
    m = {
        "x_prompt": f(inp["x_prompt"][b]),
        "x_sample": f(inp["x_sample"][s0:s1].reshape(TSM, D)),
        "mem_prompt": f(inp["mem_prompt"][b]),
        "cache_mem_k": f(inp["cache_mem_k"][:, s0:s1].reshape(2, NS, NMEM, 512)),
        "cache_mem_v": f(inp["cache_mem_v"][:, s0:s1].reshape(2, NS, NMEM, 512)),
        "state_rwkv": f(inp["state_rwkv"][0, s0:s1]),
        "state_shift": f(inp["state_shift"][0, s0:s1]),
        "state_hgrn": f(inp["state_hgrn"][0, s0:s1]),
        "state_conv": f(inp["state_conv"][:, s0:s1].reshape(2, NS * 2, DFF)),
        "norm_mix": f(inp["norm_mix"]), "norm_ffn": f(inp["norm_ffn"]),
        "norm_final": f(inp["norm_final"].reshape(1, D)), "mem_norm": f(inp["mem_norm"]),
        "w_mem_kv": f(inp["w_mem_kv"]), "a_w_in": f(inp["a_w_in"][0]),
        "a_mu": f(inp["a_mu"][0].reshape(-1, 1)), "a_w0": f(inp["a_w0"][0].reshape(-1, 1)),
        "a_w2": f(inp["a_w2"][0]), "a_a0": f(inp["a_a0"][0].reshape(-1, 1)), "a_a2": f(inp["a_a2"][0]),
        "a_g2": f(inp["a_g2"][0]), "a_k_k": f(inp["a_k_k"][0].reshape(-1, 1)),
        "a_k_a": f(inp["a_k_a"][0].reshape(-1, 1)), "a_r_k": f(inp["a_r_k"][0].reshape(-1, 1)),
        "a_ln_w": f(inp["a_ln_w"][0].reshape(-1, 1)), "a_ln_b": f(inp["a_ln_b"][0].reshape(-1, 1)),
        "a_w_out": f(inp["a_w_out"][0]), "b_w_in": f(inp["b_w_in"][0]),
        "b_lower_bounds": f(inp["b_lower_bounds"]), "b_g_norm": f(inp["b_g_norm"][0].reshape(-1, 1)),
        "b_w_out": f(inp["b_w_out"][0]), "ffn_w_up": f(inp["ffn_w_up"]),
        "ffn_conv_w": f(inp["ffn_conv_w"]), "ffn_conv_b": f(inp["ffn_conv_b"]),
        "ffn_w_down": f(inp["ffn_w_down"]), "consts": make_consts(),
    }
    return m


def run(inputs, stop_after=None, dbg=False, ncores=8):
    key = (stop_after, dbg)
    if key not in _CACHE:
        _CACHE[key] = build_program(stop_after, dbg)
    nc, kb = _CACHE[key]
    in_maps = [_prep_inputs(inputs, c) for c in range(ncores)]
    res = run_bass_kernel_spmd(nc, in_maps, core_ids=list(range(ncores)))
    return res.results


def kernel(**inputs):
    r = run(inputs)
    f = np.float32
    B, DEC = 4, 128
    y_prompt = np.stack([r[b]["y_prompt"] for b in range(B)]).astype(f)
    y_sample = np.concatenate([r[c]["y_sample"].reshape(NS, TS, D) for c in range(8)], 0).astype(f)
    mem_k = np.stack([r[b]["mem_k_prompt"] for b in range(B)], 1).reshape(2, B, NMEM, 4, 128).astype(f)
    mem_v = np.stack([r[b]["mem_v_prompt"] for b in range(B)], 1).reshape(2, B, NMEM, 4, 128).astype(f)
    rwkv_p = np.stack([r[b]["rwkv_prompt"].reshape(24, 64, 64) for b in range(B)])[None].astype(f)
    rwkv_s = np.concatenate([r[c]["rwkv_sample"].reshape(NS, 24, 64, 64) for c in range(8)], 0)[None].astype(f)
    shift_p = np.stack([r[b]["shift_prompt"].reshape(D) for b in range(B)])[None].astype(f)
    shift_s = np.concatenate([r[c]["shift_sample"] for c in range(8)], 0)[None].astype(f)
    hgrn_p = np.stack([r[b]["hgrn_prompt"] for b in range(B)])[None].astype(f)
    hgrn_s = np.concatenate([r[c]["hgrn_sample"] for c in range(8)], 0)[None].astype(f)
    conv_p = np.stack([r[b]["conv_prompt"] for b in range(B)], 1).astype(f)
    conv_s = np.concatenate([r[c]["conv_sample"].reshape(2, NS, 2, DFF) for c in range(8)], 1).astype(f)
    return (y_prompt, y_sample, mem_k, mem_v, rwkv_p, rwkv_s, shift_p, shift_s, hgrn_p, hgrn_s, conv_p, conv_s)
```
